# Optimizing a Trainium2 kernel written in Bass

```python
import jax, jax.numpy as jnp
from jax import lax
import numpy as np

D_MODEL = 1024
BATCH = 2
SEQ = 16384
DEPTH = 1
DEC_BATCH = 32
DEC_SEQ = 32
PAST_LEN = 2048

CHUNK = 64
LEFT_CHUNKS = 8
BAND = (LEFT_CHUNKS + 1) * CHUNK
HEAD_DIM = 64
ATTN_WIDTH = D_MODEL // 2
ATTN_HEADS = ATTN_WIDTH // HEAD_DIM
LRU_WIDTH = D_MODEL // 2
LRU_BLOCKS = 8
LRU_BLOCK = LRU_WIDTH // LRU_BLOCKS
CONV_WIDTH = 4
LRU_C = 8.0
REL_CLIP = 128
REL_SIZE = REL_CLIP + CHUNK
MIX_WIDTH = ATTN_WIDTH + LRU_WIDTH
IN_WIDTH = 3 * ATTN_WIDTH + 2 * LRU_WIDTH
D_FF = -(-8 * D_MODEL // (3 * 256)) * 256
PLE_DIM = 256
EPS = 1e-6
NEG = -1e30
SCALE = HEAD_DIM ** -0.5

kernel_name = 'hymba_rglru_chunkband_stream_step'


def rmsnorm(x, g):
    xf = x.astype(jnp.float32)
    y = xf * lax.rsqrt(jnp.mean(xf * xf, axis=-1, keepdims=True) + EPS)
    return (y * g.astype(jnp.float32)).astype(x.dtype)


def rel_bias(rel_table, rel):
    idx = jnp.clip(rel, -REL_CLIP, CHUNK - 1) + REL_CLIP
    return rel_table.astype(jnp.float32)[:, idx]


def softmax_attend(q, k, v, bias, mask=None):
    s = jnp.einsum('bqhd,bkhd->bhqk', q.astype(jnp.float32), k.astype(jnp.float32)) * SCALE + bias
    if mask is not None:
        s = jnp.where(mask, s, NEG)
    p = jax.nn.softmax(s, axis=-1)
    return jnp.einsum('bhqk,bkhd->bqhd', p, v.astype(jnp.float32)).astype(q.dtype)


def chunk_band_attention(q, k, v, rel_table):
    b_, s_ = q.shape[:2]
    n_chunks = s_ // CHUNK
    pad = LEFT_CHUNKS * CHUNK
    kp = jnp.pad(k, ((0, 0), (pad, 0), (0, 0), (0, 0)))
    vp = jnp.pad(v, ((0, 0), (pad, 0), (0, 0), (0, 0)))
    kpos = jnp.arange(BAND) - pad
    qpos = jnp.arange(CHUNK)
    bias = rel_bias(rel_table, kpos[None, :] - qpos[:, None])

    def one_chunk(c):
        start = c * CHUNK
        qb = lax.dynamic_slice_in_dim(q, start, CHUNK, axis=1)
        kb = lax.dynamic_slice_in_dim(kp, start, BAND, axis=1)
        vb = lax.dynamic_slice_in_dim(vp, start, BAND, axis=1)
        valid = (start + kpos) >= pad
        return softmax_attend(qb, kb, vb, bias, valid[None, None, None, :])

    out = lax.map(one_chunk, jnp.arange(n_chunks))
    return jnp.moveaxis(out, 0, 1).reshape(b_, s_, ATTN_WIDTH)


def cached_band_attention(q, k, v, cache_k, cache_v, rel_table):
    b_, t = q.shape[:2]
    n_cache = cache_k.shape[1]
    keys = jnp.concatenate([cache_k.astype(k.dtype), k], axis=1)
    vals = jnp.concatenate([cache_v.astype(v.dtype), v], axis=1)
    kpos = jnp.concatenate([jnp.arange(n_cache) - n_cache, jnp.arange(t)])
    bias = rel_bias(rel_table, kpos[None, :] - jnp.arange(t)[:, None])
    return softmax_attend(q, keys, vals, bias).reshape(b_, t, ATTN_WIDTH)


def causal_conv(xb, state, conv_w, conv_b):
    t = xb.shape[1]
    xp = jnp.concatenate([state.astype(xb.dtype), xb], axis=1)
    y = conv_b + sum(xp[:, j:j + t] * conv_w[j] for j in range(CONV_WIDTH))
    return y, xp[:, -(CONV_WIDTH - 1):]


def rglru(xc, h0, w_rgate, b_rgate, w_igate, b_igate, lru_lambda):
    b_, t = xc.shape[:2]
    xf = xc.astype(jnp.float32)
    xblk = xf.reshape(b_, t, LRU_BLOCKS, LRU_BLOCK)
    r = jax.nn.sigmoid(jnp.einsum('btnk,nkj->btnj', xblk, w_rgate.astype(jnp.float32)).reshape(b_, t, LRU_WIDTH) + b_rgate.astype(jnp.float32))
    ig = jax.nn.sigmoid(jnp.einsum('btnk,nkj->btnj', xblk, w_igate.astype(jnp.float32)).reshape(b_, t, LRU_WIDTH) + b_igate.astype(jnp.float32))
    log_a = -LRU_C * r * jax.nn.softplus(-lru_lambda.astype(jnp.float32))
    a = jnp.exp(log_a)
    u = jnp.sqrt(-jnp.expm1(2.0 * log_a)) * (ig * xf)
    u = u.at[:, 0].add(a[:, 0] * h0.astype(jnp.float32))

    def combine(left, right):
        a_l, u_l = left
        a_r, u_r = right
        return a_l * a_r, a_r * u_l + u_r

    _, hs = lax.associative_scan(combine, (a, u), axis=1)
    return hs.astype(xc.dtype), hs[:, -1]


def trunk_layer(h, p, attend, conv_state, lru_state, g_mix, w_in, conv_w, conv_b, w_rgate, b_rgate,
                w_igate, b_igate, lru_lambda, g_attn_out, g_lru_out, w_out, g_ffn, w_ffn_gate,
                w_ffn_up, w_ffn_down, g_ple, w_ple_gate, w_ple_proj):
    b_, t = h.shape[:2]
    xn = rmsnorm(h, g_mix)
    proj = xn @ w_in
    q, k, v, xr, gr = jnp.split(proj, [ATTN_WIDTH, 2 * ATTN_WIDTH, 3 * ATTN_WIDTH, 3 * ATTN_WIDTH + LRU_WIDTH], axis=-1)
    hd_shape = (b_, t, ATTN_HEADS, HEAD_DIM)
    attn_out, k_new, v_new = attend(q.reshape(hd_shape), k.reshape(hd_shape), v.reshape(hd_shape))
    xc, conv_new = causal_conv(xr, conv_state, conv_w, conv_b)
    hs, lru_new = rglru(xc, lru_state, w_rgate, b_rgate, w_igate, b_igate, lru_lambda)
    lru_out = hs * jax.nn.gelu(gr, approximate=True)
    mixed = jnp.concatenate([rmsnorm(attn_out, g_attn_out), rmsnorm(lru_out, g_lru_out)], axis=-1)
    h = h + mixed @ w_out
    hn = rmsnorm(h, g_ffn)
    h = h + (jax.nn.silu(hn @ w_ffn_gate) * (hn @ w_ffn_up)) @ w_ffn_down
    gate = jax.nn.sigmoid(rmsnorm(h, g_ple) @ w_ple_gate)
    h = h + (p.astype(h.dtype) @ w_ple_proj) * gate
    return h, k_new, v_new, conv_new, lru_new


def setup_inputs(seed: int = 0) -> dict:
    key = jax.random.key(seed)
    ks = jax.random.split(key, 32)
    f32 = jnp.float32

    def nrm(k, shape, scale=1.0):
        return jax.random.normal(k, shape, f32) * scale

    kv_len = min(LEFT_CHUNKS * CHUNK, PAST_LEN)
    a_c = jax.random.uniform(ks[20], (DEPTH, LRU_WIDTH), f32, 0.9, 0.999)
    base = a_c ** (1.0 / LRU_C)
    lru_lambda = jnp.log(base) - jnp.log1p(-base)
    return {
        'x_prompt': nrm(ks[0], (BATCH, SEQ, D_MODEL)),
        'x_sample': nrm(ks[1], (DEC_BATCH, DEC_SEQ, D_MODEL)),
        'p_prompt': nrm(ks[2], (DEPTH, BATCH, SEQ, PLE_DIM)),
        'p_sample': nrm(ks[3], (DEPTH, DEC_BATCH, DEC_SEQ, PLE_DIM)),
        'cache_k': nrm(ks[4], (DEPTH, DEC_BATCH, kv_len, ATTN_HEADS, HEAD_DIM)),
        'cache_v': nrm(ks[5], (DEPTH, DEC_BATCH, kv_len, ATTN_HEADS, HEAD_DIM)),
        'state_conv': nrm(ks[6], (DEPTH, DEC_BATCH, CONV_WIDTH - 1, LRU_WIDTH)),
        'state_h': nrm(ks[7], (DEPTH, DEC_BATCH, LRU_WIDTH), 0.5),
        'g_mix': 1.0 + nrm(ks[8], (DEPTH, D_MODEL), 0.02),
        'w_in': nrm(ks[9], (DEPTH, D_MODEL, IN_WIDTH), D_MODEL ** -0.5),
        'conv_w': nrm(ks[10], (DEPTH, CONV_WIDTH, LRU_WIDTH), CONV_WIDTH ** -0.5),
        'conv_b': nrm(ks[11], (DEPTH, LRU_WIDTH), 0.01),
        'w_rgate': nrm(ks[12], (DEPTH, LRU_BLOCKS, LRU_BLOCK, LRU_BLOCK), LRU_BLOCK ** -0.5),
        'b_rgate': nrm(ks[13], (DEPTH, LRU_WIDTH), 0.01),
        'w_igate': nrm(ks[14], (DEPTH, LRU_BLOCKS, LRU_BLOCK, LRU_BLOCK), LRU_BLOCK ** -0.5),
        'b_igate': nrm(ks[15], (DEPTH, LRU_WIDTH), 0.01),
        'lru_lambda': lru_lambda,
        'rel_bias_table': nrm(ks[16], (DEPTH, ATTN_HEADS, REL_SIZE), 0.1),
        'g_attn_out': 1.0 + nrm(ks[17], (DEPTH, ATTN_WIDTH), 0.02),
        'g_lru_out': 1.0 + nrm(ks[18], (DEPTH, LRU_WIDTH), 0.02),
        'w_out': nrm(ks[19], (DEPTH, MIX_WIDTH, D_MODEL), MIX_WIDTH ** -0.5),
        'g_ffn': 1.0 + nrm(ks[21], (DEPTH, D_MODEL), 0.02),
        'w_ffn_gate': nrm(ks[22], (DEPTH, D_MODEL, D_FF), D_MODEL ** -0.5),
        'w_ffn_up': nrm(ks[23], (DEPTH, D_MODEL, D_FF), D_MODEL ** -0.5),
        'w_ffn_down': nrm(ks[24], (DEPTH, D_FF, D_MODEL), D_FF ** -0.5),
        'g_ple': 1.0 + nrm(ks[25], (DEPTH, D_MODEL), 0.02),
        'w_ple_gate': nrm(ks[26], (DEPTH, D_MODEL, D_MODEL), D_MODEL ** -0.5),
        'w_ple_proj': nrm(ks[27], (DEPTH, PLE_DIM, D_MODEL), PLE_DIM ** -0.5),
        'g_final': 1.0 + nrm(ks[28], (D_MODEL,), 0.02),
    }


def reference(x_prompt, x_sample, p_prompt, p_sample, cache_k, cache_v, state_conv, state_h,
              g_mix, w_in, conv_w, conv_b, w_rgate, b_rgate, w_igate, b_igate, lru_lambda,
              rel_bias_table, g_attn_out, g_lru_out, w_out, g_ffn, w_ffn_gate, w_ffn_up,
              w_ffn_down, g_ple, w_ple_gate, w_ple_proj, g_final):
    hp = x_prompt
    hsmp = x_sample
    n_keep = min(LEFT_CHUNKS * CHUNK, x_prompt.shape[1])
    kp_l, vp_l, cp_l, hp_l = [], [], [], []
    ks_l, vs_l, cs_l, hs_l = [], [], [], []
    for i in range(DEPTH):
        lw = (g_mix[i], w_in[i], conv_w[i], conv_b[i], w_rgate[i], b_rgate[i], w_igate[i], b_igate[i],
              lru_lambda[i], g_attn_out[i], g_lru_out[i], w_out[i], g_ffn[i], w_ffn_gate[i],
              w_ffn_up[i], w_ffn_down[i], g_ple[i], w_ple_gate[i], w_ple_proj[i])
        table = rel_bias_table[i]

        def attend_prompt(q, k, v, table=table):
            return chunk_band_attention(q, k, v, table), k[:, -n_keep:], v[:, -n_keep:]

        def attend_sample(q, k, v, table=table, ck=cache_k[i], cv=cache_v[i]):
            return cached_band_attention(q, k, v, ck, cv, table), k, v

        zero_conv = jnp.zeros((hp.shape[0], CONV_WIDTH - 1, LRU_WIDTH), hp.dtype)
        zero_h = jnp.zeros((hp.shape[0], LRU_WIDTH), jnp.float32)
        hp, k1, v1, c1, r1 = trunk_layer(hp, p_prompt[i], attend_prompt, zero_conv, zero_h, *lw)
        hsmp, k2, v2, c2, r2 = trunk_layer(hsmp, p_sample[i], attend_sample, state_conv[i], state_h[i], *lw)
        kp_l.append(k1); vp_l.append(v1); cp_l.append(c1); hp_l.append(r1)
        ks_l.append(k2); vs_l.append(v2); cs_l.append(c2); hs_l.append(r2)
    y_prompt = rmsnorm(hp, g_final)
    y_sample = rmsnorm(hsmp, g_final)
    return (y_prompt, y_sample,
            jnp.stack(kp_l), jnp.stack(vp_l), jnp.stack(cp_l), jnp.stack(hp_l),
            jnp.stack(ks_l), jnp.stack(vs_l), jnp.stack(cs_l), jnp.stack(hs_l))
```

```python
import contextlib
import numpy as np
import concourse.bass as bass
import concourse.mybir as mybir
from concourse.bass_utils import run_bass_kernel_spmd

F32 = mybir.dt.float32
BF16 = mybir.dt.bfloat16
AF = mybir.ActivationFunctionType
ALU = mybir.AluOpType

D = 1024
T = 256
NLEAN = 48
NMAIN = 16
NT = NLEAN + NMAIN
SEG = 4096
DFF = 2816
NF = DFF // 128
SCALE = 0.125
EPS = 1e-6
NEG = -1e30
NWS = 10
FILL = {}


_DBG = {}


class Tok:
    __slots__ = ("lastw", "readers", "name")

    def __init__(self, name=""):
        self.lastw = None
        self.readers = []
        self.name = name


def toks(n, name=""):
    return [Tok(name + str(i)) for i in range(n)]


class KB:
    def __init__(self, nc, st):
        self.nc = nc
        self.st = st
        self.eh = {"pe": nc.tensor, "act": nc.scalar, "dve": nc.vector, "pool": nc.gpsimd, "sp": nc.sync}
        self.prog = {e: [] for e in self.eh}
        self.sem = {e: st.enter_context(nc.semaphore("sem_" + e)) for e in self.eh}
        self.cnt = {e: 0 for e in self.eh}
        self.seen = {e: {} for e in self.eh}
        self.dsem = {}
        self.total = 0
        self.limit = _DBG.get("limit")

    def semh(self, k):
        return self.sem[k] if isinstance(k, str) else self.dsem[k[1]][0]

    def _deps(self, eng, reads, writes):
        w = {}
        for b in reads:
            if b.lastw:
                k, v = b.lastw
                w[k] = max(w.get(k, 0), v)
        for b in writes:
            if b.lastw:
                k, v = b.lastw
                w[k] = max(w.get(k, 0), v)
            for k, v in b.readers:
                w[k] = max(w.get(k, 0), v)
        out = []
        for k, v in w.items():
            if k == eng and eng == "pe":
                continue
            if self.seen[eng].get(k, 0) < v:
                self.seen[eng][k] = v
                out.append((k, v))
        return out

    def op(self, eng, fn, reads=(), writes=()):
        self.total += 1
        if self.limit is not None and self.total > self.limit:
            return
        wl = self._deps(eng, reads, writes)
        self.cnt[eng] += 1
        v = self.cnt[eng]
        self.prog[eng].append((wl, fn, (eng, 1)))
        for b in reads:
            b.readers.append((eng, v))
        for b in writes:
            b.lastw = (eng, v)
            b.readers = []

    def dma(self, q, key, fn, reads=(), writes=()):
        self.total += 1
        if self.limit is not None and self.total > self.limit:
            return
        wl = self._deps(q, reads, writes)
        if key not in self.dsem:
            self.dsem[key] = [self.st.enter_context(self.nc.semaphore("d_" + key)), 0]
        d = self.dsem[key]
        d[1] += 16
        v = d[1]
        sk = ("D", key)
        self.prog[q].append((wl, fn, (sk, 16)))
        for b in reads:
            b.readers.append((sk, v))
        for b in writes:
            b.lastw = (sk, v)
            b.readers = []

    def replay(self, e, h, final_keys=()):
        for wl, fn, (sk, n) in self.prog[e]:
            for k, v in wl:
                h.wait_ge(self.semh(k), v)
            ins = fn(h)
            ins.then_inc(self.semh(sk), n)
        for key in final_keys:
            h.wait_ge(self.dsem[key][0], self.dsem[key][1])


def build_nc(dbg_tiles=None, dbg_sample=True, dbg_conv=True):
    nc = bass.Bass("TRN2", target_bir_lowering=False)

    def din(name, shape, dt=F32):
        return nc.dram_tensor(name, list(shape), dt, kind="ExternalInput").ap()

    def dout(name, shape):
        return nc.dram_tensor(name, list(shape), F32, kind="ExternalOutput").ap()

    def dint(name, shape, dt=BF16):
        return nc.dram_tensor(name, list(shape), dt, kind="Internal").ap()

    xs_d = din("xs", [NT * T, D])
    pm_d = din("pm", [SEG, 256])
    xsm_d = din("xsm", [128, D])
    psm_d = din("psm", [128, 256])
    ck_d = din("ck", [4, 512, 512])
    cv_d = din("cv", [4, 512, 512])
    sconv_d = din("sconv", [128, 4, 4, 3])
    sh_d = din("sh", [128, 4, 4])
    flags_d = din("flags", [128, NT])
    biasp_d = din("biasp", [128, 8 * 640])
    biass_d = din("biass", [128, 8 * 160])
    ident_d = din("ident", [128, 128])
    vec_d = din("vecs", [128, 74])
    band_d = din("band", [128, 640])
    w_in_d = din("w_in", [D, 2560])
    w_out_d = din("w_out", [D, D])
    w_g_d = din("w_ffn_gate", [D, DFF])
    w_u_d = din("w_ffn_up", [D, DFF])
    w_d_d = din("w_ffn_down", [DFF, D])
    w_pg_d = din("w_ple_gate", [D, D])
    w_pp_d = din("w_ple_proj", [256, D])
    w_r_d = din("w_rgate", [8, 64, 64])
    w_i_d = din("w_igate", [8, 64, 64])
    y_d = dout("y", [SEG, D])
    ys_d = dout("ys", [128, D])
    kp_d = dout("kp", [512, 512])
    vp_d = dout("vp", [512, 512])
    convp_d = dout("convp", [128, 4, 3])
    hp_d = dout("hp", [128, 4])
    ksm_d = dout("ksm", [128, 512])
    vsm_d = dout("vsm", [128, 512])
    convs_d = dout("convs", [128, 4, 4, 3])
    hsm_d = dout("hsm", [128, 4, 4])
    dbg_d = dout("dbgo", [128, 4, T]) if _DBG.get("dump") is not None else None
    sc_in = dint("sc_in", [20, 128, 8, 128])
    sc_out = dint("sc_out", [8, 128, 8, 128])
    sc_g = dint("sc_g", [NF, 128, 8, 128])
    sc_u = dint("sc_u", [NF, 128, 8, 128])
    sc_d = dint("sc_d", [NF, 128, 1024])
    sc_pg = dint("sc_pg", [8, 128, 8, 128])
    xbf = dint("xbf", [(NLEAN - 2) * T, D])

    with contextlib.ExitStack() as st:
        def sb(name, shape, dt=F32):
            return st.enter_context(nc.sbuf_tensor("s_" + name, list(shape), dt))

        def ps(name, shape, dt=F32):
            return st.enter_context(nc.psum_tensor("p_" + name, list(shape), dt))

        k = KB(nc, st)
        ident = sb("ident", [128, 128])
        ones_bf = sb("ones_bf", [128, 128], BF16)
        eps_t = sb("eps_t", [128, 1])
        one_t = sb("one_t", [128, 1])
        vecs = sb("vecs", [128, 74])
        band_bf = sb("band_bf", [128, 640], BF16)
        flags = sb("flags", [128, NT])
        nsp = sb("nsp", [128, 4])
        nsp2 = sb("nsp2", [128, 4])
        nb = sb("nb", [128, 8])
        lamt = sb("lamt", [128, 4])
        expb = sb("expb", [128, 8, 640])
        expbs = sb("expbs", [128, 8, 160])
        w_xr = sb("w_xr", [128, 8, 512], BF16)
        w_pp = sb("w_pp", [128, 2, 1024], BF16)
        wr_bd = sb("wr_bd", [128, 4, 128], BF16)
        wi_bd = sb("wi_bd", [128, 4, 128], BF16)
        WS = [sb(f"ws{i}", [128, 8, 128], BF16) for i in range(NWS)]
        xtok = [sb(f"xtok{i}", [128, 2, 1024]) for i in range(2)]
        ptok = sb("ptok", [128, 2, 256])
        pT = sb("pT", [128, 2, T], BF16)
        hT = sb("hT", [128, 8, T])
        xn = sb("xn", [128, 8, T], BF16)
        sq = sb("sq", [128, 8, T], BF16)
        rt = sb("rt", [128, T])
        rstd = sb("rstd", [128, T])
        qT = sb("qT", [128, 4, T], BF16)
        kT = sb("kT", [128, 4, 768], BF16)
        Vr = sb("Vr", [128, 6, 512], BF16)
        vones = sb("vones", [128, 6, 64], BF16)
        XR2 = [sb(f"XR{i}", [128, 4, T + 3]) for i in range(2)]
        XRs = sb("XRs", [128, 4, 4, 35])
        G = sb("G", [128, 4, T])
        hs = sb("hs", [128, 4, T])
        lruo = sb("lruo", [128, 4, T])
        hst = sb("hst", [128, 4])
        hS = sb("hS", [128, 4, 4])
        hso = sb("hso", [128, 4, 4])
        xc = [sb(f"xc{i}", [128, T]) for i in range(2)]
        xc2 = [sb(f"xc2{i}", [128, T]) for i in range(2)]
        xcb = [sb(f"xcb{i}", [128, T], BF16) for i in range(2)]
        rg = [sb(f"rg{i}", [128, T]) for i in range(4)]
        ig = [sb(f"ig{i}", [128, T]) for i in range(4)]
        av = [sb(f"av{i}", [128, T]) for i in range(2)]
        a2 = [sb(f"a2{i}", [128, T]) for i in range(2)]
        E = [sb(f"E{i}", [128, 768]) for i in range(2)]
        Pm = [sb(f"Pm{i}", [128, 768], BF16) for i in range(4)]
        rd = [sb(f"rd{i}", [128, 256]) for i in range(2)]
        attnT = sb("attnT", [128, 4, T])
        mixT = sb("mixT", [128, 8, T], BF16)
        sg = [sb(f"sg{i}", [128, T]) for i in range(2)]
        act = [sb(f"act{i}", [128, T], BF16) for i in range(2)]
        ost = [sb(f"ost{i}", [128, 1024]) for i in range(2)]
        Vn = sb("Vn", [32, 4, 512], BF16)
        PG = [ps(f"pg{i}", [128, 512]) for i in range(4)]
        PSS = [ps(f"pss{i}", [128, 1024]) for i in range(2)]
        TPG = toks(4, "pg")
        TPSS = toks(2, "pss")
        pgc = [0]

        def nextpg():
            i = pgc[0] % 3
            pgc[0] += 1
            return PG[i], TPG[i]

        Tconst = Tok("const")
        Tw_xr = Tok("w_xr")
        THT = toks(8, "hT")
        TXN = toks(8, "xn")
        TSQ, TRT, TRS = Tok("sq"), Tok("rt"), Tok("rstd")
        TQ = toks(4, "q")
        TKR = toks(6, "kr")
        TVR = toks(6, "vr")
        TVO = toks(6, "vo")
        TXR2 = [toks(4, "xra"), toks(4, "xrb")]
        TG = toks(4, "g")
        THS = toks(4, "hs")
        TLO = toks(4, "lo")
        THST = toks(4, "hst")
        TXC, TXCB, TAV, TA2 = (toks(2, n) for n in ("xc", "xcb", "av", "a2"))
        TRG, TIG = toks(4, "rg"), toks(4, "ig")
        TXC2 = toks(2, "xc2")
        TE, TP, TRD = toks(2, "E"), toks(4, "P"), toks(2, "rd")
        TAT = toks(4, "at")
        TMX = toks(8, "mx")
        TSG, TACT = toks(2, "sg"), toks(2, "act")
        TOS = toks(2, "os")
        TXT = toks(2, "xt")
        TPT, TPTT = Tok("ptok"), Tok("pT")
        TWS = toks(NWS, "ws")
        TSCR = Tok("scr")
        TVC, TKC, TVN = Tok("vc"), Tok("kc"), Tok("vn")
        TXRS, THS_S, THSO = Tok("xrs"), Tok("hS"), Tok("hso")
        setup_toks = []

        def act_op(func, out, in_, reads, writes, bias=None, scale=None):
            def fn(h):
                kw = {}
                if bias is not None:
                    kw["bias"] = bias
                if scale is not None:
                    kw["scale"] = scale
                return h.activation(out=out, in_=in_, func=func, **kw)
            k.op("act", fn, reads, writes)

        def tt(out, in0, in1, op, reads, writes, eng="dve"):
            k.op(eng, lambda h: h.tensor_tensor(out=out, in0=in0, in1=in1, op=op), reads, writes)

        def ts(out, in0, s1, s2, op0, op1, reads, writes):
            if op1 is None:
                k.op("dve", lambda h: h.tensor_scalar(out=out, in0=in0, scalar1=s1, scalar2=None, op0=op0), reads, writes)
            else:
                k.op("dve", lambda h: h.tensor_scalar(out=out, in0=in0, scalar1=s1, scalar2=s2, op0=op0, op1=op1), reads, writes)

        def stt(out, in0, scalar, in1, op0, op1, reads, writes):
            k.op("dve", lambda h: h.scalar_tensor_tensor(out=out, in0=in0, scalar=scalar, in1=in1, op0=op0, op1=op1), reads, writes)

        def cp(eng, out, in_, reads, writes):
            if eng == "act":
                k.op("act", lambda h: h.copy(out=out, in_=in_), reads, writes)
            else:
                k.op("dve", lambda h: h.tensor_copy(out=out, in_=in_), reads, writes)

        evc = [0]

        def evac(out, in_, reads, writes):
            evc[0] += 1
            cp("act" if evc[0] % 2 else "dve", out, in_, reads, writes)

        def mmg(out, pairs, reads, writes, start=True, stop=True):
            def fn(h):
                n = len(pairs)
                ins = None
                for i, (l, r) in enumerate(pairs):
                    ins = h.matmul(out, l, r, start=(start and i == 0), stop=(stop and i == n - 1))
                return ins
            k.op("pe", fn, reads, writes)

        def mmlist(items, reads, writes):
            def fn(h):
                ins = None
                for (o, l, r, s0, s1) in items:
                    ins = h.matmul(o, l, r, start=s0, stop=s1)
                return ins
            k.op("pe", fn, reads, writes)

        def filler(n):
            if n <= 0:
                return
            def fn(h):
                ins = None
                for _ in range(n):
                    ins = h.matmul(PG[3][:, 0:256], ones_bf[:], w_xr[:, 0, 0:256], start=True, stop=True)
                return ins
            k.op("pe", fn, (), ())

        def trlist(items, reads, writes):
            def fn(h):
                ins = None
                for (o, i_) in items:
                    ins = h.transpose(o, i_, ident[:])
                return ins
            k.op("pe", fn, reads, writes)

        def sdma(out, in_, writes):
            k.dma("pool", "setup", lambda h: h.dma_start(out=out, in_=in_), (), writes)
            setup_toks.extend(writes)

        wsc = [0]

        def wload(src_ap, view1024=False):
            i = wsc[0] % NWS
            wsc[0] += 1
            dst = WS[i][:].rearrange("p a b -> p (a b)") if view1024 else WS[i][:]
            k.dma("sp", f"ws{i}", lambda h: h.dma_start(out=dst, in_=src_ap), [TSCR], [TWS[i]])
            return WS[i], TWS[i]

        sdma(ident[:], ident_d, [Tconst])
        sdma(vecs[:], vec_d, [Tconst])
        sdma(flags[:], flags_d, [Tconst])
        sdma(E[0][:, 0:640], band_d, [Tconst])
        sdma(expb[:].rearrange("p a b -> p (a b)"), biasp_d, [Tconst])
        sdma(expbs[:].rearrange("p a b -> p (a b)"), biass_d, [Tconst])
        sdma(w_pp[:], w_pp_d.rearrange("(k p) c -> p k c", p=128), [Tconst])
        Tbd = Tok("bd")
        k.op("dve", lambda h: h.memset(wr_bd[:], 0.0), (), [Tbd])
        k.op("dve", lambda h: h.memset(wi_bd[:], 0.0), (), [Tbd])
        for c in range(4):
            for e in range(2):
                sdma(wr_bd[e * 64:(e + 1) * 64, c, e * 64:(e + 1) * 64], w_r_d[2 * c + e], [Tbd])
                sdma(wi_bd[e * 64:(e + 1) * 64, c, e * 64:(e + 1) * 64], w_i_d[2 * c + e], [Tbd])
        fin = ("D", "setup"), (k.dsem["setup"][1] if "setup" in k.dsem else 0)
        for t_ in setup_toks:
            t_.lastw = fin
        Tc2 = Tok("c2")
        k.op("dve", lambda h: h.memset(ones_bf[:], 1.0), (), [Tc2])
        k.op("dve", lambda h: h.tensor_copy(out=band_bf[:], in_=E[0][:, 0:640]), [Tconst], [Tc2])
        k.op("dve", lambda h: h.memset(eps_t[:], EPS), (), [Tc2])
        k.op("dve", lambda h: h.memset(one_t[:], 1.0), (), [Tc2])
        for i_ in range(4):
            k.op("dve", lambda h, i_=i_: h.memset(Pm[i_][:], 0.0), (), [TP[i_]])
        k.op("dve", lambda h: h.memset(hst[:], 0.0), (), THST)
        k.op("dve", lambda h: h.memset(XR2[0][:], 0.0), (), TXR2[0])
        k.op("dve", lambda h: h.memset(XR2[1][:], 0.0), (), TXR2[1])
        k.op("dve", lambda h: h.memset(vones[:], 0.0), (), TVO)
        k.op("dve", lambda h: h.memset(kT[:], 0.0), (), TKR)
        k.op("dve", lambda h: h.memset(Vr[:], 0.0), (), TVR)
        act_op(AF.Exp, expb[:], expb[:], [Tconst], [Tconst])
        act_op(AF.Exp, expbs[:], expbs[:], [Tconst], [Tconst])
        act_op(AF.Exp, lamt[:], vecs[:, 52:56], [Tconst], [Tc2], scale=-1.0)
        act_op(AF.Ln, lamt[:], lamt[:], [Tc2], [Tc2], bias=one_t[:])
        ts(nsp[:], lamt[:], -8.0, None, ALU.mult, None, [Tc2], [Tc2])
        ts(nsp2[:], lamt[:], -16.0, None, ALU.mult, None, [Tc2], [Tc2])
        ts(nb[:], vecs[:, 44:52], -1.0, None, ALU.mult, None, [Tconst], [Tc2])
        CONST = [Tconst, Tc2, Tbd]
        g_mix, g_ffn, g_ple, g_fin = vecs[:, 0:8], vecs[:, 8:16], vecs[:, 16:24], vecs[:, 24:32]
        g_attn, g_lru = vecs[:, 32:36], vecs[:, 36:40]
        cb, br, bi = vecs[:, 40:44], vecs[:, 44:48], vecs[:, 48:52]
        cw = vecs[:, 56:72]
        c0s, nc0 = vecs[:, 72:73], vecs[:, 73:74]

        for half in range(2):
            stg = xtok[0][:].rearrange("p a (b c) -> p (a b) c", b=2)
            k.dma("pool", "x0", lambda h, half=half, stg=stg: h.dma_start(
                out=stg, in_=w_in_d[half * 512:(half + 1) * 512, 1536:2048].rearrange("(k p) c -> p k c", p=128)), (), [TXT[0]])
            for kk in range(4):
                ts(w_xr[:, half * 4 + kk, :], stg[:, kk, :], g_mix[:, half * 4 + kk:half * 4 + kk + 1], None, ALU.mult, None,
                   [TXT[0]] + CONST, [Tw_xr])
        conv_list = []
        for j in range(20):
            conv_list.append((sc_in[j], w_in_d[:, j * 128:(j + 1) * 128].rearrange("(k p) c -> p k c", p=128)))
        for j in range(8):
            conv_list.append((sc_out[j], w_out_d[:, j * 128:(j + 1) * 128].rearrange("(k p) c -> p k c", p=128)))
        for f in range(NF):
            conv_list.append((sc_g[f], w_g_d[:, f * 128:(f + 1) * 128].rearrange("(k p) c -> p k c", p=128)))
            conv_list.append((sc_u[f], w_u_d[:, f * 128:(f + 1) * 128].rearrange("(k p) c -> p k c", p=128)))
            conv_list.append((sc_d[f], w_d_d[f * 128:(f + 1) * 128, :]))
        for j in range(8):
            conv_list.append((sc_pg[j], w_pg_d[:, j * 128:(j + 1) * 128].rearrange("(k p) c -> p k c", p=128)))
        conv_pos = [0]
        conv_tok = []

        def emit_conv(n):
            for _ in range(n):
                if conv_pos[0] >= len(conv_list):
                    return
                o, i_ = conv_list[conv_pos[0]]
                conv_pos[0] += 1
                k.dma("pool", "wconv", lambda h, o=o, i_=i_: h.dma_start(out=o, in_=i_), (), [TSCR])
                if conv_pos[0] == len(conv_list):
                    TSCR.lastw = (("D", "wconv"), k.dsem["wconv"][1])

        def load_x(g):
            s_ = g % 2
            for b in range(2):
                r0 = g * T + b * 128
                k.dma("pool", f"x{s_}", lambda h, s_=s_, b=b, r0=r0: h.dma_start(out=xtok[s_][:, b, :], in_=xs_d[r0:r0 + 128, :]), (), [TXT[s_]])

        def transpose_in(src_tile, src_tok, nsb, Tn):
            for b in range(nsb):
                for kq in range(2):
                    pg, tpg = nextpg()
                    items = [(pg[:, i * 128:(i + 1) * 128], src_tile[:, b, (kq * 4 + i) * 128:(kq * 4 + i + 1) * 128]) for i in range(4)]
                    trlist(items, [src_tok] + CONST, [tpg])
                    evac(hT[:, kq * 4:(kq + 1) * 4, b * 128:(b + 1) * 128], pg[:, :].rearrange("p (a b) -> p a b", a=4), [tpg], THT[kq * 4:(kq + 1) * 4])

        def rms(src, stoks, nk, gvec, dst, dtoks, Tn, dim):
            act_op(AF.Square, sq[:, 0:nk, 0:Tn], src[:, 0:nk, 0:Tn], stoks, [TSQ])
            pg, tpg = nextpg()
            mmg(pg[:, 0:Tn], [(ones_bf[:], sq[:, kk, 0:Tn]) for kk in range(nk)], [TSQ] + CONST, [tpg])
            filler(FILL.get("rms", 0))
            act_op(AF.Ln, rt[:, 0:Tn], pg[:, 0:Tn], [tpg] + CONST, [TRT], bias=eps_t[:], scale=1.0 / dim)
            act_op(AF.Exp, rstd[:, 0:Tn], rt[:, 0:Tn], [TRT], [TRS], scale=-0.5)
            for kk in range(nk):
                stt(dst[:, kk, 0:Tn], src[:, kk, 0:Tn], gvec[:, kk:kk + 1], rstd[:, 0:Tn], ALU.mult, ALU.mult,
                    [stoks[kk], TRS] + CONST, [dtoks[kk]])

        def lru_p(c, Tn, g, sample):
            xb = g % 2
            XRc, XRn = XR2[xb], XR2[1 - xb]
            i2 = c % 2
            if not sample:
                xin = lambda j: XRc[:, c, j:j + Tn]
                xcv = xc[i2][:, 0:Tn]
                xrt = [TXR2[xb][c]]
            else:
                xin = lambda j: XRs[:, c, :, j:j + 32]
                xcv = xc[i2][:, 0:Tn].rearrange("p (s t) -> p s t", s=4)
                xrt = [TXRS]
            x2v = xc2[i2][:, 0:Tn] if not sample else xc2[i2][:, 0:Tn].rearrange("p (s t) -> p s t", s=4)
            ts(xcv, xin(3), cw[:, c * 4 + 3:c * 4 + 4], cb[:, c:c + 1], ALU.mult, ALU.add, xrt + CONST, [TXC[i2]])
            ts(x2v, xin(1), cw[:, c * 4 + 1:c * 4 + 2], None, ALU.mult, None, xrt + CONST, [TXC2[i2]])
            stt(xcv, xin(0), cw[:, c * 4 + 0:c * 4 + 1], xcv, ALU.mult, ALU.add, xrt + [TXC[i2]] + CONST, [TXC[i2]])
            stt(x2v, xin(2), cw[:, c * 4 + 2:c * 4 + 3], x2v, ALU.mult, ALU.add, xrt + [TXC2[i2]] + CONST, [TXC2[i2]])
            tt(xcv, xcv, x2v, ALU.add, [TXC[i2], TXC2[i2]], [TXC[i2]])
            cp("dve", xcb[i2][:, 0:Tn], xc[i2][:, 0:Tn], [TXC[i2]], [TXCB[i2]])
            if not sample:
                cp("act", XRn[:, c, 0:3], XRc[:, c, Tn:Tn + 3], [TXR2[xb][c]], [TXR2[1 - xb][c]])
            if g >= NLEAN and not sample:
                filler(FILL.get("lru", 0))
            pg, tpg = nextpg()
            mmlist([(pg[:, 0:Tn], wr_bd[:, c, :], xcb[i2][:, 0:Tn], True, True),
                    (pg[:, 256:256 + Tn], wi_bd[:, c, :], xcb[i2][:, 0:Tn], True, True)], [TXCB[i2]] + CONST, [tpg])
            act_op(AF.Sigmoid, rg[c][:, 0:Tn], pg[:, 0:Tn], [tpg] + CONST, [TRG[c]], bias=br[:, c:c + 1])
            act_op(AF.Sigmoid, ig[c][:, 0:Tn], pg[:, 256:256 + Tn], [tpg] + CONST, [TIG[c]], bias=bi[:, c:c + 1])
            tt(ig[c][:, 0:Tn], ig[c][:, 0:Tn], xc[i2][:, 0:Tn], ALU.mult, [TIG[c], TXC[i2]], [TIG[c]], eng="pool")

        def lru_q(c, Tn, g, sample, with_out):
            i2 = c % 2
            act_op(AF.Exp, av[i2][:, 0:Tn], rg[c][:, 0:Tn], [TRG[c]] + CONST, [TAV[i2]], scale=nsp[:, c:c + 1])
            act_op(AF.Exp, a2[i2][:, 0:Tn], rg[c][:, 0:Tn], [TRG[c]] + CONST, [TA2[i2]], scale=nsp2[:, c:c + 1])
            act_op(AF.Ln, a2[i2][:, 0:Tn], a2[i2][:, 0:Tn], [TA2[i2]] + CONST, [TA2[i2]], bias=one_t[:], scale=-1.0)
            act_op(AF.Exp, a2[i2][:, 0:Tn], a2[i2][:, 0:Tn], [TA2[i2]], [TA2[i2]], scale=0.5)
            tt(ig[c][:, 0:Tn], ig[c][:, 0:Tn], a2[i2][:, 0:Tn], ALU.mult, [TIG[c], TA2[i2]], [TIG[c]], eng="dve")
            if not sample:
                k.op("dve", lambda h: h.tensor_tensor_scan(out=hs[:, c, 0:Tn], data0=av[i2][:, 0:Tn], data1=ig[c][:, 0:Tn],
                                                           initial=hst[:, c:c + 1], op0=ALU.mult, op1=ALU.add),
                     [TAV[i2], TIG[c], THST[c]], [THS[c]])
                ts(hst[:, c:c + 1], hs[:, c, Tn - 1:Tn], flags[:, g:g + 1], None, ALU.mult, None, [THS[c]] + CONST, [THST[c]])
            else:
                for s_ in range(4):
                    k.op("dve", lambda h, s_=s_: h.tensor_tensor_scan(out=hs[:, c, s_ * 32:(s_ + 1) * 32], data0=av[i2][:, s_ * 32:(s_ + 1) * 32],
                                                                       data1=ig[c][:, s_ * 32:(s_ + 1) * 32], initial=hS[:, c, s_:s_ + 1],
                                                                       op0=ALU.mult, op1=ALU.add),
                         [TAV[i2], TIG[c], THS_S], [THS[c]])
            if with_out:
                tt(lruo[:, c, 0:Tn], hs[:, c, 0:Tn], G[:, c, 0:Tn], ALU.mult, [THS[c], TG[c]], [TLO[c]], eng="pool")

        def lru_all(Tn, g, sample, with_out):
            for c in range(4):
                lru_p(c, Tn, g, sample)
            for c in range(4):
                lru_q(c, Tn, g, sample, with_out)

        XB = [xn, mixT]
        TXB = [TXN, TMX]
        TXC3 = toks(3, "xcast")

        def lean_cast(g):
            i = g % 3
            k.dma("pool", f"xcast{i}", lambda h: h.dma_start(out=xbf[g * T:(g + 1) * T, :], in_=xs_d[g * T:(g + 1) * T, :]), (), [TXC3[i]])

        def lean_tr(g):
            p = g % 2
            for kk in range(8):
                k.dma("sp", f"xtr{p}_{kk}", lambda h, kk=kk: h.dma_start(out=XB[p][:, kk, :], in_=xbf[g * T:(g + 1) * T, kk * 128:(kk + 1) * 128], transpose=True),
                      [TXC3[g % 3]], [TXB[p][kk]])

        def lean_f0(g):
            p = g % 2
            tt(sq[:, :, :], XB[p][:, :, :], XB[p][:, :, :], ALU.mult, TXB[p], [TSQ], eng="pool")

        def lean_f2(g):
            pg, tpg = nextpg()
            mmg(pg[:, 0:T], [(ones_bf[:], sq[:, kk, :]) for kk in range(8)], [TSQ] + CONST, [tpg])
            act_op(AF.Ln, rt[:, :], pg[:, 0:T], [tpg] + CONST, [TRT], bias=eps_t[:], scale=1.0 / 1024.0)
            act_op(AF.Exp, rstd[:, :], rt[:, :], [TRT], [TRS], scale=-0.5)

        def lean_f3(g):
            xb = g % 2
            for c in range(4):
                pg, tpg = nextpg()
                mmg(pg[:, 0:T], [(w_xr[:, kk, c * 128:(c + 1) * 128], XB[xb][:, kk, :]) for kk in range(8)], TXB[xb] + [Tw_xr], [tpg])
                tt(XR2[xb][:, c, 3:3 + T], pg[:, 0:T], rstd[:, :], ALU.mult, [tpg, TRS], [TXR2[xb][c]])

        def lean_front(g):
            lean_f0(g)
            lean_f2(g)
            lean_f3(g)

        def proj_main(Tn, g, nsb, want_q, want_gr, tok_out, sample, want_xr=False):
            slot6 = [(2 * g + b) % 6 for b in range(nsb)] if not sample else [0]
            blocks = []
            if want_q:
                blocks += list(range(0, 4))
            blocks += list(range(4, 12))
            if want_gr:
                blocks += list(range(12, 20))
            elif want_xr:
                blocks += list(range(12, 16))
            pkt = None
            for j in blocks:
                W, tw = wload(sc_in[j])
                if j < 8 or j >= 12:
                    pg, tpg = nextpg()
                    mmg(pg[:, 0:Tn], [(W[:, kk, :], xn[:, kk, 0:Tn]) for kk in range(8)], [tw] + TXN, [tpg])
                    if j < 4:
                        evac(qT[:, j, 0:Tn], pg[:, 0:Tn], [tpg], [TQ[j]])
                    elif j < 8:
                        if sample:
                            evac(kT[:, j - 4, 0:Tn], pg[:, 0:Tn], [tpg], [TKR[0]])
                        else:
                            c0 = slot6[0] * 128
                            evac(kT[:, j - 4, c0:c0 + Tn], pg[:, 0:Tn], [tpg], [TKR[s_] for s_ in slot6])
                    elif j < 16:
                        c = j - 12
                        if sample:
                            evac(XRs[:, c, :, 3:35], pg[:, 0:Tn].rearrange("p (s t) -> p s t", s=4), [tpg], [TXRS])
                        else:
                            evac(XR2[g % 2][:, c, 3:3 + Tn], pg[:, 0:Tn], [tpg], [TXR2[g % 2][c]])
                    else:
                        c = j - 16
                        act_op(AF.Gelu_apprx_tanh, G[:, c, 0:Tn], pg[:, 0:Tn], [tpg], [TG[c]])
                if (4 <= j < 8 and tok_out) or (8 <= j < 12):
                    jj = (j - 4) % 4
                    if jj == 0:
                        pkt = [(PG[3], TPG[3])] if sample else [(PSS[b_], TPSS[b_]) for b_ in range(nsb)]
                    for b in range(nsb):
                        mmg(pkt[b][0][:, jj * 128:(jj + 1) * 128], [(xn[:, kk, b * 128:(b + 1) * 128], W[:, kk, :]) for kk in range(8)],
                            [tw] + TXN, [pkt[b][1]])
                    if sample and j >= 8:
                        for s_ in range(4):
                            mmg(PSS[s_ // 2][0:32, (s_ % 2) * 512 + jj * 128:(s_ % 2) * 512 + (jj + 1) * 128],
                                [(xn[:, kk, s_ * 32:(s_ + 1) * 32], W[:, kk, :]) for kk in range(8)], [tw] + TXN, [TPSS[s_ // 2]])
                    if jj == 3:
                        for b in range(nsb):
                            pgt, tpgt = pkt[b]
                            if j >= 8:
                                if not sample:
                                    cp("dve", Vr[:, slot6[b], :], pgt[:, 0:512], [tpgt], [TVR[slot6[b]]])
                                else:
                                    for s_ in range(4):
                                        cp("dve", Vn[0:32, s_, :], PSS[s_ // 2][0:32, (s_ % 2) * 512:(s_ % 2 + 1) * 512], [TPSS[s_ // 2]], [TVN])
                            if tok_out:
                                oi = nexto()
                                cp("dve" if (j >= 8 and not sample) else "act", ost[oi][:, 0:512], pgt[:, 0:512], [tpgt], [TOS[oi]])
                                if sample:
                                    dst = (ksm_d if j < 8 else vsm_d)[:, :]
                                else:
                                    r0 = (g - (NT - 2)) * T + b * 128
                                    dst = (kp_d if j < 8 else vp_d)[r0:r0 + 128, :]
                                k.dma("pool", f"os{oi}", lambda h, oi=oi, dst=dst: h.dma_start(out=dst, in_=ost[oi][:, 0:512]), [TOS[oi]], [])

        osc = [0]

        def nexto():
            i = osc[0] % 2
            osc[0] += 1
            return i

        def attention_prompt(g, hook=None):
            expb5 = expb[:].rearrange("p h (k q) -> p h k q", k=5)
            units = [(hp, e, ps_) for hp in range(4) for e in range(2) for ps_ in range(2)]

            def slot_of(kb):
                return (2 * g - 4 + kb) % 6

            def emit_s(ui):
                hp, e, ps_ = units[ui]
                rows = slice(e * 64, (e + 1) * 64)
                S = PSS[ui % 2]
                sl = [slot_of(3 * ps_ + j) for j in range(3)]
                mmlist([(S[:, j * 256:(j + 1) * 256], kT[rows, hp, sl[j] * 128:(sl[j] + 1) * 128], qT[rows, hp, 0:256], True, True) for j in range(3)],
                       [TKR[s_] for s_ in sl] + [TQ[hp]], [TPSS[ui % 2]])

            po_cur = [None]
            emit_s(0)
            for ui, (hp, e, ps_) in enumerate(units):
                h_ = 2 * hp + e
                rows = slice(e * 64, (e + 1) * 64)
                si = ui % 2
                S = PSS[si]
                pi = 2 * ps_ + (ui // 2) % 2
                if ui + 1 < len(units):
                    emit_s(ui + 1)
                if e == 0 and ps_ == 0:
                    po_cur[0] = nextpg()
                po, tpo = po_cur[0]
                act_op(AF.Exp, E[si][:], S[:, 0:768], [TPSS[si]], [TE[si]], scale=SCALE)
                Ev = E[si][:].rearrange("p (k q) -> p k q", k=3)
                Pv = Pm[pi][:].rearrange("p (k q) -> p k q", k=3)
                if ps_ == 0:
                    tt(Pv[:, 0:3, 0:128], Ev[:, 0:3, 0:128], expb5[:, h_, 0:3, :], ALU.mult, [TE[si]] + CONST, [TP[pi]], eng="pool")
                    tt(Pv[:, 1:3, 128:256], Ev[:, 1:3, 128:256], expb5[:, h_, 0:2, :], ALU.mult, [TE[si]] + CONST, [TP[pi]], eng="pool")
                else:
                    tt(Pv[:, 0:2, 0:128], Ev[:, 0:2, 0:128], expb5[:, h_, 3:5, :], ALU.mult, [TE[si]] + CONST, [TP[pi]], eng="pool")
                    tt(Pv[:, 0:3, 128:256], Ev[:, 0:3, 128:256], expb5[:, h_, 2:5, :], ALU.mult, [TE[si]] + CONST, [TP[pi]], eng="pool")
                sl = [slot_of(3 * ps_ + j) for j in range(3)]
                items = [(po[rows, 0:256], Vr[:, sl[j], h_ * 64:(h_ + 1) * 64], Pm[pi][:, j * 256:(j + 1) * 256], (ps_ == 0 and j == 0), False) for j in range(3)]
                items += [(po[rows, 256:512], vones[:, sl[j], :], Pm[pi][:, j * 256:(j + 1) * 256], False, (ps_ == 1 and j == 2)) for j in range(3)]
                mmlist(items, [TP[pi]] + [TVR[s_] for s_ in sl] + [TVO[s_] for s_ in sl], [tpo])
                if hook is not None:
                    hook(ui)
                if e == 1 and ps_ == 1:
                    ri = hp % 2
                    k.op("dve", lambda h, ri=ri, po=po: h.reciprocal(out=rd[ri][:], in_=po[:, 256:512]), [tpo], [TRD[ri]])
                    tt(attnT[:, hp, :], po[:, 0:256], rd[ri][:], ALU.mult, [tpo, TRD[ri]], [TAT[hp]])
                    if g in (NLEAN, NLEAN + 1):
                        for b in range(2):
                            slots = [(2 * g + b - 4 + kb) % 6 for kb in range(5)]
                            pu, tpu = nextpg()
                            items = []
                            for e2 in range(2):
                                h2 = 2 * hp + e2
                                rows2 = slice(e2 * 64, (e2 + 1) * 64)
                                items += [(pu[rows2, 0:128], Vr[:, slots[kb], h2 * 64:(h2 + 1) * 64], band_bf[:, kb * 128:(kb + 1) * 128], kb == 0, kb == 4) for kb in range(5)]
                            mmlist(items, [TVR[s_] for s_ in slots] + CONST, [tpu])
                            ts(attnT[:, hp, b * 128:(b + 1) * 128], attnT[:, hp, b * 128:(b + 1) * 128], nc0, None, ALU.mult, None, [TAT[hp]] + CONST, [TAT[hp]])
                            stt(attnT[:, hp, b * 128:(b + 1) * 128], pu[:, 0:128], c0s, attnT[:, hp, b * 128:(b + 1) * 128], ALU.mult, ALU.add,
                                [tpu, TAT[hp]] + CONST, [TAT[hp]])
            if g == NLEAN + 1:
                ts(Vr[:, 0:4, :], Vr[:, 0:4, :], nc0, None, ALU.mult, None, TVR[0:4] + CONST, TVR[0:4])
                ts(vones[:, 0:4, :], vones[:, 0:4, :], nc0, None, ALU.mult, None, TVO[0:4] + CONST, TVO[0:4])

        def attention_sample():
            si_c = [0]
            for s_ in range(4):
                xk = xtok[s_ % 2]
                kview = xk[:].rearrange("p a (b c) -> p (a b) c", b=2)
                k.dma("pool", f"x{s_ % 2}", lambda h, s_=s_, kview=kview: h.dma_start(out=kview, in_=ck_d[s_].rearrange("(b p) f -> p b f", p=128)), (), [TXT[s_ % 2]])
                k.dma("pool", "vc", lambda h, s_=s_: h.dma_start(out=Vr[:, 0:4, :], in_=cv_d[s_].rearrange("(b p) f -> p b f", p=128)), (), [TVC] + TVR[0:4])
                for hp in range(4):
                    pg, tpg = nextpg()
                    trlist([(pg[:, blk * 128:(blk + 1) * 128], kview[:, blk, hp * 128:(hp + 1) * 128]) for blk in range(4)], [TXT[s_ % 2]] + CONST, [tpg])
                    evac(kT[:, hp, 256:768], pg[:, :], [tpg], [TKC] + TKR[2:6])
                for hp in range(4):
                    po, tpo = nextpg()
                    for e in range(2):
                        h_ = 2 * hp + e
                        rows = slice(e * 64, (e + 1) * 64)
                        si = si_c[0] % 2
                        si_c[0] += 1
                        S = PSS[si]
                        qv = qT[rows, hp, s_ * 32:(s_ + 1) * 32]
                        items = [(S[:, kb * 32:(kb + 1) * 32], kT[rows, hp, 256 + kb * 128:256 + (kb + 1) * 128], qv, True, True) for kb in range(4)]
                        items.append((S[0:32, 128:160], kT[rows, hp, s_ * 32:(s_ + 1) * 32], qv, True, True))
                        mmlist(items, [TKC, TKR[0], TQ[hp]], [TPSS[si]])
                        act_op(AF.Exp, E[si][:, 0:128], S[:, 0:128], [TPSS[si]], [TE[si]], scale=SCALE)
                        act_op(AF.Exp, E[si][0:32, 128:160], S[0:32, 128:160], [TPSS[si]], [TE[si]], scale=SCALE)
                        tt(Pm[si][:, 0:128], E[si][:, 0:128], expbs[:, h_, 0:128], ALU.mult, [TE[si]] + CONST, [TP[si]])
                        tt(Pm[si][0:32, 128:160], E[si][0:32, 128:160], expbs[0:32, h_, 128:160], ALU.mult, [TE[si]] + CONST, [TP[si]])
                        items = [(po[rows, 0:32], Vr[:, kb, h_ * 64:(h_ + 1) * 64], Pm[si][:, kb * 32:(kb + 1) * 32], kb == 0, False) for kb in range(4)]
                        items.append((po[rows, 0:32], Vn[0:32, s_, h_ * 64:(h_ + 1) * 64], Pm[si][0:32, 128:160], False, True))
                        items += [(po[rows, 32:64], ones_bf[:, 0:64], Pm[si][:, kb * 32:(kb + 1) * 32], kb == 0, False) for kb in range(4)]
                        items.append((po[rows, 32:64], ones_bf[0:32, 0:64], Pm[si][0:32, 128:160], False, True))
                        mmlist(items, [TP[si], TVC, TVN] + CONST, [tpo])
                    ri = hp % 2
                    k.op("dve", lambda h, ri=ri, po=po: h.reciprocal(out=rd[ri][:, 0:32], in_=po[:, 32:64]), [tpo], [TRD[ri]])
                    tt(attnT[:, hp, s_ * 32:(s_ + 1) * 32], po[:, 0:32], rd[ri][:, 0:32], ALU.mult, [tpo, TRD[ri]], [TAT[hp]])

        def back_half(Tn, nsb, p_src, p_r0, y_dst, y_r0):
            rms(attnT, TAT, 4, g_attn, mixT, TMX[0:4], Tn, 512.0)
            act_op(AF.Square, sq[:, 0:4, 0:Tn], lruo[:, 0:4, 0:Tn], TLO, [TSQ])
            pg, tpg = nextpg()
            mmg(pg[:, 0:Tn], [(ones_bf[:], sq[:, kk, 0:Tn]) for kk in range(4)], [TSQ] + CONST, [tpg])
            act_op(AF.Ln, rt[:, 0:Tn], pg[:, 0:Tn], [tpg] + CONST, [TRT], bias=eps_t[:], scale=1.0 / 512.0)
            act_op(AF.Exp, rstd[:, 0:Tn], rt[:, 0:Tn], [TRT], [TRS], scale=-0.5)
            for kk in range(4):
                stt(mixT[:, 4 + kk, 0:Tn], lruo[:, kk, 0:Tn], g_lru[:, kk:kk + 1], rstd[:, 0:Tn], ALU.mult, ALU.mult,
                    [TLO[kk], TRS] + CONST, [TMX[4 + kk]])
            for o in range(8):
                W, tw = wload(sc_out[o])
                pg, tpg = nextpg()
                mmg(pg[:, 0:Tn], [(W[:, kk, :], mixT[:, kk, 0:Tn]) for kk in range(8)], [tw] + TMX, [tpg])
                tt(hT[:, o, 0:Tn], hT[:, o, 0:Tn], pg[:, 0:Tn], ALU.add, [THT[o], tpg], [THT[o]])
            rms(hT, THT, 8, g_ffn, xn, TXN, Tn, 1024.0)
            def ffn_gu(f):
                Wg, twg = wload(sc_g[f])
                Wu, twu = wload(sc_u[f])
                Wd, twd = wload(sc_d[f], view1024=True)
                pg, tpg = nextpg()
                items = [(pg[:, 0:Tn], Wg[:, kk, :], xn[:, kk, 0:Tn], kk == 0, kk == 7) for kk in range(8)]
                items += [(pg[:, 256:256 + Tn], Wu[:, kk, :], xn[:, kk, 0:Tn], kk == 0, kk == 7) for kk in range(8)]
                mmlist(items, [twg, twu] + TXN, [tpg])
                return pg, tpg, Wd, twd

            cur = ffn_gu(0)
            for f in range(NF):
                nxt = ffn_gu(f + 1) if f + 1 < NF else None
                pg, tpg, Wd, twd = cur
                Wdv = Wd[:].rearrange("p a b -> p (a b)")
                fi = f % 2
                act_op(AF.Silu, sg[fi][:, 0:Tn], pg[:, 0:Tn], [tpg], [TSG[fi]])
                tt(act[fi][:, 0:Tn], sg[fi][:, 0:Tn], pg[:, 256:256 + Tn], ALU.mult, [TSG[fi], tpg], [TACT[fi]])
                items = [(PSS[o // 4][:, (o % 4) * 256:(o % 4) * 256 + Tn], Wdv[:, o * 128:(o + 1) * 128], act[fi][:, 0:Tn], (f == 0 and o % 2 == 0), (f == NF - 1 and o % 2 == 1)) for o in range(8)]
                mmlist(items, [twd, TACT[fi]], TPSS)
                cur = nxt
            for o in range(8):
                tt(hT[:, o, 0:Tn], hT[:, o, 0:Tn], PSS[o // 4][:, (o % 4) * 256:(o % 4) * 256 + Tn], ALU.add, [THT[o]] + TPSS, [THT[o]])
            rms(hT, THT, 8, g_ple, xn, TXN, Tn, 1024.0)
            k.dma("pool", "ptok", lambda h: h.dma_start(out=ptok[:, 0:nsb, :], in_=p_src[p_r0:p_r0 + Tn, :].rearrange("(b p) f -> p b f", p=128)), (), [TPT])
            pg, tpg = nextpg()
            trlist([(pg[:, (b * 2 + k2) * 128:(b * 2 + k2 + 1) * 128], ptok[:, b, k2 * 128:(k2 + 1) * 128]) for b in range(nsb) for k2 in range(2)],
                   [TPT] + CONST, [tpg])
            for b in range(nsb):
                evac(pT[:, 0:2, b * 128:(b + 1) * 128], pg[:, b * 256:(b + 1) * 256].rearrange("p (a b) -> p a b", a=2), [tpg], [TPTT])
            for o in range(8):
                W, tw = wload(sc_pg[o])
                pg, tpg = nextpg()
                items = [(pg[:, 0:Tn], W[:, kk, :], xn[:, kk, 0:Tn], kk == 0, kk == 7) for kk in range(8)]
                items += [(pg[:, 256:256 + Tn], w_pp[:, k2, o * 128:(o + 1) * 128], pT[:, k2, 0:Tn], k2 == 0, k2 == 1) for k2 in range(2)]
                mmlist(items, [tw, TPTT] + TXN + CONST, [tpg])
                fi = o % 2
                act_op(AF.Sigmoid, sg[fi][:, 0:Tn], pg[:, 0:Tn], [tpg], [TSG[fi]])
                tt(sg[fi][:, 0:Tn], pg[:, 256:256 + Tn], sg[fi][:, 0:Tn], ALU.mult, [TSG[fi], tpg], [TSG[fi]])
                tt(hT[:, o, 0:Tn], hT[:, o, 0:Tn], sg[fi][:, 0:Tn], ALU.add, [THT[o], TSG[fi]], [THT[o]])
            rms(hT, THT, 8, g_fin, hT, THT, Tn, 1024.0)
            for b in range(nsb):
                oi = nexto()
                for half in range(2):
                    pg, tpg = nextpg()
                    trlist([(pg[:, i * 128:(i + 1) * 128], hT[:, half * 4 + i, b * 128:(b + 1) * 128]) for i in range(4)], THT + CONST, [tpg])
                    evac(ost[oi][:, half * 512:(half + 1) * 512], pg[:, :], [tpg], [TOS[oi]])
                r0 = y_r0 + b * 128
                k.dma("pool", f"os{oi}", lambda h, oi=oi, r0=r0: h.dma_start(out=y_dst[r0:r0 + 128, :], in_=ost[oi][:]), [TOS[oi]], [])

        def sample_tile(Ts):
            k.dma("pool", "x0", lambda h: h.dma_start(out=xtok[0][:, 0, :], in_=xsm_d), (), [TXT[0]])
            k.dma("pool", "sst", lambda h: h.dma_start(out=XRs[:, :, :, 0:3], in_=sconv_d), (), [TXRS])
            k.dma("pool", "sst", lambda h: h.dma_start(out=hS[:], in_=sh_d), (), [THS_S])
            TXRS.lastw = (("D", "sst"), 32)
            THS_S.lastw = (("D", "sst"), 32)
            transpose_in(xtok[0], TXT[0], 1, Ts)
            rms(hT, THT, 8, g_mix, xn, TXN, Ts, 1024.0)
            proj_main(Ts, 0, 1, True, True, True, True)
            lru_all(Ts, 0, True, True)
            cp("act", hso[:], hs[:, :, 0:Ts].rearrange("p c (s t) -> p c s t", t=32)[:, :, :, 31], THS, [THSO])
            k.dma("pool", "st3", lambda h: h.dma_start(out=hsm_d, in_=hso[:]), [THSO], [])
            k.dma("pool", "st4", lambda h: h.dma_start(out=convs_d, in_=XRs[:, :, :, 32:35]), [TXRS], [])
            attention_sample()
            back_half(Ts, 1, psm_d, 0, ys_d, 0)

        tile_list = list(range(NT)) if dbg_tiles is None else list(dbg_tiles)
        if dbg_tiles is not None and dbg_conv:
            emit_conv(1000)
        NL2 = NLEAN - 2
        leans = [g for g in tile_list if g < NL2]
        if tile_list[0] >= NL2:
            load_x(tile_list[0])
        else:
            for g_ in leans[:2]:
                lean_cast(g_)
            lean_tr(leans[0])
        fronted = set()
        for ti, g in enumerate(tile_list):
            nxt = tile_list[ti + 1] if ti + 1 < len(tile_list) else None
            if nxt is not None and nxt >= NL2:
                load_x(nxt)
            if g < NLEAN and dbg_conv:
                emit_conv(3)
            if g < NL2:
                li = leans.index(g)
                if li + 2 < len(leans):
                    lean_cast(leans[li + 2])
                if li + 1 < len(leans):
                    lean_tr(leans[li + 1])
                if g not in fronted:
                    lean_front(g)
                if nxt is not None and nxt < NL2:
                    fronted.add(nxt)
                    lru_p(0, T, g, False)
                    lean_f0(nxt)
                    lru_p(1, T, g, False)
                    lru_p(2, T, g, False)
                    lru_p(3, T, g, False)
                    lean_f2(nxt)
                    lru_q(0, T, g, False, False)
                    lru_q(1, T, g, False, False)
                    lean_f3(nxt)
                    lru_q(2, T, g, False, False)
                    lru_q(3, T, g, False, False)
                else:
                    lru_all(T, g, False, False)
                continue
            transpose_in(xtok[g % 2], TXT[g % 2], 2, T)
            rms(hT, THT, 8, g_mix, xn, TXN, T, 1024.0)
            for b in range(2):
                s6 = (2 * g + b) % 6
                ts(vones[:, s6, :], ones_bf[:, 0:64], flags[:, g:g + 1], None, ALU.mult, None, CONST, [TVO[s6]])
            if g < NLEAN:
                proj_main(T, g, 2, False, False, False, False, want_xr=True)
                lru_all(T, g, False, False)
            else:
                proj_main(T, g, 2, True, True, g >= NT - 2, False)
                for c in range(4):
                    lru_p(c, T, g, False)
                attention_prompt(g, hook=lambda ui: lru_q(ui // 4, T, g, False, True) if ui % 4 == 3 else None)
                if _DBG.get("dump") == g:
                    k.dma("pool", "dbgo", lambda h: h.dma_start(out=dbg_d, in_=attnT[:]), TAT, [])
                back_half(T, 2, pm_d, (g - NLEAN) * T, y_d, (g - NLEAN) * T)
        if dbg_conv:
            emit_conv(1000)
        k.dma("pool", "st1", lambda h: h.dma_start(out=convp_d, in_=XR2[NT % 2][:, :, 0:3]), TXR2[NT % 2], [])
        k.dma("pool", "st2", lambda h: h.dma_start(out=hp_d, in_=hst[:]), THST, [])
        Ts = 128
        if dbg_sample:
            sample_tile(Ts)

        def _unused():
            pass

        print("total ops", k.total, {e: len(p) for e, p in k.prog.items()})
        out_keys = [kk for kk in ["os0", "os1", "st1", "st2", "st3", "st4"] if kk in k.dsem] + (["dbgo"] if "dbgo" in k.dsem else [])
        with nc.Block() as block:
            @block.tensor
            def _(h):
                k.replay("pe", h)

            @block.scalar
            def _(h):
                k.replay("act", h)

            @block.vector
            def _(h):
                k.replay("dve", h)

            @block.gpsimd
            def _(h):
                with nc.allow_non_contiguous_dma(reason="small strided state/param transfers"):
                    k.replay("pool", h, out_keys)

            @block.sync
            def _(h):
                k.replay("sp", h)
    return nc


_NC_CACHE = {}


def _fm(v, nk):
    return np.ascontiguousarray(np.asarray(v, np.float32).reshape(nk, 128).T)


def kernel(x_prompt, x_sample, p_prompt, p_sample, cache_k, cache_v, state_conv, state_h,
           g_mix, w_in, conv_w, conv_b, w_rgate, b_rgate, w_igate, b_igate, lru_lambda,
           rel_bias_table, g_attn_out, g_lru_out, w_out, g_ffn, w_ffn_gate, w_ffn_up,
           w_ffn_down, g_ple, w_ple_gate, w_ple_proj, g_final):
    f32 = np.float32
    x_prompt = np.asarray(x_prompt, f32)
    x_sample = np.asarray(x_sample, f32)
    p_prompt = np.asarray(p_prompt, f32)
    p_sample = np.asarray(p_sample, f32)
    cache_k = np.asarray(cache_k, f32)
    cache_v = np.asarray(cache_v, f32)
    state_conv = np.asarray(state_conv, f32)
    state_h = np.asarray(state_h, f32)
    table = np.asarray(rel_bias_table, f32)[0]

    p_idx = np.arange(128)
    biasp = np.full((128, 8, 5, 128), NEG, f32)
    for kb in range(5):
        rel = (kb * 128 + p_idx[:, None] - 512) - p_idx[None, :]
        idx = np.clip(rel, -128, 63) + 128
        kc = (kb * 128 + p_idx[:, None]) // 64 - 8
        qc = p_idx[None, :] // 64
        valid = (kc <= qc) & (kc >= qc - 8)
        vals = table[:, idx]
        blk = np.where(valid[None], vals, f32(NEG))
        biasp[:, :, kb, :] = blk.transpose(1, 0, 2)
    biass = np.full((128, 8, 5, 32), NEG, f32)
    q32 = np.arange(32)
    for kb in range(4):
        rel = (kb * 128 + p_idx[:, None] - 512) - q32[None, :]
        idx = np.clip(rel, -128, 63) + 128
        biass[:, :, kb, :] = table[:, idx].transpose(1, 0, 2)
    rel = q32[:, None] - q32[None, :]
    idx = np.clip(rel, -128, 63) + 128
    biass[0:32, :, 4, :] = table[:, idx].transpose(1, 0, 2)
    cwl = np.asarray(conv_w, f32)[0]
    cwfm = np.ascontiguousarray(cwl.reshape(4, 4, 128).transpose(2, 1, 0)).reshape(128, 16)
    vecs = np.concatenate([
        _fm(g_mix[0], 8), _fm(g_ffn[0], 8), _fm(g_ple[0], 8), _fm(g_final, 8),
        _fm(g_attn_out[0], 4), _fm(g_lru_out[0], 4), _fm(conv_b[0], 4), _fm(b_rgate[0], 4),
        _fm(b_igate[0], 4), _fm(lru_lambda[0], 4), cwfm, np.zeros((128, 2), f32)], axis=1).astype(f32)
    ident = np.eye(128, dtype=f32)
    shared = {
        "band": np.ascontiguousarray((biasp[:, 0] > -1e29).astype(f32).reshape(128, 640)),
        "biasp": biasp.reshape(128, 8 * 640), "biass": biass.reshape(128, 8 * 160), "ident": ident, "vecs": vecs,
        "w_in": np.asarray(w_in, f32)[0], "w_out": np.asarray(w_out, f32)[0],
        "w_ffn_gate": np.asarray(w_ffn_gate, f32)[0], "w_ffn_up": np.asarray(w_ffn_up, f32)[0],
        "w_ffn_down": np.asarray(w_ffn_down, f32)[0], "w_ple_gate": np.asarray(w_ple_gate, f32)[0],
        "w_ple_proj": np.asarray(w_ple_proj, f32)[0], "w_rgate": np.asarray(w_rgate, f32)[0],
        "w_igate": np.asarray(w_igate, f32)[0],
    }
    in_maps = []
    for c in range(8):
        s, j = c // 4, c % 4
        npad = (3 - j) * SEG
        xs = np.zeros((NT * T, D), f32)
        xs[npad:] = x_prompt[s, 0:(j + 1) * SEG]
        fl = np.zeros((NT,), f32)
        fl[npad // T:] = 1.0
        sq_ = slice(4 * c, 4 * c + 4)
        m = dict(shared)
        vc_ = vecs.copy()
        vc_[:, 72] = (1.0 / 576.0) if j == 0 else 0.0
        vc_[:, 73] = 0.0 if j == 0 else 1.0
        m["vecs"] = vc_
        m.update({
            "xs": xs, "pm": np.ascontiguousarray(p_prompt[0, s, j * SEG:(j + 1) * SEG]),
            "xsm": np.ascontiguousarray(x_sample[sq_].reshape(128, D)),
            "psm": np.ascontiguousarray(p_sample[0, sq_].reshape(128, 256)),
            "ck": np.ascontiguousarray(cache_k[0, sq_].reshape(4, 512, 512)),
            "cv": np.ascontiguousarray(cache_v[0, sq_].reshape(4, 512, 512)),
            "sconv": np.ascontiguousarray(state_conv[0, sq_].reshape(4, 3, 4, 128).transpose(3, 2, 0, 1)),
            "sh": np.ascontiguousarray(state_h[0, sq_].reshape(4, 4, 128).transpose(2, 1, 0)),
            "flags": np.ascontiguousarray(np.broadcast_to(fl[None, :], (128, NT))),
        })
        in_maps.append(m)
    if _DBG.get("prep_only"):
        return in_maps
    if "nc" not in _NC_CACHE:
        _NC_CACHE["nc"] = build_nc()
    res = run_bass_kernel_spmd(_NC_CACHE["nc"], in_maps, core_ids=list(range(8)))
    R = res.results
    y_prompt = np.stack([np.concatenate([R[s * 4 + j]["y"] for j in range(4)], axis=0) for s in range(2)]).astype(f32)
    y_sample = np.concatenate([R[c]["ys"].reshape(4, 32, D) for c in range(8)], axis=0).astype(f32)
    new_k_prompt = np.stack([R[s * 4 + 3]["kp"].reshape(512, 8, 64) for s in range(2)])[None].astype(f32)
    new_v_prompt = np.stack([R[s * 4 + 3]["vp"].reshape(512, 8, 64) for s in range(2)])[None].astype(f32)
    new_conv_prompt = np.stack([R[s * 4 + 3]["convp"].transpose(2, 1, 0).reshape(3, 512) for s in range(2)])[None].astype(f32)
    new_h_prompt = np.stack([R[s * 4 + 3]["hp"].T.reshape(512) for s in range(2)])[None].astype(f32)
    new_k_sample = np.concatenate([R[c]["ksm"].reshape(4, 32, 8, 64) for c in range(8)], axis=0)[None].astype(f32)
    new_v_sample = np.concatenate([R[c]["vsm"].reshape(4, 32, 8, 64) for c in range(8)], axis=0)[None].astype(f32)
    new_conv_sample = np.concatenate([R[c]["convs"].transpose(2, 3, 1, 0).reshape(4, 3, 512) for c in range(8)], axis=0)[None].astype(f32)
    new_h_sample = np.concatenate([R[c]["hsm"].transpose(2, 1, 0).reshape(4, 512) for c in range(8)], axis=0)[None].astype(f32)
    return (y_prompt, y_sample, new_k_prompt, new_v_prompt, new_conv_prompt, new_h_prompt,
            new_k_sample, new_v_sample, new_conv_sample, new_h_sample)
```

```python
import contextlib
import numpy as np
import concourse.bass as bass
import concourse.mybir as mybir
from concourse.bass_utils import run_bass_kernel_spmd

F32 = mybir.dt.float32
BF16 = mybir.dt.bfloat16
AF = mybir.ActivationFunctionType
ALU = mybir.AluOpType

D = 1024
T = 256
NLEAN = 48
NMAIN = 16
NT = NLEAN + NMAIN
SEG = 4096
DFF = 2816
NF = DFF // 128
SCALE = 0.125
EPS = 1e-6
NEG = -1e30
NWS = 10
FILL = {}


_DBG = {}


class Tok:
    __slots__ = ("lastw", "readers", "name")

    def __init__(self, name=""):
        self.lastw = None
        self.readers = []
        self.name = name


def toks(n, name=""):
    return [Tok(name + str(i)) for i in range(n)]


class KB:
    def __init__(self, nc, st):
        self.nc = nc
        self.st = st
        self.eh = {"pe": nc.tensor, "act": nc.scalar, "dve": nc.vector, "pool": nc.gpsimd, "sp": nc.sync}
        self.prog = {e: [] for e in self.eh}
        self.sem = {e: st.enter_context(nc.semaphore("sem_" + e)) for e in self.eh}
        self.cnt = {e: 0 for e in self.eh}
        self.seen = {e: {} for e in self.eh}
        self.dsem = {}
        self.total = 0
        self.limit = _DBG.get("limit")

    def semh(self, k):
        return self.sem[k] if isinstance(k, str) else self.dsem[k[1]][0]

    def _deps(self, eng, reads, writes):
        w = {}
        for b in reads:
            if b.lastw:
                k, v = b.lastw
                w[k] = max(w.get(k, 0), v)
        for b in writes:
            if b.lastw:
                k, v = b.lastw
                w[k] = max(w.get(k, 0), v)
            for k, v in b.readers:
                w[k] = max(w.get(k, 0), v)
        out = []
        for k, v in w.items():
            if k == eng and eng == "pe":
                continue
            if self.seen[eng].get(k, 0) < v:
                self.seen[eng][k] = v
                out.append((k, v))
        return out

    def op(self, eng, fn, reads=(), writes=()):
        self.total += 1
        if self.limit is not None and self.total > self.limit:
            return
        wl = self._deps(eng, reads, writes)
        self.cnt[eng] += 1
        v = self.cnt[eng]
        self.prog[eng].append((wl, fn, (eng, 1)))
        for b in reads:
            b.readers.append((eng, v))
        for b in writes:
            b.lastw = (eng, v)
            b.readers = []

    def dma(self, q, key, fn, reads=(), writes=()):
        self.total += 1
        if self.limit is not None and self.total > self.limit:
            return
        wl = self._deps(q, reads, writes)
        if key not in self.dsem:
            self.dsem[key] = [self.st.enter_context(self.nc.semaphore("d_" + key)), 0]
        d = self.dsem[key]
        d[1] += 16
        v = d[1]
        sk = ("D", key)
        self.prog[q].append((wl, fn, (sk, 16)))
        for b in reads:
            b.readers.append((sk, v))
        for b in writes:
            b.lastw = (sk, v)
            b.readers = []

    def replay(self, e, h, final_keys=()):
        for wl, fn, (sk, n) in self.prog[e]:
            for k, v in wl:
                h.wait_ge(self.semh(k), v)
            ins = fn(h)
            ins.then_inc(self.semh(sk), n)
        for key in final_keys:
            h.wait_ge(self.dsem[key][0], self.dsem[key][1])


def build_nc(dbg_tiles=None, dbg_sample=True, dbg_conv=True):
    nc = bass.Bass("TRN2", target_bir_lowering=False)

    def din(name, shape, dt=F32):
        return nc.dram_tensor(name, list(shape), dt, kind="ExternalInput").ap()

    def dout(name, shape):
        return nc.dram_tensor(name, list(shape), F32, kind="ExternalOutput").ap()

    def dint(name, shape, dt=BF16):
        return nc.dram_tensor(name, list(shape), dt, kind="Internal").ap()

    xs_d = din("xs", [NT * T, D])
    pm_d = din("pm", [SEG, 256])
    xsm_d = din("xsm", [128, D])
    psm_d = din("psm", [128, 256])
    ck_d = din("ck", [4, 512, 512])
    cv_d = din("cv", [4, 512, 512])
    sconv_d = din("sconv", [128, 4, 4, 3])
    sh_d = din("sh", [128, 4, 4])
    flags_d = din("flags", [128, NT])
    biasp_d = din("biasp", [128, 8 * 640])
    biass_d = din("biass", [128, 8 * 160])
    ident_d = din("ident", [128, 128])
    vec_d = din("vecs", [128, 74])
    band_d = din("band", [128, 640])
    w_in_d = din("w_in", [D, 2560])
    w_out_d = din("w_out", [D, D])
    w_g_d = din("w_ffn_gate", [D, DFF])
    w_u_d = din("w_ffn_up", [D, DFF])
    w_d_d = din("w_ffn_down", [DFF, D])
    w_pg_d = din("w_ple_gate", [D, D])
    w_pp_d = din("w_ple_proj", [256, D])
    w_r_d = din("w_rgate", [8, 64, 64])
    w_i_d = din("w_igate", [8, 64, 64])
    y_d = dout("y", [SEG, D])
    ys_d = dout("ys", [128, D])
    kp_d = dout("kp", [512, 512])
    vp_d = dout("vp", [512, 512])
    convp_d = dout("convp", [128, 4, 3])
    hp_d = dout("hp", [128, 4])
    ksm_d = dout("ksm", [128, 512])
    vsm_d = dout("vsm", [128, 512])
    convs_d = dout("convs", [128, 4, 4, 3])
    hsm_d = dout("hsm", [128, 4, 4])
    dbg_d = dout("dbgo", [128, 4, T]) if _DBG.get("dump") is not None else None
    sc_in = dint("sc_in", [20, 128, 8, 128])
    sc_out = dint("sc_out", [8, 128, 8, 128])
    sc_g = dint("sc_g", [NF, 128, 8, 128])
    sc_u = dint("sc_u", [NF, 128, 8, 128])
    sc_d = dint("sc_d", [NF, 128, 1024])
    sc_pg = dint("sc_pg", [8, 128, 8, 128])
    xbf = dint("xbf", [(NLEAN - 2) * T, D])

    with contextlib.ExitStack() as st:
        def sb(name, shape, dt=F32):
            return st.enter_context(nc.sbuf_tensor("s_" + name, list(shape), dt))

        def ps(name, shape, dt=F32):
            return st.enter_context(nc.psum_tensor("p_" + name, list(shape), dt))

        k = KB(nc, st)
        ident = sb("ident", [128, 128])
        ones_bf = sb("ones_bf", [128, 128], BF16)
        eps_t = sb("eps_t", [128, 1])
        one_t = sb("one_t", [128, 1])
        vecs = sb("vecs", [128, 74])
        band_bf = sb("band_bf", [128, 640], BF16)
        flags = sb("flags", [128, NT])
        nsp = sb("nsp", [128, 4])
        nsp2 = sb("nsp2", [128, 4])
        nb = sb("nb", [128, 8])
        lamt = sb("lamt", [128, 4])
        expb = sb("expb", [128, 8, 640])
        expbs = sb("expbs", [128, 8, 160])
        w_xr = sb("w_xr", [128, 8, 512], BF16)
        w_pp = sb("w_pp", [128, 2, 1024], BF16)
        wr_bd = sb("wr_bd", [128, 4, 128], BF16)
        wi_bd = sb("wi_bd", [128, 4, 128], BF16)
        WS = [sb(f"ws{i}", [128, 8, 128], BF16) for i in range(NWS)]
        xtok = [sb(f"xtok{i}", [128, 2, 1024]) for i in range(2)]
        ptok = sb("ptok", [128, 2, 256])
        pT = sb("pT", [128, 2, T], BF16)
        hT = sb("hT", [128, 8, T])
        xn = sb("xn", [128, 8, T], BF16)
        sq = sb("sq", [128, 8, T], BF16)
        rt = sb("rt", [128, T])
        rstd = sb("rstd", [128, T])
        qT = sb("qT", [128, 4, T], BF16)
        kT = sb("kT", [128, 4, 768], BF16)
        Vr = sb("Vr", [128, 6, 512], BF16)
        vones = sb("vones", [128, 6, 64], BF16)
        XR2 = [sb(f"XR{i}", [128, 4, T + 3]) for i in range(2)]
        XRs = sb("XRs", [128, 4, 4, 35])
        G = sb("G", [128, 4, T])
        hs = sb("hs", [128, 4, T])
        lruo = sb("lruo", [128, 4, T])
        hst = sb("hst", [128, 4])
        hS = sb("hS", [128, 4, 4])
        hso = sb("hso", [128, 4, 4])
        xc = [sb(f"xc{i}", [128, T]) for i in range(2)]
        xc2 = [sb(f"xc2{i}", [128, T]) for i in range(2)]
        xcb = [sb(f"xcb{i}", [128, T], BF16) for i in range(2)]
        rg = [sb(f"rg{i}", [128, T]) for i in range(4)]
        ig = [sb(f"ig{i}", [128, T]) for i in range(4)]
        av = [sb(f"av{i}", [128, T]) for i in range(2)]
        a2 = [sb(f"a2{i}", [128, T]) for i in range(2)]
        E = [sb(f"E{i}", [128, 768]) for i in range(2)]
        Pm = [sb(f"Pm{i}", [128, 768], BF16) for i in range(4)]
        rd = [sb(f"rd{i}", [128, 256]) for i in range(2)]
        attnT = sb("attnT", [128, 4, T])
        mixT = sb("mixT", [128, 8, T], BF16)
        sg = [sb(f"sg{i}", [128, T]) for i in range(2)]
        act = [sb(f"act{i}", [128, T], BF16) for i in range(2)]
        ost = [sb(f"ost{i}", [128, 1024]) for i in range(2)]
        Vn = sb("Vn", [32, 4, 512], BF16)
        PG = [ps(f"pg{i}", [128, 512]) for i in range(4)]
        PSS = [ps(f"pss{i}", [128, 1024]) for i in range(2)]
        TPG = toks(4, "pg")
        TPSS = toks(2, "pss")
        pgc = [0]

        def nextpg():
            i = pgc[0] % 3
            pgc[0] += 1
            return PG[i], TPG[i]

        Tconst = Tok("const")
        Tw_xr = Tok("w_xr")
        THT = toks(8, "hT")
        TXN = toks(8, "xn")
        TSQ, TRT, TRS = Tok("sq"), Tok("rt"), Tok("rstd")
        TQ = toks(4, "q")
        TKR = toks(6, "kr")
        TVR = toks(6, "vr")
        TVO = toks(6, "vo")
        TXR2 = [toks(4, "xra"), toks(4, "xrb")]
        TG = toks(4, "g")
        THS = toks(4, "hs")
        TLO = toks(4, "lo")
        THST = toks(4, "hst")
        TXC, TXCB, TAV, TA2 = (toks(2, n) for n in ("xc", "xcb", "av", "a2"))
        TRG, TIG = toks(4, "rg"), toks(4, "ig")
        TXC2 = toks(2, "xc2")
        TE, TP, TRD = toks(2, "E"), toks(4, "P"), toks(2, "rd")
        TAT = toks(4, "at")
        TMX = toks(8, "mx")
        TSG, TACT = toks(2, "sg"), toks(2, "act")
        TOS = toks(2, "os")
        TXT = toks(2, "xt")
        TPT, TPTT = Tok("ptok"), Tok("pT")
        TWS = toks(NWS, "ws")
        TSCR = Tok("scr")
        TVC, TKC, TVN = Tok("vc"), Tok("kc"), Tok("vn")
        TXRS, THS_S, THSO = Tok("xrs"), Tok("hS"), Tok("hso")
        setup_toks = []

        def act_op(func, out, in_, reads, writes, bias=None, scale=None):
            def fn(h):
                kw = {}
                if bias is not None:
                    kw["bias"] = bias
                if scale is not None:
                    kw["scale"] = scale
                return h.activation(out=out, in_=in_, func=func, **kw)
            k.op("act", fn, reads, writes)

        def tt(out, in0, in1, op, reads, writes, eng="dve"):
            k.op(eng, lambda h: h.tensor_tensor(out=out, in0=in0, in1=in1, op=op), reads, writes)

        def ts(out, in0, s1, s2, op0, op1, reads, writes):
            if op1 is None:
                k.op("dve", lambda h: h.tensor_scalar(out=out, in0=in0, scalar1=s1, scalar2=None, op0=op0), reads, writes)
            else:
                k.op("dve", lambda h: h.tensor_scalar(out=out, in0=in0, scalar1=s1, scalar2=s2, op0=op0, op1=op1), reads, writes)

        def stt(out, in0, scalar, in1, op0, op1, reads, writes):
            k.op("dve", lambda h: h.scalar_tensor_tensor(out=out, in0=in0, scalar=scalar, in1=in1, op0=op0, op1=op1), reads, writes)

        def cp(eng, out, in_, reads, writes):
            if eng == "act":
                k.op("act", lambda h: h.copy(out=out, in_=in_), reads, writes)
            else:
                k.op("dve", lambda h: h.tensor_copy(out=out, in_=in_), reads, writes)

        evc = [0]

        def evac(out, in_, reads, writes):
            evc[0] += 1
            cp("act" if evc[0] % 2 else "dve", out, in_, reads, writes)

        def mmg(out, pairs, reads, writes, start=True, stop=True):
            def fn(h):
                n = len(pairs)
                ins = None
                for i, (l, r) in enumerate(pairs):
                    ins = h.matmul(out, l, r, start=(start and i == 0), stop=(stop and i == n - 1))
                return ins
            k.op("pe", fn, reads, writes)

        def mmlist(items, reads, writes):
            def fn(h):
                ins = None
                for (o, l, r, s0, s1) in items:
                    ins = h.matmul(o, l, r, start=s0, stop=s1)
                return ins
            k.op("pe", fn, reads, writes)

        def filler(n):
            if n <= 0:
                return
            def fn(h):
                ins = None
                for _ in range(n):
                    ins = h.matmul(PG[3][:, 0:256], ones_bf[:], w_xr[:, 0, 0:256], start=True, stop=True)
                return ins
            k.op("pe", fn, (), ())

        def trlist(items, reads, writes):
            def fn(h):
                ins = None
                for (o, i_) in items:
                    ins = h.transpose(o, i_, ident[:])
                return ins
            k.op("pe", fn, reads, writes)

        def sdma(out, in_, writes):
            k.dma("pool", "setup", lambda h: h.dma_start(out=out, in_=in_), (), writes)
            setup_toks.extend(writes)

        wsc = [0]

        def wload(src_ap, view1024=False):
            i = wsc[0] % NWS
            wsc[0] += 1
            dst = WS[i][:].rearrange("p a b -> p (a b)") if view1024 else WS[i][:]
            k.dma("sp", f"ws{i}", lambda h: h.dma_start(out=dst, in_=src_ap), [TSCR], [TWS[i]])
            return WS[i], TWS[i]

        sdma(ident[:], ident_d, [Tconst])
        sdma(vecs[:], vec_d, [Tconst])
        sdma(flags[:], flags_d, [Tconst])
        sdma(E[0][:, 0:640], band_d, [Tconst])
        sdma(expb[:].rearrange("p a b -> p (a b)"), biasp_d, [Tconst])
        sdma(expbs[:].rearrange("p a b -> p (a b)"), biass_d, [Tconst])
        sdma(w_pp[:], w_pp_d.rearrange("(k p) c -> p k c", p=128), [Tconst])
        Tbd = Tok("bd")
        k.op("dve", lambda h: h.memset(wr_bd[:], 0.0), (), [Tbd])
        k.op("dve", lambda h: h.memset(wi_bd[:], 0.0), (), [Tbd])
        for c in range(4):
            for e in range(2):
                sdma(wr_bd[e * 64:(e + 1) * 64, c, e * 64:(e + 1) * 64], w_r_d[2 * c + e], [Tbd])
                sdma(wi_bd[e * 64:(e + 1) * 64, c, e * 64:(e + 1) * 64], w_i_d[2 * c + e], [Tbd])
        fin = ("D", "setup"), (k.dsem["setup"][1] if "setup" in k.dsem else 0)
        for t_ in setup_toks:
            t_.lastw = fin
        Tc2 = Tok("c2")
        k.op("dve", lambda h: h.memset(ones_bf[:], 1.0), (), [Tc2])
        k.op("dve", lambda h: h.tensor_copy(out=band_bf[:], in_=E[0][:, 0:640]), [Tconst], [Tc2])
        k.op("dve", lambda h: h.memset(eps_t[:], EPS), (), [Tc2])
        k.op("dve", lambda h: h.memset(one_t[:], 1.0), (), [Tc2])
        for i_ in range(4):
            k.op("dve", lambda h, i_=i_: h.memset(Pm[i_][:], 0.0), (), [TP[i_]])
        k.op("dve", lambda h: h.memset(hst[:], 0.0), (), THST)
        k.op("dve", lambda h: h.memset(XR2[0][:], 0.0), (), TXR2[0])
        k.op("dve", lambda h: h.memset(XR2[1][:], 0.0), (), TXR2[1])
        k.op("dve", lambda h: h.memset(vones[:], 0.0), (), TVO)
        k.op("dve", lambda h: h.memset(kT[:], 0.0), (), TKR)
        k.op("dve", lambda h: h.memset(Vr[:], 0.0), (), TVR)
        act_op(AF.Exp, expb[:], expb[:], [Tconst], [Tconst])
        act_op(AF.Exp, expbs[:], expbs[:], [Tconst], [Tconst])
        act_op(AF.Exp, lamt[:], vecs[:, 52:56], [Tconst], [Tc2], scale=-1.0)
        act_op(AF.Ln, lamt[:], lamt[:], [Tc2], [Tc2], bias=one_t[:])
        ts(nsp[:], lamt[:], -8.0, None, ALU.mult, None, [Tc2], [Tc2])
        ts(nsp2[:], lamt[:], -16.0, None, ALU.mult, None, [Tc2], [Tc2])
        ts(nb[:], vecs[:, 44:52], -1.0, None, ALU.mult, None, [Tconst], [Tc2])
        CONST = [Tconst, Tc2, Tbd]
        g_mix, g_ffn, g_ple, g_fin = vecs[:, 0:8], vecs[:, 8:16], vecs[:, 16:24], vecs[:, 24:32]
        g_attn, g_lru = vecs[:, 32:36], vecs[:, 36:40]
        cb, br, bi = vecs[:, 40:44], vecs[:, 44:48], vecs[:, 48:52]
        cw = vecs[:, 56:72]
        c0s, nc0 = vecs[:, 72:73], vecs[:, 73:74]

        for half in range(2):
            stg = xtok[0][:].rearrange("p a (b c) -> p (a b) c", b=2)
            k.dma("pool", "x0", lambda h, half=half, stg=stg: h.dma_start(
                out=stg, in_=w_in_d[half * 512:(half + 1) * 512, 1536:2048].rearrange("(k p) c -> p k c", p=128)), (), [TXT[0]])
            for kk in range(4):
                ts(w_xr[:, half * 4 + kk, :], stg[:, kk, :], g_mix[:, half * 4 + kk:half * 4 + kk + 1], None, ALU.mult, None,
                   [TXT[0]] + CONST, [Tw_xr])
        conv_list = []
        for j in range(20):
            conv_list.append((sc_in[j], w_in_d[:, j * 128:(j + 1) * 128].rearrange("(k p) c -> p k c", p=128)))
        for j in range(8):
            conv_list.append((sc_out[j], w_out_d[:, j * 128:(j + 1) * 128].rearrange("(k p) c -> p k c", p=128)))
        for f in range(NF):
            conv_list.append((sc_g[f], w_g_d[:, f * 128:(f + 1) * 128].rearrange("(k p) c -> p k c", p=128)))
            conv_list.append((sc_u[f], w_u_d[:, f * 128:(f + 1) * 128].rearrange("(k p) c -> p k c", p=128)))
            conv_list.append((sc_d[f], w_d_d[f * 128:(f + 1) * 128, :]))
        for j in range(8):
            conv_list.append((sc_pg[j], w_pg_d[:, j * 128:(j + 1) * 128].rearrange("(k p) c -> p k c", p=128)))
        conv_pos = [0]
        conv_tok = []

        def emit_conv(n):
            for _ in range(n):
                if conv_pos[0] >= len(conv_list):
                    return
                o, i_ = conv_list[conv_pos[0]]
                conv_pos[0] += 1
                k.dma("pool", "wconv", lambda h, o=o, i_=i_: h.dma_start(out=o, in_=i_), (), [TSCR])
                if conv_pos[0] == len(conv_list):
                    TSCR.lastw = (("D", "wconv"), k.dsem["wconv"][1])

        def load_x(g):
            s_ = g % 2
            for b in range(2):
                r0 = g * T + b * 128
                k.dma("pool", f"x{s_}", lambda h, s_=s_, b=b, r0=r0: h.dma_start(out=xtok[s_][:, b, :], in_=xs_d[r0:r0 + 128, :]), (), [TXT[s_]])

        def transpose_in(src_tile, src_tok, nsb, Tn):
            for b in range(nsb):
                for kq in range(2):
                    pg, tpg = nextpg()
                    items = [(pg[:, i * 128:(i + 1) * 128], src_tile[:, b, (kq * 4 + i) * 128:(kq * 4 + i + 1) * 128]) for i in range(4)]
                    trlist(items, [src_tok] + CONST, [tpg])
                    evac(hT[:, kq * 4:(kq + 1) * 4, b * 128:(b + 1) * 128], pg[:, :].rearrange("p (a b) -> p a b", a=4), [tpg], THT[kq * 4:(kq + 1) * 4])

        def rms(src, stoks, nk, gvec, dst, dtoks, Tn, dim):
            act_op(AF.Square, sq[:, 0:nk, 0:Tn], src[:, 0:nk, 0:Tn], stoks, [TSQ])
            pg, tpg = nextpg()
            mmg(pg[:, 0:Tn], [(ones_bf[:], sq[:, kk, 0:Tn]) for kk in range(nk)], [TSQ] + CONST, [tpg])
            filler(FILL.get("rms", 0))
            act_op(AF.Ln, rt[:, 0:Tn], pg[:, 0:Tn], [tpg] + CONST, [TRT], bias=eps_t[:], scale=1.0 / dim)
            act_op(AF.Exp, rstd[:, 0:Tn], rt[:, 0:Tn], [TRT], [TRS], scale=-0.5)
            for kk in range(nk):
                stt(dst[:, kk, 0:Tn], src[:, kk, 0:Tn], gvec[:, kk:kk + 1], rstd[:, 0:Tn], ALU.mult, ALU.mult,
                    [stoks[kk], TRS] + CONST, [dtoks[kk]])

        def lru_p(c, Tn, g, sample):
            xb = g % 2
            XRc, XRn = XR2[xb], XR2[1 - xb]
            i2 = c % 2
            if not sample:
                xin = lambda j: XRc[:, c, j:j + Tn]
                xcv = xc[i2][:, 0:Tn]
                xrt = [TXR2[xb][c]]
            else:
                xin = lambda j: XRs[:, c, :, j:j + 32]
                xcv = xc[i2][:, 0:Tn].rearrange("p (s t) -> p s t", s=4)
                xrt = [TXRS]
            x2v = xc2[i2][:, 0:Tn] if not sample else xc2[i2][:, 0:Tn].rearrange("p (s t) -> p s t", s=4)
            ts(xcv, xin(3), cw[:, c * 4 + 3:c * 4 + 4], cb[:, c:c + 1], ALU.mult, ALU.add, xrt + CONST, [TXC[i2]])
            ts(x2v, xin(1), cw[:, c * 4 + 1:c * 4 + 2], None, ALU.mult, None, xrt + CONST, [TXC2[i2]])
            stt(xcv, xin(0), cw[:, c * 4 + 0:c * 4 + 1], xcv, ALU.mult, ALU.add, xrt + [TXC[i2]] + CONST, [TXC[i2]])
            stt(x2v, xin(2), cw[:, c * 4 + 2:c * 4 + 3], x2v, ALU.mult, ALU.add, xrt + [TXC2[i2]] + CONST, [TXC2[i2]])
            tt(xcv, xcv, x2v, ALU.add, [TXC[i2], TXC2[i2]], [TXC[i2]])
            cp("dve", xcb[i2][:, 0:Tn], xc[i2][:, 0:Tn], [TXC[i2]], [TXCB[i2]])
            if not sample:
                cp("dve", XRn[:, c, 0:3], XRc[:, c, Tn:Tn + 3], [TXR2[xb][c]], [TXR2[1 - xb][c]])
            if g >= NLEAN and not sample:
                filler(FILL.get("lru", 0))
            pg, tpg = nextpg()
            mmlist([(pg[:, 0:Tn], wr_bd[:, c, :], xcb[i2][:, 0:Tn], True, True),
                    (pg[:, 256:256 + Tn], wi_bd[:, c, :], xcb[i2][:, 0:Tn], True, True)], [TXCB[i2]] + CONST, [tpg])
            act_op(AF.Sigmoid, rg[c][:, 0:Tn], pg[:, 0:Tn], [tpg] + CONST, [TRG[c]], bias=br[:, c:c + 1])
            act_op(AF.Sigmoid, ig[c][:, 0:Tn], pg[:, 256:256 + Tn], [tpg] + CONST, [TIG[c]], bias=bi[:, c:c + 1])
            tt(ig[c][:, 0:Tn], ig[c][:, 0:Tn], xc[i2][:, 0:Tn], ALU.mult, [TIG[c], TXC[i2]], [TIG[c]], eng="pool")

        def lru_q(c, Tn, g, sample, with_out):
            i2 = c % 2
            act_op(AF.Exp, av[i2][:, 0:Tn], rg[c][:, 0:Tn], [TRG[c]] + CONST, [TAV[i2]], scale=nsp[:, c:c + 1])
            act_op(AF.Exp, a2[i2][:, 0:Tn], rg[c][:, 0:Tn], [TRG[c]] + CONST, [TA2[i2]], scale=nsp2[:, c:c + 1])
            act_op(AF.Ln, a2[i2][:, 0:Tn], a2[i2][:, 0:Tn], [TA2[i2]] + CONST, [TA2[i2]], bias=one_t[:], scale=-1.0)
            act_op(AF.Exp, a2[i2][:, 0:Tn], a2[i2][:, 0:Tn], [TA2[i2]], [TA2[i2]], scale=0.5)
            tt(ig[c][:, 0:Tn], ig[c][:, 0:Tn], a2[i2][:, 0:Tn], ALU.mult, [TIG[c], TA2[i2]], [TIG[c]], eng="dve")
            if not sample:
                k.op("dve", lambda h: h.tensor_tensor_scan(out=hs[:, c, 0:Tn], data0=av[i2][:, 0:Tn], data1=ig[c][:, 0:Tn],
                                                           initial=hst[:, c:c + 1], op0=ALU.mult, op1=ALU.add),
                     [TAV[i2], TIG[c], THST[c]], [THS[c]])
                ts(hst[:, c:c + 1], hs[:, c, Tn - 1:Tn], flags[:, g:g + 1], None, ALU.mult, None, [THS[c]] + CONST, [THST[c]])
            else:
                for s_ in range(4):
                    k.op("dve", lambda h, s_=s_: h.tensor_tensor_scan(out=hs[:, c, s_ * 32:(s_ + 1) * 32], data0=av[i2][:, s_ * 32:(s_ + 1) * 32],
                                                                       data1=ig[c][:, s_ * 32:(s_ + 1) * 32], initial=hS[:, c, s_:s_ + 1],
                                                                       op0=ALU.mult, op1=ALU.add),
                         [TAV[i2], TIG[c], THS_S], [THS[c]])
            if with_out:
                tt(lruo[:, c, 0:Tn], hs[:, c, 0:Tn], G[:, c, 0:Tn], ALU.mult, [THS[c], TG[c]], [TLO[c]], eng="pool")

        def lru_all(Tn, g, sample, with_out):
            for c in range(4):
                lru_p(c, Tn, g, sample)
            for c in range(4):
                lru_q(c, Tn, g, sample, with_out)

        XB = [xn, mixT]
        TXB = [TXN, TMX]
        TXC3 = toks(3, "xcast")

        def lean_cast(g):
            i = g % 3
            k.dma("pool", f"xcast{i}", lambda h: h.dma_start(out=xbf[g * T:(g + 1) * T, :], in_=xs_d[g * T:(g + 1) * T, :]), (), [TXC3[i]])

        def lean_tr(g):
            p = g % 2
            for kk in range(8):
                k.dma("sp", f"xtr{p}_{kk}", lambda h, kk=kk: h.dma_start(out=XB[p][:, kk, :], in_=xbf[g * T:(g + 1) * T, kk * 128:(kk + 1) * 128], transpose=True),
                      [TXC3[g % 3]], [TXB[p][kk]])

        def lean_f0(g):
            p = g % 2
            act_op(AF.Square, sq[:, :, :], XB[p][:, :, :], TXB[p], [TSQ])

        def lean_f2(g):
            pg, tpg = nextpg()
            mmg(pg[:, 0:T], [(ones_bf[:], sq[:, kk, :]) for kk in range(8)], [TSQ] + CONST, [tpg])
            act_op(AF.Ln, rt[:, :], pg[:, 0:T], [tpg] + CONST, [TRT], bias=eps_t[:], scale=1.0 / 1024.0)
            act_op(AF.Exp, rstd[:, :], rt[:, :], [TRT], [TRS], scale=-0.5)

        def lean_f3(g):
            xb = g % 2
            for c in range(4):
                pg, tpg = nextpg()
                mmg(pg[:, 0:T], [(w_xr[:, kk, c * 128:(c + 1) * 128], XB[xb][:, kk, :]) for kk in range(8)], TXB[xb] + [Tw_xr], [tpg])
                tt(XR2[xb][:, c, 3:3 + T], pg[:, 0:T], rstd[:, :], ALU.mult, [tpg, TRS], [TXR2[xb][c]])

        def lean_front(g):
            lean_f0(g)
            lean_f2(g)
            lean_f3(g)

        def proj_main(Tn, g, nsb, want_q, want_gr, tok_out, sample, want_xr=False):
            slot6 = [(2 * g + b) % 6 for b in range(nsb)] if not sample else [0]
            blocks = []
            if want_q:
                blocks += list(range(0, 4))
            blocks += list(range(4, 12))
            if want_gr:
                blocks += list(range(12, 20))
            elif want_xr:
                blocks += list(range(12, 16))
            pkt = None
            for j in blocks:
                W, tw = wload(sc_in[j])
                if j < 8 or j >= 12:
                    pg, tpg = nextpg()
                    mmg(pg[:, 0:Tn], [(W[:, kk, :], xn[:, kk, 0:Tn]) for kk in range(8)], [tw] + TXN, [tpg])
                    if j < 4:
                        evac(qT[:, j, 0:Tn], pg[:, 0:Tn], [tpg], [TQ[j]])
                    elif j < 8:
                        if sample:
                            evac(kT[:, j - 4, 0:Tn], pg[:, 0:Tn], [tpg], [TKR[0]])
                        else:
                            c0 = slot6[0] * 128
                            evac(kT[:, j - 4, c0:c0 + Tn], pg[:, 0:Tn], [tpg], [TKR[s_] for s_ in slot6])
                    elif j < 16:
                        c = j - 12
                        if sample:
                            evac(XRs[:, c, :, 3:35], pg[:, 0:Tn].rearrange("p (s t) -> p s t", s=4), [tpg], [TXRS])
                        else:
                            cp("dve", XR2[g % 2][:, c, 3:3 + Tn], pg[:, 0:Tn], [tpg], [TXR2[g % 2][c]])
                    else:
                        c = j - 16
                        act_op(AF.Gelu_apprx_tanh, G[:, c, 0:Tn], pg[:, 0:Tn], [tpg], [TG[c]])
                if (4 <= j < 8 and tok_out) or (8 <= j < 12):
                    jj = (j - 4) % 4
                    if jj == 0:
                        pkt = [(PG[3], TPG[3])] if sample else [(PSS[b_], TPSS[b_]) for b_ in range(nsb)]
                    for b in range(nsb):
                        mmg(pkt[b][0][:, jj * 128:(jj + 1) * 128], [(xn[:, kk, b * 128:(b + 1) * 128], W[:, kk, :]) for kk in range(8)],
                            [tw] + TXN, [pkt[b][1]])
                    if sample and j >= 8:
                        for s_ in range(4):
                            mmg(PSS[s_ // 2][0:32, (s_ % 2) * 512 + jj * 128:(s_ % 2) * 512 + (jj + 1) * 128],
                                [(xn[:, kk, s_ * 32:(s_ + 1) * 32], W[:, kk, :]) for kk in range(8)], [tw] + TXN, [TPSS[s_ // 2]])
                    if jj == 3:
                        for b in range(nsb):
                            pgt, tpgt = pkt[b]
                            if j >= 8:
                                if not sample:
                                    cp("dve", Vr[:, slot6[b], :], pgt[:, 0:512], [tpgt], [TVR[slot6[b]]])
                                else:
                                    for s_ in range(4):
                                        cp("dve", Vn[0:32, s_, :], PSS[s_ // 2][0:32, (s_ % 2) * 512:(s_ % 2 + 1) * 512], [TPSS[s_ // 2]], [TVN])
                            if tok_out:
                                oi = nexto()
                                cp("dve" if (j >= 8 and not sample) else "act", ost[oi][:, 0:512], pgt[:, 0:512], [tpgt], [TOS[oi]])
                                if sample:
                                    dst = (ksm_d if j < 8 else vsm_d)[:, :]
                                else:
                                    r0 = (g - (NT - 2)) * T + b * 128
                                    dst = (kp_d if j < 8 else vp_d)[r0:r0 + 128, :]
                                k.dma("pool", f"os{oi}", lambda h, oi=oi, dst=dst: h.dma_start(out=dst, in_=ost[oi][:, 0:512]), [TOS[oi]], [])

        osc = [0]

        def nexto():
            i = osc[0] % 2
            osc[0] += 1
            return i

        def attention_prompt(g, hook=None):
            expb5 = expb[:].rearrange("p h (k q) -> p h k q", k=5)
            units = [(hp, e, ps_) for hp in range(4) for e in range(2) for ps_ in range(2)]

            def slot_of(kb):
                return (2 * g - 4 + kb) % 6

            def emit_s(ui):
                hp, e, ps_ = units[ui]
                rows = slice(e * 64, (e + 1) * 64)
                S = PSS[ui % 2]
                sl = [slot_of(3 * ps_ + j) for j in range(3)]
                mmlist([(S[:, j * 256:(j + 1) * 256], kT[rows, hp, sl[j] * 128:(sl[j] + 1) * 128], qT[rows, hp, 0:256], True, True) for j in range(3)],
                       [TKR[s_] for s_ in sl] + [TQ[hp]], [TPSS[ui % 2]])

            po_cur = [None]
            emit_s(0)
            for ui, (hp, e, ps_) in enumerate(units):
                h_ = 2 * hp + e
                rows = slice(e * 64, (e + 1) * 64)
                si = ui % 2
                S = PSS[si]
                pi = 2 * ps_ + (ui // 2) % 2
                if ui + 1 < len(units):
                    emit_s(ui + 1)
                if e == 0 and ps_ == 0:
                    po_cur[0] = nextpg()
                po, tpo = po_cur[0]
                act_op(AF.Exp, E[si][:], S[:, 0:768], [TPSS[si]], [TE[si]], scale=SCALE)
                Ev = E[si][:].rearrange("p (k q) -> p k q", k=3)
                Pv = Pm[pi][:].rearrange("p (k q) -> p k q", k=3)
                if ps_ == 0:
                    tt(Pv[:, 0:3, 0:128], Ev[:, 0:3, 0:128], expb5[:, h_, 0:3, :], ALU.mult, [TE[si]] + CONST, [TP[pi]], eng="pool")
                    tt(Pv[:, 1:3, 128:256], Ev[:, 1:3, 128:256], expb5[:, h_, 0:2, :], ALU.mult, [TE[si]] + CONST, [TP[pi]], eng="pool")
                else:
                    tt(Pv[:, 0:2, 0:128], Ev[:, 0:2, 0:128], expb5[:, h_, 3:5, :], ALU.mult, [TE[si]] + CONST, [TP[pi]], eng="pool")
                    tt(Pv[:, 0:3, 128:256], Ev[:, 0:3, 128:256], expb5[:, h_, 2:5, :], ALU.mult, [TE[si]] + CONST, [TP[pi]], eng="pool")
                sl = [slot_of(3 * ps_ + j) for j in range(3)]
                items = [(po[rows, 0:256], Vr[:, sl[j], h_ * 64:(h_ + 1) * 64], Pm[pi][:, j * 256:(j + 1) * 256], (ps_ == 0 and j == 0), False) for j in range(3)]
                items += [(po[rows, 256:512], vones[:, sl[j], :], Pm[pi][:, j * 256:(j + 1) * 256], False, (ps_ == 1 and j == 2)) for j in range(3)]
                mmlist(items, [TP[pi]] + [TVR[s_] for s_ in sl] + [TVO[s_] for s_ in sl], [tpo])
                if hook is not None:
                    hook(ui)
                if e == 1 and ps_ == 1:
                    ri = hp % 2
                    k.op("dve", lambda h, ri=ri, po=po: h.reciprocal(out=rd[ri][:], in_=po[:, 256:512]), [tpo], [TRD[ri]])
                    tt(attnT[:, hp, :], po[:, 0:256], rd[ri][:], ALU.mult, [tpo, TRD[ri]], [TAT[hp]])
                    if g in (NLEAN, NLEAN + 1):
                        for b in range(2):
                            slots = [(2 * g + b - 4 + kb) % 6 for kb in range(5)]
                            pu, tpu = nextpg()
                            items = []
                            for e2 in range(2):
                                h2 = 2 * hp + e2
                                rows2 = slice(e2 * 64, (e2 + 1) * 64)
                                items += [(pu[rows2, 0:128], Vr[:, slots[kb], h2 * 64:(h2 + 1) * 64], band_bf[:, kb * 128:(kb + 1) * 128], kb == 0, kb == 4) for kb in range(5)]
                            mmlist(items, [TVR[s_] for s_ in slots] + CONST, [tpu])
                            ts(attnT[:, hp, b * 128:(b + 1) * 128], attnT[:, hp, b * 128:(b + 1) * 128], nc0, None, ALU.mult, None, [TAT[hp]] + CONST, [TAT[hp]])
                            stt(attnT[:, hp, b * 128:(b + 1) * 128], pu[:, 0:128], c0s, attnT[:, hp, b * 128:(b + 1) * 128], ALU.mult, ALU.add,
                                [tpu, TAT[hp]] + CONST, [TAT[hp]])
            if g == NLEAN + 1:
                ts(Vr[:, 0:4, :], Vr[:, 0:4, :], nc0, None, ALU.mult, None, TVR[0:4] + CONST, TVR[0:4])
                ts(vones[:, 0:4, :], vones[:, 0:4, :], nc0, None, ALU.mult, None, TVO[0:4] + CONST, TVO[0:4])

        def attention_sample():
            si_c = [0]
            for s_ in range(4):
                xk = xtok[s_ % 2]
                kview = xk[:].rearrange("p a (b c) -> p (a b) c", b=2)
                k.dma("pool", f"x{s_ % 2}", lambda h, s_=s_, kview=kview: h.dma_start(out=kview, in_=ck_d[s_].rearrange("(b p) f -> p b f", p=128)), (), [TXT[s_ % 2]])
                k.dma("pool", "vc", lambda h, s_=s_: h.dma_start(out=Vr[:, 0:4, :], in_=cv_d[s_].rearrange("(b p) f -> p b f", p=128)), (), [TVC] + TVR[0:4])
                for hp in range(4):
                    pg, tpg = nextpg()
                    trlist([(pg[:, blk * 128:(blk + 1) * 128], kview[:, blk, hp * 128:(hp + 1) * 128]) for blk in range(4)], [TXT[s_ % 2]] + CONST, [tpg])
                    evac(kT[:, hp, 256:768], pg[:, :], [tpg], [TKC] + TKR[2:6])
                for hp in range(4):
                    po, tpo = nextpg()
                    for e in range(2):
                        h_ = 2 * hp + e
                        rows = slice(e * 64, (e + 1) * 64)
                        si = si_c[0] % 2
                        si_c[0] += 1
                        S = PSS[si]
                        qv = qT[rows, hp, s_ * 32:(s_ + 1) * 32]
                        items = [(S[:, kb * 32:(kb + 1) * 32], kT[rows, hp, 256 + kb * 128:256 + (kb + 1) * 128], qv, True, True) for kb in range(4)]
                        items.append((S[0:32, 128:160], kT[rows, hp, s_ * 32:(s_ + 1) * 32], qv, True, True))
                        mmlist(items, [TKC, TKR[0], TQ[hp]], [TPSS[si]])
                        act_op(AF.Exp, E[si][:, 0:128], S[:, 0:128], [TPSS[si]], [TE[si]], scale=SCALE)
                        act_op(AF.Exp, E[si][0:32, 128:160], S[0:32, 128:160], [TPSS[si]], [TE[si]], scale=SCALE)
                        tt(Pm[si][:, 0:128], E[si][:, 0:128], expbs[:, h_, 0:128], ALU.mult, [TE[si]] + CONST, [TP[si]])
                        tt(Pm[si][0:32, 128:160], E[si][0:32, 128:160], expbs[0:32, h_, 128:160], ALU.mult, [TE[si]] + CONST, [TP[si]])
                        items = [(po[rows, 0:32], Vr[:, kb, h_ * 64:(h_ + 1) * 64], Pm[si][:, kb * 32:(kb + 1) * 32], kb == 0, False) for kb in range(4)]
                        items.append((po[rows, 0:32], Vn[0:32, s_, h_ * 64:(h_ + 1) * 64], Pm[si][0:32, 128:160], False, True))
                        items += [(po[rows, 32:64], ones_bf[:, 0:64], Pm[si][:, kb * 32:(kb + 1) * 32], kb == 0, False) for kb in range(4)]
                        items.append((po[rows, 32:64], ones_bf[0:32, 0:64], Pm[si][0:32, 128:160], False, True))
                        mmlist(items, [TP[si], TVC, TVN] + CONST, [tpo])
                    ri = hp % 2
                    k.op("dve", lambda h, ri=ri, po=po: h.reciprocal(out=rd[ri][:, 0:32], in_=po[:, 32:64]), [tpo], [TRD[ri]])
                    tt(attnT[:, hp, s_ * 32:(s_ + 1) * 32], po[:, 0:32], rd[ri][:, 0:32], ALU.mult, [tpo, TRD[ri]], [TAT[hp]])

        def back_half(Tn, nsb, p_src, p_r0, y_dst, y_r0):
            rms(attnT, TAT, 4, g_attn, mixT, TMX[0:4], Tn, 512.0)
            act_op(AF.Square, sq[:, 0:4, 0:Tn], lruo[:, 0:4, 0:Tn], TLO, [TSQ])
            pg, tpg = nextpg()
            mmg(pg[:, 0:Tn], [(ones_bf[:], sq[:, kk, 0:Tn]) for kk in range(4)], [TSQ] + CONST, [tpg])
            act_op(AF.Ln, rt[:, 0:Tn], pg[:, 0:Tn], [tpg] + CONST, [TRT], bias=eps_t[:], scale=1.0 / 512.0)
            act_op(AF.Exp, rstd[:, 0:Tn], rt[:, 0:Tn], [TRT], [TRS], scale=-0.5)
            for kk in range(4):
                stt(mixT[:, 4 + kk, 0:Tn], lruo[:, kk, 0:Tn], g_lru[:, kk:kk + 1], rstd[:, 0:Tn], ALU.mult, ALU.mult,
                    [TLO[kk], TRS] + CONST, [TMX[4 + kk]])
            for o in range(8):
                W, tw = wload(sc_out[o])
                pg, tpg = nextpg()
                mmg(pg[:, 0:Tn], [(W[:, kk, :], mixT[:, kk, 0:Tn]) for kk in range(8)], [tw] + TMX, [tpg])
                tt(hT[:, o, 0:Tn], hT[:, o, 0:Tn], pg[:, 0:Tn], ALU.add, [THT[o], tpg], [THT[o]])
            rms(hT, THT, 8, g_ffn, xn, TXN, Tn, 1024.0)
            def ffn_gu(f):
                Wg, twg = wload(sc_g[f])
                Wu, twu = wload(sc_u[f])
                Wd, twd = wload(sc_d[f], view1024=True)
                pg, tpg = nextpg()
                items = [(pg[:, 0:Tn], Wg[:, kk, :], xn[:, kk, 0:Tn], kk == 0, kk == 7) for kk in range(8)]
                items += [(pg[:, 256:256 + Tn], Wu[:, kk, :], xn[:, kk, 0:Tn], kk == 0, kk == 7) for kk in range(8)]
                mmlist(items, [twg, twu] + TXN, [tpg])
                return pg, tpg, Wd, twd

            cur = ffn_gu(0)
            for f in range(NF):
                nxt = ffn_gu(f + 1) if f + 1 < NF else None
                pg, tpg, Wd, twd = cur
                Wdv = Wd[:].rearrange("p a b -> p (a b)")
                fi = f % 2
                act_op(AF.Silu, sg[fi][:, 0:Tn], pg[:, 0:Tn], [tpg], [TSG[fi]])
                tt(act[fi][:, 0:Tn], sg[fi][:, 0:Tn], pg[:, 256:256 + Tn], ALU.mult, [TSG[fi], tpg], [TACT[fi]])
                items = [(PSS[o // 4][:, (o % 4) * 256:(o % 4) * 256 + Tn], Wdv[:, o * 128:(o + 1) * 128], act[fi][:, 0:Tn], (f == 0 and o % 2 == 0), (f == NF - 1 and o % 2 == 1)) for o in range(8)]
                mmlist(items, [twd, TACT[fi]], TPSS)
                cur = nxt
            for o in range(8):
                tt(hT[:, o, 0:Tn], hT[:, o, 0:Tn], PSS[o // 4][:, (o % 4) * 256:(o % 4) * 256 + Tn], ALU.add, [THT[o]] + TPSS, [THT[o]])
            rms(hT, THT, 8, g_ple, xn, TXN, Tn, 1024.0)
            k.dma("pool", "ptok", lambda h: h.dma_start(out=ptok[:, 0:nsb, :], in_=p_src[p_r0:p_r0 + Tn, :].rearrange("(b p) f -> p b f", p=128)), (), [TPT])
            pg, tpg = nextpg()
            trlist([(pg[:, (b * 2 + k2) * 128:(b * 2 + k2 + 1) * 128], ptok[:, b, k2 * 128:(k2 + 1) * 128]) for b in range(nsb) for k2 in range(2)],
                   [TPT] + CONST, [tpg])
            for b in range(nsb):
                evac(pT[:, 0:2, b * 128:(b + 1) * 128], pg[:, b * 256:(b + 1) * 256].rearrange("p (a b) -> p a b", a=2), [tpg], [TPTT])
            for o in range(8):
                W, tw = wload(sc_pg[o])
                pg, tpg = nextpg()
                items = [(pg[:, 0:Tn], W[:, kk, :], xn[:, kk, 0:Tn], kk == 0, kk == 7) for kk in range(8)]
                items += [(pg[:, 256:256 + Tn], w_pp[:, k2, o * 128:(o + 1) * 128], pT[:, k2, 0:Tn], k2 == 0, k2 == 1) for k2 in range(2)]
                mmlist(items, [tw, TPTT] + TXN + CONST, [tpg])
                fi = o % 2
                act_op(AF.Sigmoid, sg[fi][:, 0:Tn], pg[:, 0:Tn], [tpg], [TSG[fi]])
                tt(sg[fi][:, 0:Tn], pg[:, 256:256 + Tn], sg[fi][:, 0:Tn], ALU.mult, [TSG[fi], tpg], [TSG[fi]])
                tt(hT[:, o, 0:Tn], hT[:, o, 0:Tn], sg[fi][:, 0:Tn], ALU.add, [THT[o], TSG[fi]], [THT[o]])
            rms(hT, THT, 8, g_fin, hT, THT, Tn, 1024.0)
            for b in range(nsb):
                oi = nexto()
                for half in range(2):
                    pg, tpg = nextpg()
                    trlist([(pg[:, i * 128:(i + 1) * 128], hT[:, half * 4 + i, b * 128:(b + 1) * 128]) for i in range(4)], THT + CONST, [tpg])
                    evac(ost[oi][:, half * 512:(half + 1) * 512], pg[:, :], [tpg], [TOS[oi]])
                r0 = y_r0 + b * 128
                k.dma("pool", f"os{oi}", lambda h, oi=oi, r0=r0: h.dma_start(out=y_dst[r0:r0 + 128, :], in_=ost[oi][:]), [TOS[oi]], [])

        def sample_tile(Ts):
            k.dma("pool", "x0", lambda h: h.dma_start(out=xtok[0][:, 0, :], in_=xsm_d), (), [TXT[0]])
            k.dma("pool", "sst", lambda h: h.dma_start(out=XRs[:, :, :, 0:3], in_=sconv_d), (), [TXRS])
            k.dma("pool", "sst", lambda h: h.dma_start(out=hS[:], in_=sh_d), (), [THS_S])
            TXRS.lastw = (("D", "sst"), 32)
            THS_S.lastw = (("D", "sst"), 32)
            transpose_in(xtok[0], TXT[0], 1, Ts)
            rms(hT, THT, 8, g_mix, xn, TXN, Ts, 1024.0)
            proj_main(Ts, 0, 1, True, True, True, True)
            lru_all(Ts, 0, True, True)
            cp("act", hso[:], hs[:, :, 0:Ts].rearrange("p c (s t) -> p c s t", t=32)[:, :, :, 31], THS, [THSO])
            k.dma("pool", "st3", lambda h: h.dma_start(out=hsm_d, in_=hso[:]), [THSO], [])
            k.dma("pool", "st4", lambda h: h.dma_start(out=convs_d, in_=XRs[:, :, :, 32:35]), [TXRS], [])
            attention_sample()
            back_half(Ts, 1, psm_d, 0, ys_d, 0)

        tile_list = list(range(NT)) if dbg_tiles is None else list(dbg_tiles)
        if dbg_tiles is not None and dbg_conv:
            emit_conv(1000)
        NL2 = NLEAN - 2
        leans = [g for g in tile_list if g < NL2]
        if tile_list[0] >= NL2:
            load_x(tile_list[0])
        else:
            for g_ in leans[:2]:
                lean_cast(g_)
            lean_tr(leans[0])
        fronted = set()
        for ti, g in enumerate(tile_list):
            nxt = tile_list[ti + 1] if ti + 1 < len(tile_list) else None
            if nxt is not None and nxt >= NL2:
                load_x(nxt)
            if g < NLEAN and dbg_conv:
                emit_conv(3)
            if g < NL2:
                li = leans.index(g)
                if li + 2 < len(leans):
                    lean_cast(leans[li + 2])
                if li + 1 < len(leans):
                    lean_tr(leans[li + 1])
                if g not in fronted:
                    lean_front(g)
                if nxt is not None and nxt < NL2:
                    fronted.add(nxt)
                    lru_p(0, T, g, False)
                    lean_f0(nxt)
                    lru_p(1, T, g, False)
                    lru_p(2, T, g, False)
                    lru_p(3, T, g, False)
                    lean_f2(nxt)
                    lru_q(0, T, g, False, False)
                    lru_q(1, T, g, False, False)
                    lean_f3(nxt)
                    lru_q(2, T, g, False, False)
                    lru_q(3, T, g, False, False)
                else:
                    lru_all(T, g, False, False)
                continue
            transpose_in(xtok[g % 2], TXT[g % 2], 2, T)
            rms(hT, THT, 8, g_mix, xn, TXN, T, 1024.0)
            for b in range(2):
                s6 = (2 * g + b) % 6
                ts(vones[:, s6, :], ones_bf[:, 0:64], flags[:, g:g + 1], None, ALU.mult, None, CONST, [TVO[s6]])
            if g < NLEAN:
                proj_main(T, g, 2, False, False, False, False, want_xr=True)
                lru_all(T, g, False, False)
            else:
                proj_main(T, g, 2, True, True, g >= NT - 2, False)
                for c in range(4):
                    lru_p(c, T, g, False)
                attention_prompt(g, hook=lambda ui: lru_q(ui // 4, T, g, False, True) if ui % 4 == 3 else None)
                if _DBG.get("dump") == g:
                    k.dma("pool", "dbgo", lambda h: h.dma_start(out=dbg_d, in_=attnT[:]), TAT, [])
                back_half(T, 2, pm_d, (g - NLEAN) * T, y_d, (g - NLEAN) * T)
        if dbg_conv:
            emit_conv(1000)
        k.dma("pool", "st1", lambda h: h.dma_start(out=convp_d, in_=XR2[NT % 2][:, :, 0:3]), TXR2[NT % 2], [])
        k.dma("pool", "st2", lambda h: h.dma_start(out=hp_d, in_=hst[:]), THST, [])
        Ts = 128
        if dbg_sample:
            sample_tile(Ts)

        def _unused():
            pass

        print("total ops", k.total, {e: len(p) for e, p in k.prog.items()})
        out_keys = [kk for kk in ["os0", "os1", "st1", "st2", "st3", "st4"] if kk in k.dsem] + (["dbgo"] if "dbgo" in k.dsem else [])
        with nc.Block() as block:
            @block.tensor
            def _(h):
                k.replay("pe", h)

            @block.scalar
            def _(h):
                k.replay("act", h)

            @block.vector
            def _(h):
                k.replay("dve", h)

            @block.gpsimd
            def _(h):
                with nc.allow_non_contiguous_dma(reason="small strided state/param transfers"):
                    k.replay("pool", h, out_keys)

            @block.sync
            def _(h):
                k.replay("sp", h)
    return nc


_NC_CACHE = {}


def _fm(v, nk):
    return np.ascontiguousarray(np.asarray(v, np.float32).reshape(nk, 128).T)


def kernel(x_prompt, x_sample, p_prompt, p_sample, cache_k, cache_v, state_conv, state_h,
           g_mix, w_in, conv_w, conv_b, w_rgate, b_rgate, w_igate, b_igate, lru_lambda,
           rel_bias_table, g_attn_out, g_lru_out, w_out, g_ffn, w_ffn_gate, w_ffn_up,
           w_ffn_down, g_ple, w_ple_gate, w_ple_proj, g_final):
    f32 = np.float32
    x_prompt = np.asarray(x_prompt, f32)
    x_sample = np.asarray(x_sample, f32)
    p_prompt = np.asarray(p_prompt, f32)
    p_sample = np.asarray(p_sample, f32)
    cache_k = np.asarray(cache_k, f32)
    cache_v = np.asarray(cache_v, f32)
    state_conv = np.asarray(state_conv, f32)
    state_h = np.asarray(state_h, f32)
    table = np.asarray(rel_bias_table, f32)[0]

    p_idx = np.arange(128)
    biasp = np.full((128, 8, 5, 128), NEG, f32)
    for kb in range(5):
        rel = (kb * 128 + p_idx[:, None] - 512) - p_idx[None, :]
        idx = np.clip(rel, -128, 63) + 128
        kc = (kb * 128 + p_idx[:, None]) // 64 - 8
        qc = p_idx[None, :] // 64
        valid = (kc <= qc) & (kc >= qc - 8)
        vals = table[:, idx]
        blk = np.where(valid[None], vals, f32(NEG))
        biasp[:, :, kb, :] = blk.transpose(1, 0, 2)
    biass = np.full((128, 8, 5, 32), NEG, f32)
    q32 = np.arange(32)
    for kb in range(4):
        rel = (kb * 128 + p_idx[:, None] - 512) - q32[None, :]
        idx = np.clip(rel, -128, 63) + 128
        biass[:, :, kb, :] = table[:, idx].transpose(1, 0, 2)
    rel = q32[:, None] - q32[None, :]
    idx = np.clip(rel, -128, 63) + 128
    biass[0:32, :, 4, :] = table[:, idx].transpose(1, 0, 2)
    cwl = np.asarray(conv_w, f32)[0]
    cwfm = np.ascontiguousarray(cwl.reshape(4, 4, 128).transpose(2, 1, 0)).reshape(128, 16)
    vecs = np.concatenate([
        _fm(g_mix[0], 8), _fm(g_ffn[0], 8), _fm(g_ple[0], 8), _fm(g_final, 8),
        _fm(g_attn_out[0], 4), _fm(g_lru_out[0], 4), _fm(conv_b[0], 4), _fm(b_rgate[0], 4),
        _fm(b_igate[0], 4), _fm(lru_lambda[0], 4), cwfm, np.zeros((128, 2), f32)], axis=1).astype(f32)
    ident = np.eye(128, dtype=f32)
    shared = {
        "band": np.ascontiguousarray((biasp[:, 0] > -1e29).astype(f32).reshape(128, 640)),
        "biasp": biasp.reshape(128, 8 * 640), "biass": biass.reshape(128, 8 * 160), "ident": ident, "vecs": vecs,
        "w_in": np.asarray(w_in, f32)[0], "w_out": np.asarray(w_out, f32)[0],
        "w_ffn_gate": np.asarray(w_ffn_gate, f32)[0], "w_ffn_up": np.asarray(w_ffn_up, f32)[0],
        "w_ffn_down": np.asarray(w_ffn_down, f32)[0], "w_ple_gate": np.asarray(w_ple_gate, f32)[0],
        "w_ple_proj": np.asarray(w_ple_proj, f32)[0], "w_rgate": np.asarray(w_rgate, f32)[0],
        "w_igate": np.asarray(w_igate, f32)[0],
    }
    in_maps = []
    for c in range(8):
        s, j = c // 4, c % 4
        npad = (3 - j) * SEG
        xs = np.zeros((NT * T, D), f32)
        xs[npad:] = x_prompt[s, 0:(j + 1) * SEG]
        fl = np.zeros((NT,), f32)
        fl[npad // T:] = 1.0
        sq_ = slice(4 * c, 4 * c + 4)
        m = dict(shared)
        vc_ = vecs.copy()
        vc_[:, 72] = (1.0 / 576.0) if j == 0 else 0.0
        vc_[:, 73] = 0.0 if j == 0 else 1.0
        m["vecs"] = vc_
        m.update({
            "xs": xs, "pm": np.ascontiguousarray(p_prompt[0, s, j * SEG:(j + 1) * SEG]),
            "xsm": np.ascontiguousarray(x_sample[sq_].reshape(128, D)),
            "psm": np.ascontiguousarray(p_sample[0, sq_].reshape(128, 256)),
            "ck": np.ascontiguousarray(cache_k[0, sq_].reshape(4, 512, 512)),
            "cv": np.ascontiguousarray(cache_v[0, sq_].reshape(4, 512, 512)),
            "sconv": np.ascontiguousarray(state_conv[0, sq_].reshape(4, 3, 4, 128).transpose(3, 2, 0, 1)),
            "sh": np.ascontiguousarray(state_h[0, sq_].reshape(4, 4, 128).transpose(2, 1, 0)),
            "flags": np.ascontiguousarray(np.broadcast_to(fl[None, :], (128, NT))),
        })
        in_maps.append(m)
    if _DBG.get("prep_only"):
        return in_maps
    if "nc" not in _NC_CACHE:
        _NC_CACHE["nc"] = build_nc()
    res = run_bass_kernel_spmd(_NC_CACHE["nc"], in_maps, core_ids=list(range(8)))
    R = res.results
    y_prompt = np.stack([np.concatenate([R[s * 4 + j]["y"] for j in range(4)], axis=0) for s in range(2)]).astype(f32)
    y_sample = np.concatenate([R[c]["ys"].reshape(4, 32, D) for c in range(8)], axis=0).astype(f32)
    new_k_prompt = np.stack([R[s * 4 + 3]["kp"].reshape(512, 8, 64) for s in range(2)])[None].astype(f32)
    new_v_prompt = np.stack([R[s * 4 + 3]["vp"].reshape(512, 8, 64) for s in range(2)])[None].astype(f32)
    new_conv_prompt = np.stack([R[s * 4 + 3]["convp"].transpose(2, 1, 0).reshape(3, 512) for s in range(2)])[None].astype(f32)
    new_h_prompt = np.stack([R[s * 4 + 3]["hp"].T.reshape(512) for s in range(2)])[None].astype(f32)
    new_k_sample = np.concatenate([R[c]["ksm"].reshape(4, 32, 8, 64) for c in range(8)], axis=0)[None].astype(f32)
    new_v_sample = np.concatenate([R[c]["vsm"].reshape(4, 32, 8, 64) for c in range(8)], axis=0)[None].astype(f32)
    new_conv_sample = np.concatenate([R[c]["convs"].transpose(2, 3, 1, 0).reshape(4, 3, 512) for c in range(8)], axis=0)[None].astype(f32)
    new_h_sample = np.concatenate([R[c]["hsm"].transpose(2, 1, 0).reshape(4, 512) for c in range(8)], axis=0)[None].astype(f32)
    return (y_prompt, y_sample, new_k_prompt, new_v_prompt, new_conv_prompt, new_h_prompt,
            new_k_sample, new_v_sample, new_conv_sample, new_h_sample)
```

```python
import contextlib
import numpy as np
import concourse.bass as bass
import concourse.mybir as mybir
from concourse.bass_utils import run_bass_kernel_spmd

F32 = mybir.dt.float32
BF16 = mybir.dt.bfloat16
AF = mybir.ActivationFunctionType
ALU = mybir.AluOpType

D = 1024
T = 256
NLEAN = 48
NMAIN = 16
NT = NLEAN + NMAIN
SEG = 4096
DFF = 2816
NF = DFF // 128
SCALE = 0.125
EPS = 1e-6
NEG = -1e30
NWS = 10
FILL = {}


_DBG = {}


class Tok:
    __slots__ = ("lastw", "readers", "name")

    def __init__(self, name=""):
        self.lastw = None
        self.readers = []
        self.name = name


def toks(n, name=""):
    return [Tok(name + str(i)) for i in range(n)]


class KB:
    def __init__(self, nc, st):
        self.nc = nc
        self.st = st
        self.eh = {"pe": nc.tensor, "act": nc.scalar, "dve": nc.vector, "pool": nc.gpsimd, "sp": nc.sync}
        self.prog = {e: [] for e in self.eh}
        self.sem = {e: st.enter_context(nc.semaphore("sem_" + e)) for e in self.eh}
        self.cnt = {e: 0 for e in self.eh}
        self.seen = {e: {} for e in self.eh}
        self.dsem = {}
        self.total = 0
        self.limit = _DBG.get("limit")

    def semh(self, k):
        return self.sem[k] if isinstance(k, str) else self.dsem[k[1]][0]

    def _deps(self, eng, reads, writes):
        w = {}
        for b in reads:
            if b.lastw:
                k, v = b.lastw
                w[k] = max(w.get(k, 0), v)
        for b in writes:
            if b.lastw:
                k, v = b.lastw
                w[k] = max(w.get(k, 0), v)
            for k, v in b.readers:
                w[k] = max(w.get(k, 0), v)
        out = []
        for k, v in w.items():
            if k == eng and eng == "pe":
                continue
            if self.seen[eng].get(k, 0) < v:
                self.seen[eng][k] = v
                out.append((k, v))
        return out

    def op(self, eng, fn, reads=(), writes=()):
        self.total += 1
        if self.limit is not None and self.total > self.limit:
            return
        wl = self._deps(eng, reads, writes)
        self.cnt[eng] += 1
        v = self.cnt[eng]
        self.prog[eng].append((wl, fn, (eng, 1)))
        for b in reads:
            b.readers.append((eng, v))
        for b in writes:
            b.lastw = (eng, v)
            b.readers = []

    def dma(self, q, key, fn, reads=(), writes=()):
        self.total += 1
        if self.limit is not None and self.total > self.limit:
            return
        wl = self._deps(q, reads, writes)
        if key not in self.dsem:
            self.dsem[key] = [self.st.enter_context(self.nc.semaphore("d_" + key)), 0]
        d = self.dsem[key]
        d[1] += 16
        v = d[1]
        sk = ("D", key)
        self.prog[q].append((wl, fn, (sk, 16)))
        for b in reads:
            b.readers.append((sk, v))
        for b in writes:
            b.lastw = (sk, v)
            b.readers = []

    def replay(self, e, h, final_keys=()):
        for wl, fn, (sk, n) in self.prog[e]:
            for k, v in wl:
                h.wait_ge(self.semh(k), v)
            ins = fn(h)
            ins.then_inc(self.semh(sk), n)
        for key in final_keys:
            h.wait_ge(self.dsem[key][0], self.dsem[key][1])


def build_nc(dbg_tiles=None, dbg_sample=True, dbg_conv=True):
    nc = bass.Bass("TRN2", target_bir_lowering=False)

    def din(name, shape, dt=F32):
        return nc.dram_tensor(name, list(shape), dt, kind="ExternalInput").ap()

    def dout(name, shape):
        return nc.dram_tensor(name, list(shape), F32, kind="ExternalOutput").ap()

    def dint(name, shape, dt=BF16):
        return nc.dram_tensor(name, list(shape), dt, kind="Internal").ap()

    xs_d = din("xs", [NT * T, D])
    pm_d = din("pm", [SEG, 256])
    xsm_d = din("xsm", [128, D])
    psm_d = din("psm", [128, 256])
    ck_d = din("ck", [4, 512, 512])
    cv_d = din("cv", [4, 512, 512])
    sconv_d = din("sconv", [128, 4, 4, 3])
    sh_d = din("sh", [128, 4, 4])
    flags_d = din("flags", [128, NT])
    biasp_d = din("biasp", [128, 8 * 640])
    biass_d = din("biass", [128, 8 * 160])
    ident_d = din("ident", [128, 128])
    vec_d = din("vecs", [128, 74])
    band_d = din("band", [128, 640])
    w_in_d = din("w_in", [D, 2560])
    w_out_d = din("w_out", [D, D])
    w_g_d = din("w_ffn_gate", [D, DFF])
    w_u_d = din("w_ffn_up", [D, DFF])
    w_d_d = din("w_ffn_down", [DFF, D])
    w_pg_d = din("w_ple_gate", [D, D])
    w_pp_d = din("w_ple_proj", [256, D])
    w_r_d = din("w_rgate", [8, 64, 64])
    w_i_d = din("w_igate", [8, 64, 64])
    y_d = dout("y", [SEG, D])
    ys_d = dout("ys", [128, D])
    kp_d = dout("kp", [512, 512])
    vp_d = dout("vp", [512, 512])
    convp_d = dout("convp", [128, 4, 3])
    hp_d = dout("hp", [128, 4])
    ksm_d = dout("ksm", [128, 512])
    vsm_d = dout("vsm", [128, 512])
    convs_d = dout("convs", [128, 4, 4, 3])
    hsm_d = dout("hsm", [128, 4, 4])
    dbg_d = dout("dbgo", [128, 4, T]) if _DBG.get("dump") is not None else None
    sc_in = dint("sc_in", [20, 128, 8, 128])
    sc_out = dint("sc_out", [8, 128, 8, 128])
    sc_g = dint("sc_g", [NF, 128, 8, 128])
    sc_u = dint("sc_u", [NF, 128, 8, 128])
    sc_d = dint("sc_d", [NF, 128, 1024])
    sc_pg = dint("sc_pg", [8, 128, 8, 128])
    xbf = dint("xbf", [(NLEAN - 2) * T, D])

    with contextlib.ExitStack() as st:
        def sb(name, shape, dt=F32):
            return st.enter_context(nc.sbuf_tensor("s_" + name, list(shape), dt))

        def ps(name, shape, dt=F32):
            return st.enter_context(nc.psum_tensor("p_" + name, list(shape), dt))

        k = KB(nc, st)
        ident = sb("ident", [128, 128])
        ones_bf = sb("ones_bf", [128, 128], BF16)
        eps_t = sb("eps_t", [128, 1])
        one_t = sb("one_t", [128, 1])
        vecs = sb("vecs", [128, 74])
        band_bf = sb("band_bf", [128, 640], BF16)
        flags = sb("flags", [128, NT])
        nsp = sb("nsp", [128, 4])
        nsp2 = sb("nsp2", [128, 4])
        nb = sb("nb", [128, 8])
        lamt = sb("lamt", [128, 4])
        expb = sb("expb", [128, 8, 640])
        expbs = sb("expbs", [128, 8, 160])
        w_xr = sb("w_xr", [128, 8, 512], BF16)
        w_pp = sb("w_pp", [128, 2, 1024], BF16)
        wr_bd = sb("wr_bd", [128, 4, 128], BF16)
        wi_bd = sb("wi_bd", [128, 4, 128], BF16)
        WS = [sb(f"ws{i}", [128, 8, 128], BF16) for i in range(NWS)]
        xtok = [sb(f"xtok{i}", [128, 2, 1024]) for i in range(2)]
        ptok = sb("ptok", [128, 2, 256])
        pT = sb("pT", [128, 2, T], BF16)
        hT = sb("hT", [128, 8, T])
        xn = sb("xn", [128, 8, T], BF16)
        sq = sb("sq", [128, 8, T], BF16)
        rt = sb("rt", [128, T])
        rstd = sb("rstd", [128, T])
        qT = sb("qT", [128, 4, T], BF16)
        kT = sb("kT", [128, 4, 768], BF16)
        Vr = sb("Vr", [128, 6, 512], BF16)
        vones = sb("vones", [128, 6, 64], BF16)
        XR2 = [sb(f"XR{i}", [128, 4, T + 3]) for i in range(2)]
        XRs = sb("XRs", [128, 4, 4, 35])
        G = sb("G", [128, 4, T])
        hs = sb("hs", [128, 4, T])
        lruo = sb("lruo", [128, 4, T])
        hst = sb("hst", [128, 4])
        hS = sb("hS", [128, 4, 4])
        hso = sb("hso", [128, 4, 4])
        xc = [sb(f"xc{i}", [128, T]) for i in range(2)]
        xc2 = [sb(f"xc2{i}", [128, T]) for i in range(2)]
        xcb = [sb(f"xcb{i}", [128, T], BF16) for i in range(2)]
        rg = [sb(f"rg{i}", [128, T]) for i in range(4)]
        ig = [sb(f"ig{i}", [128, T]) for i in range(4)]
        av = [sb(f"av{i}", [128, T]) for i in range(2)]
        a2 = [sb(f"a2{i}", [128, T]) for i in range(2)]
        E = [sb(f"E{i}", [128, 768]) for i in range(2)]
        Pm = [sb(f"Pm{i}", [128, 768], BF16) for i in range(4)]
        rd = [sb(f"rd{i}", [128, 256]) for i in range(2)]
        attnT = sb("attnT", [128, 4, T])
        mixT = sb("mixT", [128, 8, T], BF16)
        sg = [sb(f"sg{i}", [128, T]) for i in range(2)]
        act = [sb(f"act{i}", [128, T], BF16) for i in range(2)]
        ost = [sb(f"ost{i}", [128, 1024]) for i in range(2)]
        Vn = sb("Vn", [32, 4, 512], BF16)
        PG = [ps(f"pg{i}", [128, 512]) for i in range(4)]
        PSS = [ps(f"pss{i}", [128, 1024]) for i in range(2)]
        TPG = toks(4, "pg")
        TPSS = toks(2, "pss")
        pgc = [0]

        def nextpg():
            i = pgc[0] % 3
            pgc[0] += 1
            return PG[i], TPG[i]

        Tconst = Tok("const")
        Tw_xr = Tok("w_xr")
        THT = toks(8, "hT")
        TXN = toks(8, "xn")
        TSQ, TRT, TRS = Tok("sq"), Tok("rt"), Tok("rstd")
        TQ = toks(4, "q")
        TKR = toks(6, "kr")
        TVR = toks(6, "vr")
        TVO = toks(6, "vo")
        TXR2 = [toks(4, "xra"), toks(4, "xrb")]
        TG = toks(4, "g")
        THS = toks(4, "hs")
        TLO = toks(4, "lo")
        THST = toks(4, "hst")
        TXC, TXCB, TAV, TA2 = (toks(2, n) for n in ("xc", "xcb", "av", "a2"))
        TRG, TIG = toks(4, "rg"), toks(4, "ig")
        TXC2 = toks(2, "xc2")
        TE, TP, TRD = toks(2, "E"), toks(4, "P"), toks(2, "rd")
        TAT = toks(4, "at")
        TMX = toks(8, "mx")
        TSG, TACT = toks(2, "sg"), toks(2, "act")
        TOS = toks(2, "os")
        TXT = toks(2, "xt")
        TPT, TPTT = Tok("ptok"), Tok("pT")
        TWS = toks(NWS, "ws")
        TSCR = Tok("scr")
        TVC, TKC, TVN = Tok("vc"), Tok("kc"), Tok("vn")
        TXRS, THS_S, THSO = Tok("xrs"), Tok("hS"), Tok("hso")
        setup_toks = []

        def act_op(func, out, in_, reads, writes, bias=None, scale=None):
            def fn(h):
                kw = {}
                if bias is not None:
                    kw["bias"] = bias
                if scale is not None:
                    kw["scale"] = scale
                return h.activation(out=out, in_=in_, func=func, **kw)
            k.op("act", fn, reads, writes)

        def tt(out, in0, in1, op, reads, writes, eng="dve"):
            k.op(eng, lambda h: h.tensor_tensor(out=out, in0=in0, in1=in1, op=op), reads, writes)

        def ts(out, in0, s1, s2, op0, op1, reads, writes):
            if op1 is None:
                k.op("dve", lambda h: h.tensor_scalar(out=out, in0=in0, scalar1=s1, scalar2=None, op0=op0), reads, writes)
            else:
                k.op("dve", lambda h: h.tensor_scalar(out=out, in0=in0, scalar1=s1, scalar2=s2, op0=op0, op1=op1), reads, writes)

        def stt(out, in0, scalar, in1, op0, op1, reads, writes):
            k.op("dve", lambda h: h.scalar_tensor_tensor(out=out, in0=in0, scalar=scalar, in1=in1, op0=op0, op1=op1), reads, writes)

        def cp(eng, out, in_, reads, writes):
            if eng == "act":
                k.op("act", lambda h: h.copy(out=out, in_=in_), reads, writes)
            else:
                k.op("dve", lambda h: h.tensor_copy(out=out, in_=in_), reads, writes)

        evc = [0]

        def evac(out, in_, reads, writes):
            evc[0] += 1
            cp("act" if evc[0] % 2 else "dve", out, in_, reads, writes)

        def mmg(out, pairs, reads, writes, start=True, stop=True):
            def fn(h):
                n = len(pairs)
                ins = None
                for i, (l, r) in enumerate(pairs):
                    ins = h.matmul(out, l, r, start=(start and i == 0), stop=(stop and i == n - 1))
                return ins
            k.op("pe", fn, reads, writes)

        def mmlist(items, reads, writes):
            def fn(h):
                ins = None
                for (o, l, r, s0, s1) in items:
                    ins = h.matmul(o, l, r, start=s0, stop=s1)
                return ins
            k.op("pe", fn, reads, writes)

        def filler(n):
            if n <= 0:
                return
            def fn(h):
                ins = None
                for _ in range(n):
                    ins = h.matmul(PG[3][:, 0:256], ones_bf[:], w_xr[:, 0, 0:256], start=True, stop=True)
                return ins
            k.op("pe", fn, (), ())

        def trlist(items, reads, writes):
            def fn(h):
                ins = None
                for (o, i_) in items:
                    ins = h.transpose(o, i_, ident[:])
                return ins
            k.op("pe", fn, reads, writes)

        def sdma(out, in_, writes):
            k.dma("pool", "setup", lambda h: h.dma_start(out=out, in_=in_), (), writes)
            setup_toks.extend(writes)

        wsc = [0]

        def wload(src_ap, view1024=False):
            i = wsc[0] % NWS
            wsc[0] += 1
            dst = WS[i][:].rearrange("p a b -> p (a b)") if view1024 else WS[i][:]
            k.dma("sp", f"ws{i}", lambda h: h.dma_start(out=dst, in_=src_ap), [TSCR], [TWS[i]])
            return WS[i], TWS[i]

        sdma(ident[:], ident_d, [Tconst])
        sdma(vecs[:], vec_d, [Tconst])
        sdma(flags[:], flags_d, [Tconst])
        sdma(E[0][:, 0:640], band_d, [Tconst])
        sdma(expb[:].rearrange("p a b -> p (a b)"), biasp_d, [Tconst])
        sdma(expbs[:].rearrange("p a b -> p (a b)"), biass_d, [Tconst])
        sdma(w_pp[:], w_pp_d.rearrange("(k p) c -> p k c", p=128), [Tconst])
        Tbd = Tok("bd")
        k.op("dve", lambda h: h.memset(wr_bd[:], 0.0), (), [Tbd])
        k.op("dve", lambda h: h.memset(wi_bd[:], 0.0), (), [Tbd])
        for c in range(4):
            for e in range(2):
                sdma(wr_bd[e * 64:(e + 1) * 64, c, e * 64:(e + 1) * 64], w_r_d[2 * c + e], [Tbd])
                sdma(wi_bd[e * 64:(e + 1) * 64, c, e * 64:(e + 1) * 64], w_i_d[2 * c + e], [Tbd])
        fin = ("D", "setup"), (k.dsem["setup"][1] if "setup" in k.dsem else 0)
        for t_ in setup_toks:
            t_.lastw = fin
        Tc2 = Tok("c2")
        k.op("dve", lambda h: h.memset(ones_bf[:], 1.0), (), [Tc2])
        k.op("dve", lambda h: h.tensor_copy(out=band_bf[:], in_=E[0][:, 0:640]), [Tconst], [Tc2])
        k.op("dve", lambda h: h.memset(eps_t[:], EPS), (), [Tc2])
        k.op("dve", lambda h: h.memset(one_t[:], 1.0), (), [Tc2])
        for i_ in range(4):
            k.op("dve", lambda h, i_=i_: h.memset(Pm[i_][:], 0.0), (), [TP[i_]])
        k.op("dve", lambda h: h.memset(hst[:], 0.0), (), THST)
        k.op("dve", lambda h: h.memset(XR2[0][:], 0.0), (), TXR2[0])
        k.op("dve", lambda h: h.memset(XR2[1][:], 0.0), (), TXR2[1])
        k.op("dve", lambda h: h.memset(vones[:], 0.0), (), TVO)
        k.op("dve", lambda h: h.memset(kT[:], 0.0), (), TKR)
        k.op("dve", lambda h: h.memset(Vr[:], 0.0), (), TVR)
        act_op(AF.Exp, expb[:], expb[:], [Tconst], [Tconst])
        act_op(AF.Exp, expbs[:], expbs[:], [Tconst], [Tconst])
        act_op(AF.Exp, lamt[:], vecs[:, 52:56], [Tconst], [Tc2], scale=-1.0)
        act_op(AF.Ln, lamt[:], lamt[:], [Tc2], [Tc2], bias=one_t[:])
        ts(nsp[:], lamt[:], -8.0, None, ALU.mult, None, [Tc2], [Tc2])
        ts(nsp2[:], lamt[:], -16.0, None, ALU.mult, None, [Tc2], [Tc2])
        ts(nb[:], vecs[:, 44:52], -1.0, None, ALU.mult, None, [Tconst], [Tc2])
        CONST = [Tconst, Tc2, Tbd]
        g_mix, g_ffn, g_ple, g_fin = vecs[:, 0:8], vecs[:, 8:16], vecs[:, 16:24], vecs[:, 24:32]
        g_attn, g_lru = vecs[:, 32:36], vecs[:, 36:40]
        cb, br, bi = vecs[:, 40:44], vecs[:, 44:48], vecs[:, 48:52]
        cw = vecs[:, 56:72]
        c0s, nc0 = vecs[:, 72:73], vecs[:, 73:74]

        for half in range(2):
            stg = xtok[0][:].rearrange("p a (b c) -> p (a b) c", b=2)
            k.dma("pool", "x0", lambda h, half=half, stg=stg: h.dma_start(
                out=stg, in_=w_in_d[half * 512:(half + 1) * 512, 1536:2048].rearrange("(k p) c -> p k c", p=128)), (), [TXT[0]])
            for kk in range(4):
                ts(w_xr[:, half * 4 + kk, :], stg[:, kk, :], g_mix[:, half * 4 + kk:half * 4 + kk + 1], None, ALU.mult, None,
                   [TXT[0]] + CONST, [Tw_xr])
        conv_list = []
        for j in range(20):
            conv_list.append((sc_in[j], w_in_d[:, j * 128:(j + 1) * 128].rearrange("(k p) c -> p k c", p=128)))
        for j in range(8):
            conv_list.append((sc_out[j], w_out_d[:, j * 128:(j + 1) * 128].rearrange("(k p) c -> p k c", p=128)))
        for f in range(NF):
            conv_list.append((sc_g[f], w_g_d[:, f * 128:(f + 1) * 128].rearrange("(k p) c -> p k c", p=128)))
            conv_list.append((sc_u[f], w_u_d[:, f * 128:(f + 1) * 128].rearrange("(k p) c -> p k c", p=128)))
            conv_list.append((sc_d[f], w_d_d[f * 128:(f + 1) * 128, :]))
        for j in range(8):
            conv_list.append((sc_pg[j], w_pg_d[:, j * 128:(j + 1) * 128].rearrange("(k p) c -> p k c", p=128)))
        conv_pos = [0]
        conv_tok = []

        def emit_conv(n):
            for _ in range(n):
                if conv_pos[0] >= len(conv_list):
                    return
                o, i_ = conv_list[conv_pos[0]]
                conv_pos[0] += 1
                k.dma("pool", "wconv", lambda h, o=o, i_=i_: h.dma_start(out=o, in_=i_), (), [TSCR])
                if conv_pos[0] == len(conv_list):
                    TSCR.lastw = (("D", "wconv"), k.dsem["wconv"][1])

        def load_x(g):
            s_ = g % 2
            for b in range(2):
                r0 = g * T + b * 128
                k.dma("pool", f"x{s_}", lambda h, s_=s_, b=b, r0=r0: h.dma_start(out=xtok[s_][:, b, :], in_=xs_d[r0:r0 + 128, :]), (), [TXT[s_]])

        def transpose_in(src_tile, src_tok, nsb, Tn):
            for b in range(nsb):
                for kq in range(2):
                    pg, tpg = nextpg()
                    items = [(pg[:, i * 128:(i + 1) * 128], src_tile[:, b, (kq * 4 + i) * 128:(kq * 4 + i + 1) * 128]) for i in range(4)]
                    trlist(items, [src_tok] + CONST, [tpg])
                    evac(hT[:, kq * 4:(kq + 1) * 4, b * 128:(b + 1) * 128], pg[:, :].rearrange("p (a b) -> p a b", a=4), [tpg], THT[kq * 4:(kq + 1) * 4])

        def rms(src, stoks, nk, gvec, dst, dtoks, Tn, dim):
            act_op(AF.Square, sq[:, 0:nk, 0:Tn], src[:, 0:nk, 0:Tn], stoks, [TSQ])
            pg, tpg = nextpg()
            mmg(pg[:, 0:Tn], [(ones_bf[:], sq[:, kk, 0:Tn]) for kk in range(nk)], [TSQ] + CONST, [tpg])
            filler(FILL.get("rms", 0))
            act_op(AF.Ln, rt[:, 0:Tn], pg[:, 0:Tn], [tpg] + CONST, [TRT], bias=eps_t[:], scale=1.0 / dim)
            act_op(AF.Exp, rstd[:, 0:Tn], rt[:, 0:Tn], [TRT], [TRS], scale=-0.5)
            for kk in range(nk):
                stt(dst[:, kk, 0:Tn], src[:, kk, 0:Tn], gvec[:, kk:kk + 1], rstd[:, 0:Tn], ALU.mult, ALU.mult,
                    [stoks[kk], TRS] + CONST, [dtoks[kk]])

        def lru_p(c, Tn, g, sample):
            xb = g % 2
            XRc, XRn = XR2[xb], XR2[1 - xb]
            i2 = c % 2
            if not sample:
                xin = lambda j: XRc[:, c, j:j + Tn]
                xcv = xc[i2][:, 0:Tn]
                xrt = [TXR2[xb][c]]
            else:
                xin = lambda j: XRs[:, c, :, j:j + 32]
                xcv = xc[i2][:, 0:Tn].rearrange("p (s t) -> p s t", s=4)
                xrt = [TXRS]
            x2v = xc2[i2][:, 0:Tn] if not sample else xc2[i2][:, 0:Tn].rearrange("p (s t) -> p s t", s=4)
            ts(xcv, xin(3), cw[:, c * 4 + 3:c * 4 + 4], cb[:, c:c + 1], ALU.mult, ALU.add, xrt + CONST, [TXC[i2]])
            ts(x2v, xin(1), cw[:, c * 4 + 1:c * 4 + 2], None, ALU.mult, None, xrt + CONST, [TXC2[i2]])
            stt(xcv, xin(0), cw[:, c * 4 + 0:c * 4 + 1], xcv, ALU.mult, ALU.add, xrt + [TXC[i2]] + CONST, [TXC[i2]])
            stt(x2v, xin(2), cw[:, c * 4 + 2:c * 4 + 3], x2v, ALU.mult, ALU.add, xrt + [TXC2[i2]] + CONST, [TXC2[i2]])
            tt(xcv, xcv, x2v, ALU.add, [TXC[i2], TXC2[i2]], [TXC[i2]])
            cp("dve", xcb[i2][:, 0:Tn], xc[i2][:, 0:Tn], [TXC[i2]], [TXCB[i2]])
            if not sample:
                cp("act", XRn[:, c, 0:3], XRc[:, c, Tn:Tn + 3], [TXR2[xb][c]], [TXR2[1 - xb][c]])
            if g >= NLEAN and not sample:
                filler(FILL.get("lru", 0))
            pg, tpg = nextpg()
            mmlist([(pg[:, 0:Tn], wr_bd[:, c, :], xcb[i2][:, 0:Tn], True, True),
                    (pg[:, 256:256 + Tn], wi_bd[:, c, :], xcb[i2][:, 0:Tn], True, True)], [TXCB[i2]] + CONST, [tpg])
            act_op(AF.Sigmoid, rg[c][:, 0:Tn], pg[:, 0:Tn], [tpg] + CONST, [TRG[c]], bias=br[:, c:c + 1])
            act_op(AF.Sigmoid, ig[c][:, 0:Tn], pg[:, 256:256 + Tn], [tpg] + CONST, [TIG[c]], bias=bi[:, c:c + 1])
            tt(ig[c][:, 0:Tn], ig[c][:, 0:Tn], xc[i2][:, 0:Tn], ALU.mult, [TIG[c], TXC[i2]], [TIG[c]], eng="pool")

        def lru_q(c, Tn, g, sample, with_out):
            i2 = c % 2
            act_op(AF.Exp, a2[i2][:, 0:Tn], rg[c][:, 0:Tn], [TRG[c]] + CONST, [TA2[i2]], scale=nsp2[:, c:c + 1])
            act_op(AF.Ln, a2[i2][:, 0:Tn], a2[i2][:, 0:Tn], [TA2[i2]] + CONST, [TA2[i2]], bias=one_t[:], scale=-1.0)
            act_op(AF.Exp, a2[i2][:, 0:Tn], a2[i2][:, 0:Tn], [TA2[i2]], [TA2[i2]], scale=0.5)
            act_op(AF.Exp, av[i2][:, 0:Tn], rg[c][:, 0:Tn], [TRG[c]] + CONST, [TAV[i2]], scale=nsp[:, c:c + 1])
            tt(ig[c][:, 0:Tn], ig[c][:, 0:Tn], a2[i2][:, 0:Tn], ALU.mult, [TIG[c], TA2[i2]], [TIG[c]], eng="dve")
            if not sample:
                k.op("dve", lambda h: h.tensor_tensor_scan(out=hs[:, c, 0:Tn], data0=av[i2][:, 0:Tn], data1=ig[c][:, 0:Tn],
                                                           initial=hst[:, c:c + 1], op0=ALU.mult, op1=ALU.add),
                     [TAV[i2], TIG[c], THST[c]], [THS[c]])
                ts(hst[:, c:c + 1], hs[:, c, Tn - 1:Tn], flags[:, g:g + 1], None, ALU.mult, None, [THS[c]] + CONST, [THST[c]])
            else:
                for s_ in range(4):
                    k.op("dve", lambda h, s_=s_: h.tensor_tensor_scan(out=hs[:, c, s_ * 32:(s_ + 1) * 32], data0=av[i2][:, s_ * 32:(s_ + 1) * 32],
                                                                       data1=ig[c][:, s_ * 32:(s_ + 1) * 32], initial=hS[:, c, s_:s_ + 1],
                                                                       op0=ALU.mult, op1=ALU.add),
                         [TAV[i2], TIG[c], THS_S], [THS[c]])
            if with_out:
                tt(lruo[:, c, 0:Tn], hs[:, c, 0:Tn], G[:, c, 0:Tn], ALU.mult, [THS[c], TG[c]], [TLO[c]], eng="pool")

        def lru_all(Tn, g, sample, with_out):
            for c in range(4):
                lru_p(c, Tn, g, sample)
            for c in range(4):
                lru_q(c, Tn, g, sample, with_out)

        XB = [xn, mixT]
        TXB = [TXN, TMX]
        TXC3 = toks(3, "xcast")

        def lean_cast(g):
            i = g % 3
            k.dma("pool", f"xcast{i}", lambda h: h.dma_start(out=xbf[g * T:(g + 1) * T, :], in_=xs_d[g * T:(g + 1) * T, :]), (), [TXC3[i]])

        def lean_tr(g):
            p = g % 2
            for kk in range(8):
                k.dma("sp", f"xtr{p}_{kk}", lambda h, kk=kk: h.dma_start(out=XB[p][:, kk, :], in_=xbf[g * T:(g + 1) * T, kk * 128:(kk + 1) * 128], transpose=True),
                      [TXC3[g % 3]], [TXB[p][kk]])

        def lean_f0(g):
            p = g % 2
            act_op(AF.Square, sq[:, :, :], XB[p][:, :, :], TXB[p], [TSQ])

        def lean_f2(g):
            pg, tpg = nextpg()
            mmg(pg[:, 0:T], [(ones_bf[:], sq[:, kk, :]) for kk in range(8)], [TSQ] + CONST, [tpg])
            act_op(AF.Ln, rt[:, :], pg[:, 0:T], [tpg] + CONST, [TRT], bias=eps_t[:], scale=1.0 / 1024.0)
            act_op(AF.Exp, rstd[:, :], rt[:, :], [TRT], [TRS], scale=-0.5)

        def lean_f3(g):
            xb = g % 2
            for c in range(4):
                pg, tpg = nextpg()
                mmg(pg[:, 0:T], [(w_xr[:, kk, c * 128:(c + 1) * 128], XB[xb][:, kk, :]) for kk in range(8)], TXB[xb] + [Tw_xr], [tpg])
                tt(XR2[xb][:, c, 3:3 + T], pg[:, 0:T], rstd[:, :], ALU.mult, [tpg, TRS], [TXR2[xb][c]])

        def lean_front(g):
            lean_f0(g)
            lean_f2(g)
            lean_f3(g)

        def proj_main(Tn, g, nsb, want_q, want_gr, tok_out, sample, want_xr=False):
            slot6 = [(2 * g + b) % 6 for b in range(nsb)] if not sample else [0]
            blocks = []
            if want_q:
                blocks += list(range(0, 4))
            blocks += list(range(4, 12))
            if want_gr:
                blocks += list(range(12, 20))
            elif want_xr:
                blocks += list(range(12, 16))
            pkt = None
            for j in blocks:
                W, tw = wload(sc_in[j])
                if j < 8 or j >= 12:
                    pg, tpg = nextpg()
                    mmg(pg[:, 0:Tn], [(W[:, kk, :], xn[:, kk, 0:Tn]) for kk in range(8)], [tw] + TXN, [tpg])
                    if j < 4:
                        evac(qT[:, j, 0:Tn], pg[:, 0:Tn], [tpg], [TQ[j]])
                    elif j < 8:
                        if sample:
                            evac(kT[:, j - 4, 0:Tn], pg[:, 0:Tn], [tpg], [TKR[0]])
                        else:
                            c0 = slot6[0] * 128
                            evac(kT[:, j - 4, c0:c0 + Tn], pg[:, 0:Tn], [tpg], [TKR[s_] for s_ in slot6])
                    elif j < 16:
                        c = j - 12
                        if sample:
                            evac(XRs[:, c, :, 3:35], pg[:, 0:Tn].rearrange("p (s t) -> p s t", s=4), [tpg], [TXRS])
                        else:
                            evac(XR2[g % 2][:, c, 3:3 + Tn], pg[:, 0:Tn], [tpg], [TXR2[g % 2][c]])
                    else:
                        c = j - 16
                        act_op(AF.Gelu_apprx_tanh, G[:, c, 0:Tn], pg[:, 0:Tn], [tpg], [TG[c]])
                if (4 <= j < 8 and tok_out) or (8 <= j < 12):
                    jj = (j - 4) % 4
                    if jj == 0:
                        pkt = [(PG[3], TPG[3])] if sample else [(PSS[b_], TPSS[b_]) for b_ in range(nsb)]
                    for b in range(nsb):
                        mmg(pkt[b][0][:, jj * 128:(jj + 1) * 128], [(xn[:, kk, b * 128:(b + 1) * 128], W[:, kk, :]) for kk in range(8)],
                            [tw] + TXN, [pkt[b][1]])
                    if sample and j >= 8:
                        for s_ in range(4):
                            mmg(PSS[s_ // 2][0:32, (s_ % 2) * 512 + jj * 128:(s_ % 2) * 512 + (jj + 1) * 128],
                                [(xn[:, kk, s_ * 32:(s_ + 1) * 32], W[:, kk, :]) for kk in range(8)], [tw] + TXN, [TPSS[s_ // 2]])
                    if jj == 3:
                        for b in range(nsb):
                            pgt, tpgt = pkt[b]
                            if j >= 8:
                                if not sample:
                                    cp("dve", Vr[:, slot6[b], :], pgt[:, 0:512], [tpgt], [TVR[slot6[b]]])
                                else:
                                    for s_ in range(4):
                                        cp("dve", Vn[0:32, s_, :], PSS[s_ // 2][0:32, (s_ % 2) * 512:(s_ % 2 + 1) * 512], [TPSS[s_ // 2]], [TVN])
                            if tok_out:
                                oi = nexto()
                                cp("dve" if (j >= 8 and not sample) else "act", ost[oi][:, 0:512], pgt[:, 0:512], [tpgt], [TOS[oi]])
                                if sample:
                                    dst = (ksm_d if j < 8 else vsm_d)[:, :]
                                else:
                                    r0 = (g - (NT - 2)) * T + b * 128
                                    dst = (kp_d if j < 8 else vp_d)[r0:r0 + 128, :]
                                k.dma("pool", f"os{oi}", lambda h, oi=oi, dst=dst: h.dma_start(out=dst, in_=ost[oi][:, 0:512]), [TOS[oi]], [])

        osc = [0]

        def nexto():
            i = osc[0] % 2
            osc[0] += 1
            return i

        def attention_prompt(g, hook=None):
            expb5 = expb[:].rearrange("p h (k q) -> p h k q", k=5)
            units = [(hp, e, ps_) for hp in range(4) for e in range(2) for ps_ in range(2)]

            def slot_of(kb):
                return (2 * g - 4 + kb) % 6

            def emit_s(ui):
                hp, e, ps_ = units[ui]
                rows = slice(e * 64, (e + 1) * 64)
                S = PSS[ui % 2]
                sl = [slot_of(3 * ps_ + j) for j in range(3)]
                mmlist([(S[:, j * 256:(j + 1) * 256], kT[rows, hp, sl[j] * 128:(sl[j] + 1) * 128], qT[rows, hp, 0:256], True, True) for j in range(3)],
                       [TKR[s_] for s_ in sl] + [TQ[hp]], [TPSS[ui % 2]])

            po_cur = [None]
            emit_s(0)
            for ui, (hp, e, ps_) in enumerate(units):
                h_ = 2 * hp + e
                rows = slice(e * 64, (e + 1) * 64)
                si = ui % 2
                S = PSS[si]
                pi = 2 * ps_ + (ui // 2) % 2
                if ui + 1 < len(units):
                    emit_s(ui + 1)
                if e == 0 and ps_ == 0:
                    po_cur[0] = nextpg()
                po, tpo = po_cur[0]
                act_op(AF.Exp, E[si][:], S[:, 0:768], [TPSS[si]], [TE[si]], scale=SCALE)
                Ev = E[si][:].rearrange("p (k q) -> p k q", k=3)
                Pv = Pm[pi][:].rearrange("p (k q) -> p k q", k=3)
                if ps_ == 0:
                    tt(Pv[:, 0:3, 0:128], Ev[:, 0:3, 0:128], expb5[:, h_, 0:3, :], ALU.mult, [TE[si]] + CONST, [TP[pi]], eng="pool")
                    tt(Pv[:, 1:3, 128:256], Ev[:, 1:3, 128:256], expb5[:, h_, 0:2, :], ALU.mult, [TE[si]] + CONST, [TP[pi]], eng="pool")
                else:
                    tt(Pv[:, 0:2, 0:128], Ev[:, 0:2, 0:128], expb5[:, h_, 3:5, :], ALU.mult, [TE[si]] + CONST, [TP[pi]], eng="pool")
                    tt(Pv[:, 0:3, 128:256], Ev[:, 0:3, 128:256], expb5[:, h_, 2:5, :], ALU.mult, [TE[si]] + CONST, [TP[pi]], eng="pool")
                sl = [slot_of(3 * ps_ + j) for j in range(3)]
                items = [(po[rows, 0:256], Vr[:, sl[j], h_ * 64:(h_ + 1) * 64], Pm[pi][:, j * 256:(j + 1) * 256], (ps_ == 0 and j == 0), False) for j in range(3)]
                items += [(po[rows, 256:512], vones[:, sl[j], :], Pm[pi][:, j * 256:(j + 1) * 256], False, (ps_ == 1 and j == 2)) for j in range(3)]
                mmlist(items, [TP[pi]] + [TVR[s_] for s_ in sl] + [TVO[s_] for s_ in sl], [tpo])
                if hook is not None:
                    hook(ui)
                if e == 1 and ps_ == 1:
                    ri = hp % 2
                    k.op("dve", lambda h, ri=ri, po=po: h.reciprocal(out=rd[ri][:], in_=po[:, 256:512]), [tpo], [TRD[ri]])
                    tt(attnT[:, hp, :], po[:, 0:256], rd[ri][:], ALU.mult, [tpo, TRD[ri]], [TAT[hp]])
                    if g in (NLEAN, NLEAN + 1):
                        for b in range(2):
                            slots = [(2 * g + b - 4 + kb) % 6 for kb in range(5)]
                            pu, tpu = nextpg()
                            items = []
                            for e2 in range(2):
                                h2 = 2 * hp + e2
                                rows2 = slice(e2 * 64, (e2 + 1) * 64)
                                items += [(pu[rows2, 0:128], Vr[:, slots[kb], h2 * 64:(h2 + 1) * 64], band_bf[:, kb * 128:(kb + 1) * 128], kb == 0, kb == 4) for kb in range(5)]
                            mmlist(items, [TVR[s_] for s_ in slots] + CONST, [tpu])
                            ts(attnT[:, hp, b * 128:(b + 1) * 128], attnT[:, hp, b * 128:(b + 1) * 128], nc0, None, ALU.mult, None, [TAT[hp]] + CONST, [TAT[hp]])
                            stt(attnT[:, hp, b * 128:(b + 1) * 128], pu[:, 0:128], c0s, attnT[:, hp, b * 128:(b + 1) * 128], ALU.mult, ALU.add,
                                [tpu, TAT[hp]] + CONST, [TAT[hp]])
            if g == NLEAN + 1:
                ts(Vr[:, 0:4, :], Vr[:, 0:4, :], nc0, None, ALU.mult, None, TVR[0:4] + CONST, TVR[0:4])
                ts(vones[:, 0:4, :], vones[:, 0:4, :], nc0, None, ALU.mult, None, TVO[0:4] + CONST, TVO[0:4])

        def attention_sample():
            si_c = [0]
            for s_ in range(4):
                xk = xtok[s_ % 2]
                kview = xk[:].rearrange("p a (b c) -> p (a b) c", b=2)
                k.dma("pool", f"x{s_ % 2}", lambda h, s_=s_, kview=kview: h.dma_start(out=kview, in_=ck_d[s_].rearrange("(b p) f -> p b f", p=128)), (), [TXT[s_ % 2]])
                k.dma("pool", "vc", lambda h, s_=s_: h.dma_start(out=Vr[:, 0:4, :], in_=cv_d[s_].rearrange("(b p) f -> p b f", p=128)), (), [TVC] + TVR[0:4])
                for hp in range(4):
                    pg, tpg = nextpg()
                    trlist([(pg[:, blk * 128:(blk + 1) * 128], kview[:, blk, hp * 128:(hp + 1) * 128]) for blk in range(4)], [TXT[s_ % 2]] + CONST, [tpg])
                    evac(kT[:, hp, 256:768], pg[:, :], [tpg], [TKC] + TKR[2:6])
                for hp in range(4):
                    po, tpo = nextpg()
                    for e in range(2):
                        h_ = 2 * hp + e
                        rows = slice(e * 64, (e + 1) * 64)
                        si = si_c[0] % 2
                        si_c[0] += 1
                        S = PSS[si]
                        qv = qT[rows, hp, s_ * 32:(s_ + 1) * 32]
                        items = [(S[:, kb * 32:(kb + 1) * 32], kT[rows, hp, 256 + kb * 128:256 + (kb + 1) * 128], qv, True, True) for kb in range(4)]
                        items.append((S[0:32, 128:160], kT[rows, hp, s_ * 32:(s_ + 1) * 32], qv, True, True))
                        mmlist(items, [TKC, TKR[0], TQ[hp]], [TPSS[si]])
                        act_op(AF.Exp, E[si][:, 0:128], S[:, 0:128], [TPSS[si]], [TE[si]], scale=SCALE)
                        act_op(AF.Exp, E[si][0:32, 128:160], S[0:32, 128:160], [TPSS[si]], [TE[si]], scale=SCALE)
                        tt(Pm[si][:, 0:128], E[si][:, 0:128], expbs[:, h_, 0:128], ALU.mult, [TE[si]] + CONST, [TP[si]])
                        tt(Pm[si][0:32, 128:160], E[si][0:32, 128:160], expbs[0:32, h_, 128:160], ALU.mult, [TE[si]] + CONST, [TP[si]])
                        items = [(po[rows, 0:32], Vr[:, kb, h_ * 64:(h_ + 1) * 64], Pm[si][:, kb * 32:(kb + 1) * 32], kb == 0, False) for kb in range(4)]
                        items.append((po[rows, 0:32], Vn[0:32, s_, h_ * 64:(h_ + 1) * 64], Pm[si][0:32, 128:160], False, True))
                        items += [(po[rows, 32:64], ones_bf[:, 0:64], Pm[si][:, kb * 32:(kb + 1) * 32], kb == 0, False) for kb in range(4)]
                        items.append((po[rows, 32:64], ones_bf[0:32, 0:64], Pm[si][0:32, 128:160], False, True))
                        mmlist(items, [TP[si], TVC, TVN] + CONST, [tpo])
                    ri = hp % 2
                    k.op("dve", lambda h, ri=ri, po=po: h.reciprocal(out=rd[ri][:, 0:32], in_=po[:, 32:64]), [tpo], [TRD[ri]])
                    tt(attnT[:, hp, s_ * 32:(s_ + 1) * 32], po[:, 0:32], rd[ri][:, 0:32], ALU.mult, [tpo, TRD[ri]], [TAT[hp]])

        def back_half(Tn, nsb, p_src, p_r0, y_dst, y_r0):
            rms(attnT, TAT, 4, g_attn, mixT, TMX[0:4], Tn, 512.0)
            act_op(AF.Square, sq[:, 0:4, 0:Tn], lruo[:, 0:4, 0:Tn], TLO, [TSQ])
            pg, tpg = nextpg()
            mmg(pg[:, 0:Tn], [(ones_bf[:], sq[:, kk, 0:Tn]) for kk in range(4)], [TSQ] + CONST, [tpg])
            act_op(AF.Ln, rt[:, 0:Tn], pg[:, 0:Tn], [tpg] + CONST, [TRT], bias=eps_t[:], scale=1.0 / 512.0)
            act_op(AF.Exp, rstd[:, 0:Tn], rt[:, 0:Tn], [TRT], [TRS], scale=-0.5)
            for kk in range(4):
                stt(mixT[:, 4 + kk, 0:Tn], lruo[:, kk, 0:Tn], g_lru[:, kk:kk + 1], rstd[:, 0:Tn], ALU.mult, ALU.mult,
                    [TLO[kk], TRS] + CONST, [TMX[4 + kk]])
            for o in range(8):
                W, tw = wload(sc_out[o])
                pg, tpg = nextpg()
                mmg(pg[:, 0:Tn], [(W[:, kk, :], mixT[:, kk, 0:Tn]) for kk in range(8)], [tw] + TMX, [tpg])
                tt(hT[:, o, 0:Tn], hT[:, o, 0:Tn], pg[:, 0:Tn], ALU.add, [THT[o], tpg], [THT[o]])
            rms(hT, THT, 8, g_ffn, xn, TXN, Tn, 1024.0)
            def ffn_gu(f):
                Wg, twg = wload(sc_g[f])
                Wu, twu = wload(sc_u[f])
                Wd, twd = wload(sc_d[f], view1024=True)
                pg, tpg = nextpg()
                items = [(pg[:, 0:Tn], Wg[:, kk, :], xn[:, kk, 0:Tn], kk == 0, kk == 7) for kk in range(8)]
                items += [(pg[:, 256:256 + Tn], Wu[:, kk, :], xn[:, kk, 0:Tn], kk == 0, kk == 7) for kk in range(8)]
                mmlist(items, [twg, twu] + TXN, [tpg])
                return pg, tpg, Wd, twd

            cur = ffn_gu(0)
            for f in range(NF):
                nxt = ffn_gu(f + 1) if f + 1 < NF else None
                pg, tpg, Wd, twd = cur
                Wdv = Wd[:].rearrange("p a b -> p (a b)")
                fi = f % 2
                act_op(AF.Silu, sg[fi][:, 0:Tn], pg[:, 0:Tn], [tpg], [TSG[fi]])
                tt(act[fi][:, 0:Tn], sg[fi][:, 0:Tn], pg[:, 256:256 + Tn], ALU.mult, [TSG[fi], tpg], [TACT[fi]])
                items = [(PSS[o // 4][:, (o % 4) * 256:(o % 4) * 256 + Tn], Wdv[:, o * 128:(o + 1) * 128], act[fi][:, 0:Tn], (f == 0 and o % 2 == 0), (f == NF - 1 and o % 2 == 1)) for o in range(8)]
                mmlist(items, [twd, TACT[fi]], TPSS)
                cur = nxt
            for o in range(8):
                tt(hT[:, o, 0:Tn], hT[:, o, 0:Tn], PSS[o // 4][:, (o % 4) * 256:(o % 4) * 256 + Tn], ALU.add, [THT[o]] + TPSS, [THT[o]])
            rms(hT, THT, 8, g_ple, xn, TXN, Tn, 1024.0)
            k.dma("pool", "ptok", lambda h: h.dma_start(out=ptok[:, 0:nsb, :], in_=p_src[p_r0:p_r0 + Tn, :].rearrange("(b p) f -> p b f", p=128)), (), [TPT])
            pg, tpg = nextpg()
            trlist([(pg[:, (b * 2 + k2) * 128:(b * 2 + k2 + 1) * 128], ptok[:, b, k2 * 128:(k2 + 1) * 128]) for b in range(nsb) for k2 in range(2)],
                   [TPT] + CONST, [tpg])
            for b in range(nsb):
                evac(pT[:, 0:2, b * 128:(b + 1) * 128], pg[:, b * 256:(b + 1) * 256].rearrange("p (a b) -> p a b", a=2), [tpg], [TPTT])
            for o in range(8):
                W, tw = wload(sc_pg[o])
                pg, tpg = nextpg()
                items = [(pg[:, 0:Tn], W[:, kk, :], xn[:, kk, 0:Tn], kk == 0, kk == 7) for kk in range(8)]
                items += [(pg[:, 256:256 + Tn], w_pp[:, k2, o * 128:(o + 1) * 128], pT[:, k2, 0:Tn], k2 == 0, k2 == 1) for k2 in range(2)]
                mmlist(items, [tw, TPTT] + TXN + CONST, [tpg])
                fi = o % 2
                act_op(AF.Sigmoid, sg[fi][:, 0:Tn], pg[:, 0:Tn], [tpg], [TSG[fi]])
                tt(sg[fi][:, 0:Tn], pg[:, 256:256 + Tn], sg[fi][:, 0:Tn], ALU.mult, [TSG[fi], tpg], [TSG[fi]])
                tt(hT[:, o, 0:Tn], hT[:, o, 0:Tn], sg[fi][:, 0:Tn], ALU.add, [THT[o], TSG[fi]], [THT[o]])
            rms(hT, THT, 8, g_fin, hT, THT, Tn, 1024.0)
            for b in range(nsb):
                oi = nexto()
                for half in range(2):
                    pg, tpg = nextpg()
                    trlist([(pg[:, i * 128:(i + 1) * 128], hT[:, half * 4 + i, b * 128:(b + 1) * 128]) for i in range(4)], THT + CONST, [tpg])
                    evac(ost[oi][:, half * 512:(half + 1) * 512], pg[:, :], [tpg], [TOS[oi]])
                r0 = y_r0 + b * 128
                k.dma("pool", f"os{oi}", lambda h, oi=oi, r0=r0: h.dma_start(out=y_dst[r0:r0 + 128, :], in_=ost[oi][:]), [TOS[oi]], [])

        def sample_tile(Ts):
            k.dma("pool", "x0", lambda h: h.dma_start(out=xtok[0][:, 0, :], in_=xsm_d), (), [TXT[0]])
            k.dma("pool", "sst", lambda h: h.dma_start(out=XRs[:, :, :, 0:3], in_=sconv_d), (), [TXRS])
            k.dma("pool", "sst", lambda h: h.dma_start(out=hS[:], in_=sh_d), (), [THS_S])
            TXRS.lastw = (("D", "sst"), 32)
            THS_S.lastw = (("D", "sst"), 32)
            transpose_in(xtok[0], TXT[0], 1, Ts)
            rms(hT, THT, 8, g_mix, xn, TXN, Ts, 1024.0)
            proj_main(Ts, 0, 1, True, True, True, True)
            lru_all(Ts, 0, True, True)
            cp("act", hso[:], hs[:, :, 0:Ts].rearrange("p c (s t) -> p c s t", t=32)[:, :, :, 31], THS, [THSO])
            k.dma("pool", "st3", lambda h: h.dma_start(out=hsm_d, in_=hso[:]), [THSO], [])
            k.dma("pool", "st4", lambda h: h.dma_start(out=convs_d, in_=XRs[:, :, :, 32:35]), [TXRS], [])
            attention_sample()
            back_half(Ts, 1, psm_d, 0, ys_d, 0)

        tile_list = list(range(NT)) if dbg_tiles is None else list(dbg_tiles)
        if dbg_tiles is not None and dbg_conv:
            emit_conv(1000)
        NL2 = NLEAN - 2
        leans = [g for g in tile_list if g < NL2]
        if tile_list[0] >= NL2:
            load_x(tile_list[0])
        else:
            for g_ in leans[:2]:
                lean_cast(g_)
            lean_tr(leans[0])
        fronted = set()
        for ti, g in enumerate(tile_list):
            nxt = tile_list[ti + 1] if ti + 1 < len(tile_list) else None
            if nxt is not None and nxt >= NL2:
                load_x(nxt)
            if g < NLEAN and dbg_conv:
                emit_conv(3)
            if g < NL2:
                li = leans.index(g)
                if li + 2 < len(leans):
                    lean_cast(leans[li + 2])
                if li + 1 < len(leans):
                    lean_tr(leans[li + 1])
                if g not in fronted:
                    lean_front(g)
                if nxt is not None and nxt < NL2:
                    fronted.add(nxt)
                    lru_p(0, T, g, False)
                    lean_f0(nxt)
                    lru_p(1, T, g, False)
                    lru_p(2, T, g, False)
                    lru_p(3, T, g, False)
                    lean_f2(nxt)
                    lru_q(0, T, g, False, False)
                    lru_q(1, T, g, False, False)
                    lean_f3(nxt)
                    lru_q(2, T, g, False, False)
                    lru_q(3, T, g, False, False)
                else:
                    lru_all(T, g, False, False)
                continue
            transpose_in(xtok[g % 2], TXT[g % 2], 2, T)
            rms(hT, THT, 8, g_mix, xn, TXN, T, 1024.0)
            for b in range(2):
                s6 = (2 * g + b) % 6
                ts(vones[:, s6, :], ones_bf[:, 0:64], flags[:, g:g + 1], None, ALU.mult, None, CONST, [TVO[s6]])
            if g < NLEAN:
                proj_main(T, g, 2, False, False, False, False, want_xr=True)
                lru_all(T, g, False, False)
            else:
                proj_main(T, g, 2, True, True, g >= NT - 2, False)
                for c in range(4):
                    lru_p(c, T, g, False)
                attention_prompt(g, hook=lambda ui: lru_q(ui // 4, T, g, False, True) if ui % 4 == 3 else None)
                if _DBG.get("dump") == g:
                    k.dma("pool", "dbgo", lambda h: h.dma_start(out=dbg_d, in_=attnT[:]), TAT, [])
                back_half(T, 2, pm_d, (g - NLEAN) * T, y_d, (g - NLEAN) * T)
        if dbg_conv:
            emit_conv(1000)
        k.dma("pool", "st1", lambda h: h.dma_start(out=convp_d, in_=XR2[NT % 2][:, :, 0:3]), TXR2[NT % 2], [])
        k.dma("pool", "st2", lambda h: h.dma_start(out=hp_d, in_=hst[:]), THST, [])
        Ts = 128
        if dbg_sample:
            sample_tile(Ts)

        def _unused():
            pass

        print("total ops", k.total, {e: len(p) for e, p in k.prog.items()})
        out_keys = [kk for kk in ["os0", "os1", "st1", "st2", "st3", "st4"] if kk in k.dsem] + (["dbgo"] if "dbgo" in k.dsem else [])
        with nc.Block() as block:
            @block.tensor
            def _(h):
                k.replay("pe", h)

            @block.scalar
            def _(h):
                k.replay("act", h)

            @block.vector
            def _(h):
                k.replay("dve", h)

            @block.gpsimd
            def _(h):
                with nc.allow_non_contiguous_dma(reason="small strided state/param transfers"):
                    k.replay("pool", h, out_keys)

            @block.sync
            def _(h):
                k.replay("sp", h)
    return nc


_NC_CACHE = {}


def _fm(v, nk):
    return np.ascontiguousarray(np.asarray(v, np.float32).reshape(nk, 128).T)


def kernel(x_prompt, x_sample, p_prompt, p_sample, cache_k, cache_v, state_conv, state_h,
           g_mix, w_in, conv_w, conv_b, w_rgate, b_rgate, w_igate, b_igate, lru_lambda,
           rel_bias_table, g_attn_out, g_lru_out, w_out, g_ffn, w_ffn_gate, w_ffn_up,
           w_ffn_down, g_ple, w_ple_gate, w_ple_proj, g_final):
    f32 = np.float32
    x_prompt = np.asarray(x_prompt, f32)
    x_sample = np.asarray(x_sample, f32)
    p_prompt = np.asarray(p_prompt, f32)
    p_sample = np.asarray(p_sample, f32)
    cache_k = np.asarray(cache_k, f32)
    cache_v = np.asarray(cache_v, f32)
    state_conv = np.asarray(state_conv, f32)
    state_h = np.asarray(state_h, f32)
    table = np.asarray(rel_bias_table, f32)[0]

    p_idx = np.arange(128)
    biasp = np.full((128, 8, 5, 128), NEG, f32)
    for kb in range(5):
        rel = (kb * 128 + p_idx[:, None] - 512) - p_idx[None, :]
        idx = np.clip(rel, -128, 63) + 128
        kc = (kb * 128 + p_idx[:, None]) // 64 - 8
        qc = p_idx[None, :] // 64
        valid = (kc <= qc) & (kc >= qc - 8)
        vals = table[:, idx]
        blk = np.where(valid[None], vals, f32(NEG))
        biasp[:, :, kb, :] = blk.transpose(1, 0, 2)
    biass = np.full((128, 8, 5, 32), NEG, f32)
    q32 = np.arange(32)
    for kb in range(4):
        rel = (kb * 128 + p_idx[:, None] - 512) - q32[None, :]
        idx = np.clip(rel, -128, 63) + 128
        biass[:, :, kb, :] = table[:, idx].transpose(1, 0, 2)
    rel = q32[:, None] - q32[None, :]
    idx = np.clip(rel, -128, 63) + 128
    biass[0:32, :, 4, :] = table[:, idx].transpose(1, 0, 2)
    cwl = np.asarray(conv_w, f32)[0]
    cwfm = np.ascontiguousarray(cwl.reshape(4, 4, 128).transpose(2, 1, 0)).reshape(128, 16)
    vecs = np.concatenate([
        _fm(g_mix[0], 8), _fm(g_ffn[0], 8), _fm(g_ple[0], 8), _fm(g_final, 8),
        _fm(g_attn_out[0], 4), _fm(g_lru_out[0], 4), _fm(conv_b[0], 4), _fm(b_rgate[0], 4),
        _fm(b_igate[0], 4), _fm(lru_lambda[0], 4), cwfm, np.zeros((128, 2), f32)], axis=1).astype(f32)
    ident = np.eye(128, dtype=f32)
    shared = {
        "band": np.ascontiguousarray((biasp[:, 0] > -1e29).astype(f32).reshape(128, 640)),
        "biasp": biasp.reshape(128, 8 * 640), "biass": biass.reshape(128, 8 * 160), "ident": ident, "vecs": vecs,
        "w_in": np.asarray(w_in, f32)[0], "w_out": np.asarray(w_out, f32)[0],
        "w_ffn_gate": np.asarray(w_ffn_gate, f32)[0], "w_ffn_up": np.asarray(w_ffn_up, f32)[0],
        "w_ffn_down": np.asarray(w_ffn_down, f32)[0], "w_ple_gate": np.asarray(w_ple_gate, f32)[0],
        "w_ple_proj": np.asarray(w_ple_proj, f32)[0], "w_rgate": np.asarray(w_rgate, f32)[0],
        "w_igate": np.asarray(w_igate, f32)[0],
    }
    in_maps = []
    for c in range(8):
        s, j = c // 4, c % 4
        npad = (3 - j) * SEG
        xs = np.zeros((NT * T, D), f32)
        xs[npad:] = x_prompt[s, 0:(j + 1) * SEG]
        fl = np.zeros((NT,), f32)
        fl[npad // T:] = 1.0
        sq_ = slice(4 * c, 4 * c + 4)
        m = dict(shared)
        vc_ = vecs.copy()
        vc_[:, 72] = (1.0 / 576.0) if j == 0 else 0.0
        vc_[:, 73] = 0.0 if j == 0 else 1.0
        m["vecs"] = vc_
        m.update({
            "xs": xs, "pm": np.ascontiguousarray(p_prompt[0, s, j * SEG:(j + 1) * SEG]),
            "xsm": np.ascontiguousarray(x_sample[sq_].reshape(128, D)),
            "psm": np.ascontiguousarray(p_sample[0, sq_].reshape(128, 256)),
            "ck": np.ascontiguousarray(cache_k[0, sq_].reshape(4, 512, 512)),
            "cv": np.ascontiguousarray(cache_v[0, sq_].reshape(4, 512, 512)),
            "sconv": np.ascontiguousarray(state_conv[0, sq_].reshape(4, 3, 4, 128).transpose(3, 2, 0, 1)),
            "sh": np.ascontiguousarray(state_h[0, sq_].reshape(4, 4, 128).transpose(2, 1, 0)),
            "flags": np.ascontiguousarray(np.broadcast_to(fl[None, :], (128, NT))),
        })
        in_maps.append(m)
    if _DBG.get("prep_only"):
        return in_maps
    if "nc" not in _NC_CACHE:
        _NC_CACHE["nc"] = build_nc()
    res = run_bass_kernel_spmd(_NC_CACHE["nc"], in_maps, core_ids=list(range(8)))
    R = res.results
    y_prompt = np.stack([np.concatenate([R[s * 4 + j]["y"] for j in range(4)], axis=0) for s in range(2)]).astype(f32)
    y_sample = np.concatenate([R[c]["ys"].reshape(4, 32, D) for c in range(8)], axis=0).astype(f32)
    new_k_prompt = np.stack([R[s * 4 + 3]["kp"].reshape(512, 8, 64) for s in range(2)])[None].astype(f32)
    new_v_prompt = np.stack([R[s * 4 + 3]["vp"].reshape(512, 8, 64) for s in range(2)])[None].astype(f32)
    new_conv_prompt = np.stack([R[s * 4 + 3]["convp"].transpose(2, 1, 0).reshape(3, 512) for s in range(2)])[None].astype(f32)
    new_h_prompt = np.stack([R[s * 4 + 3]["hp"].T.reshape(512) for s in range(2)])[None].astype(f32)
    new_k_sample = np.concatenate([R[c]["ksm"].reshape(4, 32, 8, 64) for c in range(8)], axis=0)[None].astype(f32)
    new_v_sample = np.concatenate([R[c]["vsm"].reshape(4, 32, 8, 64) for c in range(8)], axis=0)[None].astype(f32)
    new_conv_sample = np.concatenate([R[c]["convs"].transpose(2, 3, 1, 0).reshape(4, 3, 512) for c in range(8)], axis=0)[None].astype(f32)
    new_h_sample = np.concatenate([R[c]["hsm"].transpose(2, 1, 0).reshape(4, 512) for c in range(8)], axis=0)[None].astype(f32)
    return (y_prompt, y_sample, new_k_prompt, new_v_prompt, new_conv_prompt, new_h_prompt,
            new_k_sample, new_v_sample, new_conv_sample, new_h_sample)
```

```python
import contextlib
import numpy as np
import concourse.bass as bass
import concourse.mybir as mybir
from concourse.bass_utils import run_bass_kernel_spmd

F32 = mybir.dt.float32
BF16 = mybir.dt.bfloat16
AF = mybir.ActivationFunctionType
ALU = mybir.AluOpType

D = 1024
T = 256
NLEAN = 48
NMAIN = 16
NT = NLEAN + NMAIN
SEG = 4096
DFF = 2816
NF = DFF // 128
SCALE = 0.125
EPS = 1e-6
NEG = -1e30
NWS = 10
FILL = {}


_DBG = {}


class Tok:
    __slots__ = ("lastw", "readers", "name")

    def __init__(self, name=""):
        self.lastw = None
        self.readers = []
        self.name = name


def toks(n, name=""):
    return [Tok(name + str(i)) for i in range(n)]


class KB:
    def __init__(self, nc, st):
        self.nc = nc
        self.st = st
        self.eh = {"pe": nc.tensor, "act": nc.scalar, "dve": nc.vector, "pool": nc.gpsimd, "sp": nc.sync}
        self.prog = {e: [] for e in self.eh}
        self.sem = {e: st.enter_context(nc.semaphore("sem_" + e)) for e in self.eh}
        self.cnt = {e: 0 for e in self.eh}
        self.seen = {e: {} for e in self.eh}
        self.dsem = {}
        self.total = 0
        self.limit = _DBG.get("limit")

    def semh(self, k):
        return self.sem[k] if isinstance(k, str) else self.dsem[k[1]][0]

    def _deps(self, eng, reads, writes):
        w = {}
        for b in reads:
            if b.lastw:
                k, v = b.lastw
                w[k] = max(w.get(k, 0), v)
        for b in writes:
            if b.lastw:
                k, v = b.lastw
                w[k] = max(w.get(k, 0), v)
            for k, v in b.readers:
                w[k] = max(w.get(k, 0), v)
        out = []
        for k, v in w.items():
            if k == eng and eng == "pe":
                continue
            if self.seen[eng].get(k, 0) < v:
                self.seen[eng][k] = v
                out.append((k, v))
        return out

    def op(self, eng, fn, reads=(), writes=()):
        self.total += 1
        if self.limit is not None and self.total > self.limit:
            return
        wl = self._deps(eng, reads, writes)
        self.cnt[eng] += 1
        v = self.cnt[eng]
        self.prog[eng].append((wl, fn, (eng, 1)))
        for b in reads:
            b.readers.append((eng, v))
        for b in writes:
            b.lastw = (eng, v)
            b.readers = []

    def dma(self, q, key, fn, reads=(), writes=()):
        self.total += 1
        if self.limit is not None and self.total > self.limit:
            return
        wl = self._deps(q, reads, writes)
        if key not in self.dsem:
            self.dsem[key] = [self.st.enter_context(self.nc.semaphore("d_" + key)), 0]
        d = self.dsem[key]
        d[1] += 16
        v = d[1]
        sk = ("D", key)
        self.prog[q].append((wl, fn, (sk, 16)))
        for b in reads:
            b.readers.append((sk, v))
        for b in writes:
            b.lastw = (sk, v)
            b.readers = []

    def replay(self, e, h, final_keys=()):
        for wl, fn, (sk, n) in self.prog[e]:
            for k, v in wl:
                h.wait_ge(self.semh(k), v)
            ins = fn(h)
            ins.then_inc(self.semh(sk), n)
        for key in final_keys:
            h.wait_ge(self.dsem[key][0], self.dsem[key][1])


def build_nc(dbg_tiles=None, dbg_sample=True, dbg_conv=True):
    nc = bass.Bass("TRN2", target_bir_lowering=False)

    def din(name, shape, dt=F32):
        return nc.dram_tensor(name, list(shape), dt, kind="ExternalInput").ap()

    def dout(name, shape):
        return nc.dram_tensor(name, list(shape), F32, kind="ExternalOutput").ap()

    def dint(name, shape, dt=BF16):
        return nc.dram_tensor(name, list(shape), dt, kind="Internal").ap()

    xs_d = din("xs", [NT * T, D])
    pm_d = din("pm", [SEG, 256])
    xsm_d = din("xsm", [128, D])
    psm_d = din("psm", [128, 256])
    ck_d = din("ck", [4, 512, 512])
    cv_d = din("cv", [4, 512, 512])
    sconv_d = din("sconv", [128, 4, 4, 3])
    sh_d = din("sh", [128, 4, 4])
    flags_d = din("flags", [128, NT])
    biasp_d = din("biasp", [128, 8 * 640])
    biass_d = din("biass", [128, 8 * 160])
    ident_d = din("ident", [128, 128])
    vec_d = din("vecs", [128, 74])
    band_d = din("band", [128, 640])
    w_in_d = din("w_in", [D, 2560])
    w_out_d = din("w_out", [D, D])
    w_g_d = din("w_ffn_gate", [D, DFF])
    w_u_d = din("w_ffn_up", [D, DFF])
    w_d_d = din("w_ffn_down", [DFF, D])
    w_pg_d = din("w_ple_gate", [D, D])
    w_pp_d = din("w_ple_proj", [256, D])
    w_r_d = din("w_rgate", [8, 64, 64])
    w_i_d = din("w_igate", [8, 64, 64])
    y_d = dout("y", [SEG, D])
    ys_d = dout("ys", [128, D])
    kp_d = dout("kp", [512, 512])
    vp_d = dout("vp", [512, 512])
    convp_d = dout("convp", [128, 4, 3])
    hp_d = dout("hp", [128, 4])
    ksm_d = dout("ksm", [128, 512])
    vsm_d = dout("vsm", [128, 512])
    convs_d = dout("convs", [128, 4, 4, 3])
    hsm_d = dout("hsm", [128, 4, 4])
    dbg_d = dout("dbgo", [128, 4, T]) if _DBG.get("dump") is not None else None
    sc_in = dint("sc_in", [20, 128, 8, 128])
    sc_out = dint("sc_out", [8, 128, 8, 128])
    sc_g = dint("sc_g", [NF, 128, 8, 128])
    sc_u = dint("sc_u", [NF, 128, 8, 128])
    sc_d = dint("sc_d", [NF, 128, 1024])
    sc_pg = dint("sc_pg", [8, 128, 8, 128])
    xbf = dint("xbf", [(NLEAN - 2) * T, D])

    with contextlib.ExitStack() as st:
        def sb(name, shape, dt=F32):
            return st.enter_context(nc.sbuf_tensor("s_" + name, list(shape), dt))

        def ps(name, shape, dt=F32):
            return st.enter_context(nc.psum_tensor("p_" + name, list(shape), dt))

        k = KB(nc, st)
        ident = sb("ident", [128, 128])
        ones_bf = sb("ones_bf", [128, 128], BF16)
        eps_t = sb("eps_t", [128, 1])
        one_t = sb("one_t", [128, 1])
        vecs = sb("vecs", [128, 74])
        band_bf = sb("band_bf", [128, 640], BF16)
        flags = sb("flags", [128, NT])
        nsp = sb("nsp", [128, 4])
        nsp2 = sb("nsp2", [128, 4])
        nb = sb("nb", [128, 8])
        lamt = sb("lamt", [128, 4])
        expb = sb("expb", [128, 8, 640])
        expbs = sb("expbs", [128, 8, 160])
        w_xr = sb("w_xr", [128, 8, 512], BF16)
        w_pp = sb("w_pp", [128, 2, 1024], BF16)
        wr_bd = sb("wr_bd", [128, 4, 128], BF16)
        wi_bd = sb("wi_bd", [128, 4, 128], BF16)
        WS = [sb(f"ws{i}", [128, 8, 128], BF16) for i in range(NWS)]
        xtok = [sb(f"xtok{i}", [128, 2, 1024]) for i in range(2)]
        ptok = sb("ptok", [128, 2, 256])
        pT = sb("pT", [128, 2, T], BF16)
        hT = sb("hT", [128, 8, T])
        xn = sb("xn", [128, 8, T], BF16)
        sq = sb("sq", [128, 8, T], BF16)
        rt = sb("rt", [128, T])
        rstd = sb("rstd", [128, T])
        qT = sb("qT", [128, 4, T], BF16)
        kT = sb("kT", [128, 4, 768], BF16)
        Vr = sb("Vr", [128, 6, 512], BF16)
        vones = sb("vones", [128, 6, 64], BF16)
        XR2 = [sb(f"XR{i}", [128, 4, T + 3]) for i in range(2)]
        XRs = sb("XRs", [128, 4, 4, 35])
        G = sb("G", [128, 4, T])
        hs = sb("hs", [128, 4, T])
        lruo = sb("lruo", [128, 4, T])
        hst = sb("hst", [128, 4])
        hS = sb("hS", [128, 4, 4])
        hso = sb("hso", [128, 4, 4])
        xc = [sb(f"xc{i}", [128, T]) for i in range(2)]
        xc2 = [sb(f"xc2{i}", [128, T]) for i in range(2)]
        xcb = [sb(f"xcb{i}", [128, T], BF16) for i in range(2)]
        rg = [sb(f"rg{i}", [128, T]) for i in range(4)]
        ig = [sb(f"ig{i}", [128, T]) for i in range(4)]
        av = [sb(f"av{i}", [128, T]) for i in range(2)]
        a2 = [sb(f"a2{i}", [128, T]) for i in range(2)]
        E = [sb(f"E{i}", [128, 768]) for i in range(2)]
        Pm = [sb(f"Pm{i}", [128, 768], BF16) for i in range(4)]
        rd = [sb(f"rd{i}", [128, 256]) for i in range(2)]
        attnT = sb("attnT", [128, 4, T])
        mixT = sb("mixT", [128, 8, T], BF16)
        sg = [sb(f"sg{i}", [128, T]) for i in range(2)]
        act = [sb(f"act{i}", [128, T], BF16) for i in range(2)]
        ost = [sb(f"ost{i}", [128, 1024]) for i in range(2)]
        Vn = sb("Vn", [32, 4, 512], BF16)
        PG = [ps(f"pg{i}", [128, 512]) for i in range(4)]
        PSS = [ps(f"pss{i}", [128, 1024]) for i in range(2)]
        TPG = toks(4, "pg")
        TPSS = toks(2, "pss")
        pgc = [0]

        pgmod = [4]

        def nextpg():
            i = pgc[0] % pgmod[0]
            pgc[0] += 1
            return PG[i], TPG[i]

        Tconst = Tok("const")
        Tw_xr = Tok("w_xr")
        THT = toks(8, "hT")
        TXN = toks(8, "xn")
        TSQ, TRT, TRS = Tok("sq"), Tok("rt"), Tok("rstd")
        TQ = toks(4, "q")
        TKR = toks(6, "kr")
        TVR = toks(6, "vr")
        TVO = toks(6, "vo")
        TXR2 = [toks(4, "xra"), toks(4, "xrb")]
        TG = toks(4, "g")
        THS = toks(4, "hs")
        TLO = toks(4, "lo")
        THST = toks(4, "hst")
        TXC, TXCB, TAV, TA2 = (toks(2, n) for n in ("xc", "xcb", "av", "a2"))
        TRG, TIG = toks(4, "rg"), toks(4, "ig")
        TXC2 = toks(2, "xc2")
        TE, TP, TRD = toks(2, "E"), toks(4, "P"), toks(2, "rd")
        TAT = toks(4, "at")
        TMX = toks(8, "mx")
        TSG, TACT = toks(2, "sg"), toks(2, "act")
        TOS = toks(2, "os")
        TXT = toks(2, "xt")
        TPT, TPTT = Tok("ptok"), Tok("pT")
        TWS = toks(NWS, "ws")
        TSCR = Tok("scr")
        TVC, TKC, TVN = Tok("vc"), Tok("kc"), Tok("vn")
        TXRS, THS_S, THSO = Tok("xrs"), Tok("hS"), Tok("hso")
        setup_toks = []

        def act_op(func, out, in_, reads, writes, bias=None, scale=None):
            def fn(h):
                kw = {}
                if bias is not None:
                    kw["bias"] = bias
                if scale is not None:
                    kw["scale"] = scale
                return h.activation(out=out, in_=in_, func=func, **kw)
            k.op("act", fn, reads, writes)

        def tt(out, in0, in1, op, reads, writes, eng="dve"):
            k.op(eng, lambda h: h.tensor_tensor(out=out, in0=in0, in1=in1, op=op), reads, writes)

        def ts(out, in0, s1, s2, op0, op1, reads, writes):
            if op1 is None:
                k.op("dve", lambda h: h.tensor_scalar(out=out, in0=in0, scalar1=s1, scalar2=None, op0=op0), reads, writes)
            else:
                k.op("dve", lambda h: h.tensor_scalar(out=out, in0=in0, scalar1=s1, scalar2=s2, op0=op0, op1=op1), reads, writes)

        def stt(out, in0, scalar, in1, op0, op1, reads, writes):
            k.op("dve", lambda h: h.scalar_tensor_tensor(out=out, in0=in0, scalar=scalar, in1=in1, op0=op0, op1=op1), reads, writes)

        def cp(eng, out, in_, reads, writes):
            if eng == "act":
                k.op("act", lambda h: h.copy(out=out, in_=in_), reads, writes)
            else:
                k.op("dve", lambda h: h.tensor_copy(out=out, in_=in_), reads, writes)

        evc = [0]

        def evac(out, in_, reads, writes):
            evc[0] += 1
            cp("act" if evc[0] % 2 else "dve", out, in_, reads, writes)

        def mmg(out, pairs, reads, writes, start=True, stop=True):
            def fn(h):
                n = len(pairs)
                ins = None
                for i, (l, r) in enumerate(pairs):
                    ins = h.matmul(out, l, r, start=(start and i == 0), stop=(stop and i == n - 1))
                return ins
            k.op("pe", fn, reads, writes)

        def mmlist(items, reads, writes):
            def fn(h):
                ins = None
                for (o, l, r, s0, s1) in items:
                    ins = h.matmul(o, l, r, start=s0, stop=s1)
                return ins
            k.op("pe", fn, reads, writes)

        def filler(n):
            if n <= 0:
                return
            def fn(h):
                ins = None
                for _ in range(n):
                    ins = h.matmul(PG[3][:, 0:256], ones_bf[:], w_xr[:, 0, 0:256], start=True, stop=True)
                return ins
            k.op("pe", fn, (), ())

        def trlist(items, reads, writes):
            def fn(h):
                ins = None
                for (o, i_) in items:
                    ins = h.transpose(o, i_, ident[:])
                return ins
            k.op("pe", fn, reads, writes)

        def sdma(out, in_, writes):
            k.dma("pool", "setup", lambda h: h.dma_start(out=out, in_=in_), (), writes)
            setup_toks.extend(writes)

        wsc = [0]

        def wload(src_ap, view1024=False):
            i = wsc[0] % NWS
            wsc[0] += 1
            dst = WS[i][:].rearrange("p a b -> p (a b)") if view1024 else WS[i][:]
            k.dma("sp", f"ws{i}", lambda h: h.dma_start(out=dst, in_=src_ap), [TSCR], [TWS[i]])
            return WS[i], TWS[i]

        sdma(ident[:], ident_d, [Tconst])
        sdma(vecs[:], vec_d, [Tconst])
        sdma(flags[:], flags_d, [Tconst])
        sdma(E[0][:, 0:640], band_d, [Tconst])
        sdma(expb[:].rearrange("p a b -> p (a b)"), biasp_d, [Tconst])
        sdma(expbs[:].rearrange("p a b -> p (a b)"), biass_d, [Tconst])
        sdma(w_pp[:], w_pp_d.rearrange("(k p) c -> p k c", p=128), [Tconst])
        Tbd = Tok("bd")
        k.op("dve", lambda h: h.memset(wr_bd[:], 0.0), (), [Tbd])
        k.op("dve", lambda h: h.memset(wi_bd[:], 0.0), (), [Tbd])
        for c in range(4):
            for e in range(2):
                sdma(wr_bd[e * 64:(e + 1) * 64, c, e * 64:(e + 1) * 64], w_r_d[2 * c + e], [Tbd])
                sdma(wi_bd[e * 64:(e + 1) * 64, c, e * 64:(e + 1) * 64], w_i_d[2 * c + e], [Tbd])
        fin = ("D", "setup"), (k.dsem["setup"][1] if "setup" in k.dsem else 0)
        for t_ in setup_toks:
            t_.lastw = fin
        Tc2 = Tok("c2")
        k.op("dve", lambda h: h.memset(ones_bf[:], 1.0), (), [Tc2])
        k.op("dve", lambda h: h.tensor_copy(out=band_bf[:], in_=E[0][:, 0:640]), [Tconst], [Tc2])
        k.op("dve", lambda h: h.memset(eps_t[:], EPS), (), [Tc2])
        k.op("dve", lambda h: h.memset(one_t[:], 1.0), (), [Tc2])
        for i_ in range(4):
            k.op("dve", lambda h, i_=i_: h.memset(Pm[i_][:], 0.0), (), [TP[i_]])
        k.op("dve", lambda h: h.memset(hst[:], 0.0), (), THST)
        k.op("dve", lambda h: h.memset(XR2[0][:], 0.0), (), TXR2[0])
        k.op("dve", lambda h: h.memset(XR2[1][:], 0.0), (), TXR2[1])
        k.op("dve", lambda h: h.memset(vones[:], 0.0), (), TVO)
        k.op("dve", lambda h: h.memset(kT[:], 0.0), (), TKR)
        k.op("dve", lambda h: h.memset(Vr[:], 0.0), (), TVR)
        act_op(AF.Exp, expb[:], expb[:], [Tconst], [Tconst])
        act_op(AF.Exp, expbs[:], expbs[:], [Tconst], [Tconst])
        act_op(AF.Exp, lamt[:], vecs[:, 52:56], [Tconst], [Tc2], scale=-1.0)
        act_op(AF.Ln, lamt[:], lamt[:], [Tc2], [Tc2], bias=one_t[:])
        ts(nsp[:], lamt[:], -8.0, None, ALU.mult, None, [Tc2], [Tc2])
        ts(nsp2[:], lamt[:], -16.0, None, ALU.mult, None, [Tc2], [Tc2])
        ts(nb[:], vecs[:, 44:52], -1.0, None, ALU.mult, None, [Tconst], [Tc2])
        CONST = [Tconst, Tc2, Tbd]
        g_mix, g_ffn, g_ple, g_fin = vecs[:, 0:8], vecs[:, 8:16], vecs[:, 16:24], vecs[:, 24:32]
        g_attn, g_lru = vecs[:, 32:36], vecs[:, 36:40]
        cb, br, bi = vecs[:, 40:44], vecs[:, 44:48], vecs[:, 48:52]
        cw = vecs[:, 56:72]
        c0s, nc0 = vecs[:, 72:73], vecs[:, 73:74]

        for half in range(2):
            stg = xtok[0][:].rearrange("p a (b c) -> p (a b) c", b=2)
            k.dma("pool", "x0", lambda h, half=half, stg=stg: h.dma_start(
                out=stg, in_=w_in_d[half * 512:(half + 1) * 512, 1536:2048].rearrange("(k p) c -> p k c", p=128)), (), [TXT[0]])
            for kk in range(4):
                ts(w_xr[:, half * 4 + kk, :], stg[:, kk, :], g_mix[:, half * 4 + kk:half * 4 + kk + 1], None, ALU.mult, None,
                   [TXT[0]] + CONST, [Tw_xr])
        conv_list = []
        for j in range(20):
            conv_list.append((sc_in[j], w_in_d[:, j * 128:(j + 1) * 128].rearrange("(k p) c -> p k c", p=128)))
        for j in range(8):
            conv_list.append((sc_out[j], w_out_d[:, j * 128:(j + 1) * 128].rearrange("(k p) c -> p k c", p=128)))
        for f in range(NF):
            conv_list.append((sc_g[f], w_g_d[:, f * 128:(f + 1) * 128].rearrange("(k p) c -> p k c", p=128)))
            conv_list.append((sc_u[f], w_u_d[:, f * 128:(f + 1) * 128].rearrange("(k p) c -> p k c", p=128)))
            conv_list.append((sc_d[f], w_d_d[f * 128:(f + 1) * 128, :]))
        for j in range(8):
            conv_list.append((sc_pg[j], w_pg_d[:, j * 128:(j + 1) * 128].rearrange("(k p) c -> p k c", p=128)))
        conv_pos = [0]
        conv_tok = []

        def emit_conv(n):
            for _ in range(n):
                if conv_pos[0] >= len(conv_list):
                    return
                o, i_ = conv_list[conv_pos[0]]
                conv_pos[0] += 1
                k.dma("pool", "wconv", lambda h, o=o, i_=i_: h.dma_start(out=o, in_=i_), (), [TSCR])
                if conv_pos[0] == len(conv_list):
                    TSCR.lastw = (("D", "wconv"), k.dsem["wconv"][1])

        def load_x(g):
            s_ = g % 2
            for b in range(2):
                r0 = g * T + b * 128
                k.dma("pool", f"x{s_}", lambda h, s_=s_, b=b, r0=r0: h.dma_start(out=xtok[s_][:, b, :], in_=xs_d[r0:r0 + 128, :]), (), [TXT[s_]])

        def transpose_in(src_tile, src_tok, nsb, Tn):
            for b in range(nsb):
                for kq in range(2):
                    pg, tpg = nextpg()
                    items = [(pg[:, i * 128:(i + 1) * 128], src_tile[:, b, (kq * 4 + i) * 128:(kq * 4 + i + 1) * 128]) for i in range(4)]
                    trlist(items, [src_tok] + CONST, [tpg])
                    evac(hT[:, kq * 4:(kq + 1) * 4, b * 128:(b + 1) * 128], pg[:, :].rearrange("p (a b) -> p a b", a=4), [tpg], THT[kq * 4:(kq + 1) * 4])

        def rms(src, stoks, nk, gvec, dst, dtoks, Tn, dim):
            act_op(AF.Square, sq[:, 0:nk, 0:Tn], src[:, 0:nk, 0:Tn], stoks, [TSQ])
            pg, tpg = nextpg()
            mmg(pg[:, 0:Tn], [(ones_bf[:], sq[:, kk, 0:Tn]) for kk in range(nk)], [TSQ] + CONST, [tpg])
            filler(FILL.get("rms", 0))
            act_op(AF.Ln, rt[:, 0:Tn], pg[:, 0:Tn], [tpg] + CONST, [TRT], bias=eps_t[:], scale=1.0 / dim)
            act_op(AF.Exp, rstd[:, 0:Tn], rt[:, 0:Tn], [TRT], [TRS], scale=-0.5)
            for kk in range(nk):
                stt(dst[:, kk, 0:Tn], src[:, kk, 0:Tn], gvec[:, kk:kk + 1], rstd[:, 0:Tn], ALU.mult, ALU.mult,
                    [stoks[kk], TRS] + CONST, [dtoks[kk]])

        def lru_p(c, Tn, g, sample):
            xb = g % 2
            XRc, XRn = XR2[xb], XR2[1 - xb]
            i2 = c % 2
            if not sample:
                xin = lambda j: XRc[:, c, j:j + Tn]
                xcv = xc[i2][:, 0:Tn]
                xrt = [TXR2[xb][c]]
            else:
                xin = lambda j: XRs[:, c, :, j:j + 32]
                xcv = xc[i2][:, 0:Tn].rearrange("p (s t) -> p s t", s=4)
                xrt = [TXRS]
            x2v = xc2[i2][:, 0:Tn] if not sample else xc2[i2][:, 0:Tn].rearrange("p (s t) -> p s t", s=4)
            ts(xcv, xin(3), cw[:, c * 4 + 3:c * 4 + 4], cb[:, c:c + 1], ALU.mult, ALU.add, xrt + CONST, [TXC[i2]])
            ts(x2v, xin(1), cw[:, c * 4 + 1:c * 4 + 2], None, ALU.mult, None, xrt + CONST, [TXC2[i2]])
            stt(xcv, xin(0), cw[:, c * 4 + 0:c * 4 + 1], xcv, ALU.mult, ALU.add, xrt + [TXC[i2]] + CONST, [TXC[i2]])
            stt(x2v, xin(2), cw[:, c * 4 + 2:c * 4 + 3], x2v, ALU.mult, ALU.add, xrt + [TXC2[i2]] + CONST, [TXC2[i2]])
            tt(xcv, xcv, x2v, ALU.add, [TXC[i2], TXC2[i2]], [TXC[i2]])
            cp("dve", xcb[i2][:, 0:Tn], xc[i2][:, 0:Tn], [TXC[i2]], [TXCB[i2]])
            if not sample:
                cp("act", XRn[:, c, 0:3], XRc[:, c, Tn:Tn + 3], [TXR2[xb][c]], [TXR2[1 - xb][c]])
            if g >= NLEAN and not sample:
                filler(FILL.get("lru", 0))
            pg, tpg = nextpg()
            mmlist([(pg[:, 0:Tn], wr_bd[:, c, :], xcb[i2][:, 0:Tn], True, True),
                    (pg[:, 256:256 + Tn], wi_bd[:, c, :], xcb[i2][:, 0:Tn], True, True)], [TXCB[i2]] + CONST, [tpg])
            act_op(AF.Sigmoid, rg[c][:, 0:Tn], pg[:, 0:Tn], [tpg] + CONST, [TRG[c]], bias=br[:, c:c + 1])
            act_op(AF.Sigmoid, ig[c][:, 0:Tn], pg[:, 256:256 + Tn], [tpg] + CONST, [TIG[c]], bias=bi[:, c:c + 1])
            tt(ig[c][:, 0:Tn], ig[c][:, 0:Tn], xc[i2][:, 0:Tn], ALU.mult, [TIG[c], TXC[i2]], [TIG[c]], eng="pool")

        def lru_q(c, Tn, g, sample, with_out):
            i2 = c % 2
            act_op(AF.Exp, av[i2][:, 0:Tn], rg[c][:, 0:Tn], [TRG[c]] + CONST, [TAV[i2]], scale=nsp[:, c:c + 1])
            act_op(AF.Exp, a2[i2][:, 0:Tn], rg[c][:, 0:Tn], [TRG[c]] + CONST, [TA2[i2]], scale=nsp2[:, c:c + 1])
            act_op(AF.Ln, a2[i2][:, 0:Tn], a2[i2][:, 0:Tn], [TA2[i2]] + CONST, [TA2[i2]], bias=one_t[:], scale=-1.0)
            act_op(AF.Exp, a2[i2][:, 0:Tn], a2[i2][:, 0:Tn], [TA2[i2]], [TA2[i2]], scale=0.5)
            tt(ig[c][:, 0:Tn], ig[c][:, 0:Tn], a2[i2][:, 0:Tn], ALU.mult, [TIG[c], TA2[i2]], [TIG[c]], eng="dve")
            if not sample:
                k.op("dve", lambda h: h.tensor_tensor_scan(out=hs[:, c, 0:Tn], data0=av[i2][:, 0:Tn], data1=ig[c][:, 0:Tn],
                                                           initial=hst[:, c:c + 1], op0=ALU.mult, op1=ALU.add),
                     [TAV[i2], TIG[c], THST[c]], [THS[c]])
                ts(hst[:, c:c + 1], hs[:, c, Tn - 1:Tn], flags[:, g:g + 1], None, ALU.mult, None, [THS[c]] + CONST, [THST[c]])
            else:
                for s_ in range(4):
                    k.op("dve", lambda h, s_=s_: h.tensor_tensor_scan(out=hs[:, c, s_ * 32:(s_ + 1) * 32], data0=av[i2][:, s_ * 32:(s_ + 1) * 32],
                                                                       data1=ig[c][:, s_ * 32:(s_ + 1) * 32], initial=hS[:, c, s_:s_ + 1],
                                                                       op0=ALU.mult, op1=ALU.add),
                         [TAV[i2], TIG[c], THS_S], [THS[c]])
            if with_out:
                tt(lruo[:, c, 0:Tn], hs[:, c, 0:Tn], G[:, c, 0:Tn], ALU.mult, [THS[c], TG[c]], [TLO[c]], eng="pool")

        def lru_all(Tn, g, sample, with_out):
            for c in range(4):
                lru_p(c, Tn, g, sample)
            for c in range(4):
                lru_q(c, Tn, g, sample, with_out)

        XB = [xn, mixT]
        TXB = [TXN, TMX]
        TXC3 = toks(3, "xcast")

        def lean_cast(g):
            i = g % 3
            k.dma("pool", f"xcast{i}", lambda h: h.dma_start(out=xbf[g * T:(g + 1) * T, :], in_=xs_d[g * T:(g + 1) * T, :]), (), [TXC3[i]])

        def lean_tr(g):
            p = g % 2
            for kk in range(8):
                k.dma("sp", f"xtr{p}_{kk}", lambda h, kk=kk: h.dma_start(out=XB[p][:, kk, :], in_=xbf[g * T:(g + 1) * T, kk * 128:(kk + 1) * 128], transpose=True),
                      [TXC3[g % 3]], [TXB[p][kk]])

        def lean_f0(g):
            p = g % 2
            act_op(AF.Square, sq[:, :, :], XB[p][:, :, :], TXB[p], [TSQ])

        def lean_f2(g):
            pg, tpg = nextpg()
            mmg(pg[:, 0:T], [(ones_bf[:], sq[:, kk, :]) for kk in range(8)], [TSQ] + CONST, [tpg])
            act_op(AF.Ln, rt[:, :], pg[:, 0:T], [tpg] + CONST, [TRT], bias=eps_t[:], scale=1.0 / 1024.0)
            act_op(AF.Exp, rstd[:, :], rt[:, :], [TRT], [TRS], scale=-0.5)

        def lean_f3(g):
            xb = g % 2
            for c in range(4):
                pg, tpg = nextpg()
                mmg(pg[:, 0:T], [(w_xr[:, kk, c * 128:(c + 1) * 128], XB[xb][:, kk, :]) for kk in range(8)], TXB[xb] + [Tw_xr], [tpg])
                tt(XR2[xb][:, c, 3:3 + T], pg[:, 0:T], rstd[:, :], ALU.mult, [tpg, TRS], [TXR2[xb][c]])

        def lean_front(g):
            lean_f0(g)
            lean_f2(g)
            lean_f3(g)

        def proj_main(Tn, g, nsb, want_q, want_gr, tok_out, sample, want_xr=False):
            slot6 = [(2 * g + b) % 6 for b in range(nsb)] if not sample else [0]
            blocks = []
            if want_q:
                blocks += list(range(0, 4))
            blocks += list(range(4, 12))
            if want_gr:
                blocks += list(range(12, 20))
            elif want_xr:
                blocks += list(range(12, 16))
            pkt = None
            for j in blocks:
                W, tw = wload(sc_in[j])
                if j < 8 or j >= 12:
                    pg, tpg = nextpg()
                    mmg(pg[:, 0:Tn], [(W[:, kk, :], xn[:, kk, 0:Tn]) for kk in range(8)], [tw] + TXN, [tpg])
                    if j < 4:
                        evac(qT[:, j, 0:Tn], pg[:, 0:Tn], [tpg], [TQ[j]])
                    elif j < 8:
                        if sample:
                            evac(kT[:, j - 4, 0:Tn], pg[:, 0:Tn], [tpg], [TKR[0]])
                        else:
                            c0 = slot6[0] * 128
                            evac(kT[:, j - 4, c0:c0 + Tn], pg[:, 0:Tn], [tpg], [TKR[s_] for s_ in slot6])
                    elif j < 16:
                        c = j - 12
                        if sample:
                            evac(XRs[:, c, :, 3:35], pg[:, 0:Tn].rearrange("p (s t) -> p s t", s=4), [tpg], [TXRS])
                        else:
                            evac(XR2[g % 2][:, c, 3:3 + Tn], pg[:, 0:Tn], [tpg], [TXR2[g % 2][c]])
                    else:
                        c = j - 16
                        act_op(AF.Gelu_apprx_tanh, G[:, c, 0:Tn], pg[:, 0:Tn], [tpg], [TG[c]])
                if (4 <= j < 8 and tok_out) or (8 <= j < 12):
                    jj = (j - 4) % 4
                    if jj == 0:
                        pkt = [(PG[3], TPG[3])] if sample else [(PSS[b_], TPSS[b_]) for b_ in range(nsb)]
                    for b in range(nsb):
                        mmg(pkt[b][0][:, jj * 128:(jj + 1) * 128], [(xn[:, kk, b * 128:(b + 1) * 128], W[:, kk, :]) for kk in range(8)],
                            [tw] + TXN, [pkt[b][1]])
                    if sample and j >= 8:
                        for s_ in range(4):
                            mmg(PSS[s_ // 2][0:32, (s_ % 2) * 512 + jj * 128:(s_ % 2) * 512 + (jj + 1) * 128],
                                [(xn[:, kk, s_ * 32:(s_ + 1) * 32], W[:, kk, :]) for kk in range(8)], [tw] + TXN, [TPSS[s_ // 2]])
                    if jj == 3:
                        for b in range(nsb):
                            pgt, tpgt = pkt[b]
                            if j >= 8:
                                if not sample:
                                    cp("dve", Vr[:, slot6[b], :], pgt[:, 0:512], [tpgt], [TVR[slot6[b]]])
                                else:
                                    for s_ in range(4):
                                        cp("dve", Vn[0:32, s_, :], PSS[s_ // 2][0:32, (s_ % 2) * 512:(s_ % 2 + 1) * 512], [TPSS[s_ // 2]], [TVN])
                            if tok_out:
                                oi = nexto()
                                cp("dve" if (j >= 8 and not sample) else "act", ost[oi][:, 0:512], pgt[:, 0:512], [tpgt], [TOS[oi]])
                                if sample:
                                    dst = (ksm_d if j < 8 else vsm_d)[:, :]
                                else:
                                    r0 = (g - (NT - 2)) * T + b * 128
                                    dst = (kp_d if j < 8 else vp_d)[r0:r0 + 128, :]
                                k.dma("pool", f"os{oi}", lambda h, oi=oi, dst=dst: h.dma_start(out=dst, in_=ost[oi][:, 0:512]), [TOS[oi]], [])

        osc = [0]

        def nexto():
            i = osc[0] % 2
            osc[0] += 1
            return i

        def attention_prompt(g, hook=None):
            expb5 = expb[:].rearrange("p h (k q) -> p h k q", k=5)
            units = [(hp, e, ps_) for hp in range(4) for e in range(2) for ps_ in range(2)]

            def slot_of(kb):
                return (2 * g - 4 + kb) % 6

            def emit_s(ui):
                hp, e, ps_ = units[ui]
                rows = slice(e * 64, (e + 1) * 64)
                S = PSS[ui % 2]
                sl = [slot_of(3 * ps_ + j) for j in range(3)]
                mmlist([(S[:, j * 256:(j + 1) * 256], kT[rows, hp, sl[j] * 128:(sl[j] + 1) * 128], qT[rows, hp, 0:256], True, True) for j in range(3)],
                       [TKR[s_] for s_ in sl] + [TQ[hp]], [TPSS[ui % 2]])

            po_cur = [None]
            emit_s(0)
            for ui, (hp, e, ps_) in enumerate(units):
                h_ = 2 * hp + e
                rows = slice(e * 64, (e + 1) * 64)
                si = ui % 2
                S = PSS[si]
                pi = 2 * ps_ + (ui // 2) % 2
                if ui + 1 < len(units):
                    emit_s(ui + 1)
                if e == 0 and ps_ == 0:
                    po_cur[0] = nextpg()
                po, tpo = po_cur[0]
                act_op(AF.Exp, E[si][:], S[:, 0:768], [TPSS[si]], [TE[si]], scale=SCALE)
                Ev = E[si][:].rearrange("p (k q) -> p k q", k=3)
                Pv = Pm[pi][:].rearrange("p (k q) -> p k q", k=3)
                if ps_ == 0:
                    tt(Pv[:, 0:3, 0:128], Ev[:, 0:3, 0:128], expb5[:, h_, 0:3, :], ALU.mult, [TE[si]] + CONST, [TP[pi]], eng="pool")
                    tt(Pv[:, 1:3, 128:256], Ev[:, 1:3, 128:256], expb5[:, h_, 0:2, :], ALU.mult, [TE[si]] + CONST, [TP[pi]], eng="pool")
                else:
                    tt(Pv[:, 0:2, 0:128], Ev[:, 0:2, 0:128], expb5[:, h_, 3:5, :], ALU.mult, [TE[si]] + CONST, [TP[pi]], eng="pool")
                    tt(Pv[:, 0:3, 128:256], Ev[:, 0:3, 128:256], expb5[:, h_, 2:5, :], ALU.mult, [TE[si]] + CONST, [TP[pi]], eng="pool")
                sl = [slot_of(3 * ps_ + j) for j in range(3)]
                items = [(po[rows, 0:256], Vr[:, sl[j], h_ * 64:(h_ + 1) * 64], Pm[pi][:, j * 256:(j + 1) * 256], (ps_ == 0 and j == 0), False) for j in range(3)]
                items += [(po[rows, 256:512], vones[:, sl[j], :], Pm[pi][:, j * 256:(j + 1) * 256], False, (ps_ == 1 and j == 2)) for j in range(3)]
                mmlist(items, [TP[pi]] + [TVR[s_] for s_ in sl] + [TVO[s_] for s_ in sl], [tpo])
                if hook is not None:
                    hook(ui)
                if e == 1 and ps_ == 1:
                    ri = hp % 2
                    k.op("dve", lambda h, ri=ri, po=po: h.reciprocal(out=rd[ri][:], in_=po[:, 256:512]), [tpo], [TRD[ri]])
                    tt(attnT[:, hp, :], po[:, 0:256], rd[ri][:], ALU.mult, [tpo, TRD[ri]], [TAT[hp]])
                    if g in (NLEAN, NLEAN + 1):
                        for b in range(2):
                            slots = [(2 * g + b - 4 + kb) % 6 for kb in range(5)]
                            pu, tpu = nextpg()
                            items = []
                            for e2 in range(2):
                                h2 = 2 * hp + e2
                                rows2 = slice(e2 * 64, (e2 + 1) * 64)
                                items += [(pu[rows2, 0:128], Vr[:, slots[kb], h2 * 64:(h2 + 1) * 64], band_bf[:, kb * 128:(kb + 1) * 128], kb == 0, kb == 4) for kb in range(5)]
                            mmlist(items, [TVR[s_] for s_ in slots] + CONST, [tpu])
                            ts(attnT[:, hp, b * 128:(b + 1) * 128], attnT[:, hp, b * 128:(b + 1) * 128], nc0, None, ALU.mult, None, [TAT[hp]] + CONST, [TAT[hp]])
                            stt(attnT[:, hp, b * 128:(b + 1) * 128], pu[:, 0:128], c0s, attnT[:, hp, b * 128:(b + 1) * 128], ALU.mult, ALU.add,
                                [tpu, TAT[hp]] + CONST, [TAT[hp]])
            if g == NLEAN + 1:
                ts(Vr[:, 0:4, :], Vr[:, 0:4, :], nc0, None, ALU.mult, None, TVR[0:4] + CONST, TVR[0:4])
                ts(vones[:, 0:4, :], vones[:, 0:4, :], nc0, None, ALU.mult, None, TVO[0:4] + CONST, TVO[0:4])

        def attention_sample():
            si_c = [0]
            for s_ in range(4):
                xk = xtok[s_ % 2]
                kview = xk[:].rearrange("p a (b c) -> p (a b) c", b=2)
                k.dma("pool", f"x{s_ % 2}", lambda h, s_=s_, kview=kview: h.dma_start(out=kview, in_=ck_d[s_].rearrange("(b p) f -> p b f", p=128)), (), [TXT[s_ % 2]])
                k.dma("pool", "vc", lambda h, s_=s_: h.dma_start(out=Vr[:, 0:4, :], in_=cv_d[s_].rearrange("(b p) f -> p b f", p=128)), (), [TVC] + TVR[0:4])
                for hp in range(4):
                    pg, tpg = nextpg()
                    trlist([(pg[:, blk * 128:(blk + 1) * 128], kview[:, blk, hp * 128:(hp + 1) * 128]) for blk in range(4)], [TXT[s_ % 2]] + CONST, [tpg])
                    evac(kT[:, hp, 256:768], pg[:, :], [tpg], [TKC] + TKR[2:6])
                for hp in range(4):
                    po, tpo = nextpg()
                    for e in range(2):
                        h_ = 2 * hp + e
                        rows = slice(e * 64, (e + 1) * 64)
                        si = si_c[0] % 2
                        si_c[0] += 1
                        S = PSS[si]
                        qv = qT[rows, hp, s_ * 32:(s_ + 1) * 32]
                        items = [(S[:, kb * 32:(kb + 1) * 32], kT[rows, hp, 256 + kb * 128:256 + (kb + 1) * 128], qv, True, True) for kb in range(4)]
                        items.append((S[0:32, 128:160], kT[rows, hp, s_ * 32:(s_ + 1) * 32], qv, True, True))
                        mmlist(items, [TKC, TKR[0], TQ[hp]], [TPSS[si]])
                        act_op(AF.Exp, E[si][:, 0:128], S[:, 0:128], [TPSS[si]], [TE[si]], scale=SCALE)
                        act_op(AF.Exp, E[si][0:32, 128:160], S[0:32, 128:160], [TPSS[si]], [TE[si]], scale=SCALE)
                        tt(Pm[si][:, 0:128], E[si][:, 0:128], expbs[:, h_, 0:128], ALU.mult, [TE[si]] + CONST, [TP[si]])
                        tt(Pm[si][0:32, 128:160], E[si][0:32, 128:160], expbs[0:32, h_, 128:160], ALU.mult, [TE[si]] + CONST, [TP[si]])
                        items = [(po[rows, 0:32], Vr[:, kb, h_ * 64:(h_ + 1) * 64], Pm[si][:, kb * 32:(kb + 1) * 32], kb == 0, False) for kb in range(4)]
                        items.append((po[rows, 0:32], Vn[0:32, s_, h_ * 64:(h_ + 1) * 64], Pm[si][0:32, 128:160], False, True))
                        items += [(po[rows, 32:64], ones_bf[:, 0:64], Pm[si][:, kb * 32:(kb + 1) * 32], kb == 0, False) for kb in range(4)]
                        items.append((po[rows, 32:64], ones_bf[0:32, 0:64], Pm[si][0:32, 128:160], False, True))
                        mmlist(items, [TP[si], TVC, TVN] + CONST, [tpo])
                    ri = hp % 2
                    k.op("dve", lambda h, ri=ri, po=po: h.reciprocal(out=rd[ri][:, 0:32], in_=po[:, 32:64]), [tpo], [TRD[ri]])
                    tt(attnT[:, hp, s_ * 32:(s_ + 1) * 32], po[:, 0:32], rd[ri][:, 0:32], ALU.mult, [tpo, TRD[ri]], [TAT[hp]])

        def back_half(Tn, nsb, p_src, p_r0, y_dst, y_r0):
            rms(attnT, TAT, 4, g_attn, mixT, TMX[0:4], Tn, 512.0)
            act_op(AF.Square, sq[:, 0:4, 0:Tn], lruo[:, 0:4, 0:Tn], TLO, [TSQ])
            pg, tpg = nextpg()
            mmg(pg[:, 0:Tn], [(ones_bf[:], sq[:, kk, 0:Tn]) for kk in range(4)], [TSQ] + CONST, [tpg])
            act_op(AF.Ln, rt[:, 0:Tn], pg[:, 0:Tn], [tpg] + CONST, [TRT], bias=eps_t[:], scale=1.0 / 512.0)
            act_op(AF.Exp, rstd[:, 0:Tn], rt[:, 0:Tn], [TRT], [TRS], scale=-0.5)
            for kk in range(4):
                stt(mixT[:, 4 + kk, 0:Tn], lruo[:, kk, 0:Tn], g_lru[:, kk:kk + 1], rstd[:, 0:Tn], ALU.mult, ALU.mult,
                    [TLO[kk], TRS] + CONST, [TMX[4 + kk]])
            for o in range(8):
                W, tw = wload(sc_out[o])
                pg, tpg = nextpg()
                mmg(pg[:, 0:Tn], [(W[:, kk, :], mixT[:, kk, 0:Tn]) for kk in range(8)], [tw] + TMX, [tpg])
                tt(hT[:, o, 0:Tn], hT[:, o, 0:Tn], pg[:, 0:Tn], ALU.add, [THT[o], tpg], [THT[o]])
            rms(hT, THT, 8, g_ffn, xn, TXN, Tn, 1024.0)
            def ffn_gu(f):
                Wg, twg = wload(sc_g[f])
                Wu, twu = wload(sc_u[f])
                Wd, twd = wload(sc_d[f], view1024=True)
                pg, tpg = nextpg()
                items = [(pg[:, 0:Tn], Wg[:, kk, :], xn[:, kk, 0:Tn], kk == 0, kk == 7) for kk in range(8)]
                items += [(pg[:, 256:256 + Tn], Wu[:, kk, :], xn[:, kk, 0:Tn], kk == 0, kk == 7) for kk in range(8)]
                mmlist(items, [twg, twu] + TXN, [tpg])
                return pg, tpg, Wd, twd

            cur = ffn_gu(0)
            for f in range(NF):
                nxt = ffn_gu(f + 1) if f + 1 < NF else None
                pg, tpg, Wd, twd = cur
                Wdv = Wd[:].rearrange("p a b -> p (a b)")
                fi = f % 2
                act_op(AF.Silu, sg[fi][:, 0:Tn], pg[:, 0:Tn], [tpg], [TSG[fi]])
                tt(act[fi][:, 0:Tn], sg[fi][:, 0:Tn], pg[:, 256:256 + Tn], ALU.mult, [TSG[fi], tpg], [TACT[fi]])
                items = [(PSS[o // 4][:, (o % 4) * 256:(o % 4) * 256 + Tn], Wdv[:, o * 128:(o + 1) * 128], act[fi][:, 0:Tn], (f == 0 and o % 2 == 0), (f == NF - 1 and o % 2 == 1)) for o in range(8)]
                mmlist(items, [twd, TACT[fi]], TPSS)
                cur = nxt
            for o in range(8):
                tt(hT[:, o, 0:Tn], hT[:, o, 0:Tn], PSS[o // 4][:, (o % 4) * 256:(o % 4) * 256 + Tn], ALU.add, [THT[o]] + TPSS, [THT[o]])
            rms(hT, THT, 8, g_ple, xn, TXN, Tn, 1024.0)
            k.dma("pool", "ptok", lambda h: h.dma_start(out=ptok[:, 0:nsb, :], in_=p_src[p_r0:p_r0 + Tn, :].rearrange("(b p) f -> p b f", p=128)), (), [TPT])
            pg, tpg = nextpg()
            trlist([(pg[:, (b * 2 + k2) * 128:(b * 2 + k2 + 1) * 128], ptok[:, b, k2 * 128:(k2 + 1) * 128]) for b in range(nsb) for k2 in range(2)],
                   [TPT] + CONST, [tpg])
            for b in range(nsb):
                evac(pT[:, 0:2, b * 128:(b + 1) * 128], pg[:, b * 256:(b + 1) * 256].rearrange("p (a b) -> p a b", a=2), [tpg], [TPTT])
            for o in range(8):
                W, tw = wload(sc_pg[o])
                pg, tpg = nextpg()
                items = [(pg[:, 0:Tn], W[:, kk, :], xn[:, kk, 0:Tn], kk == 0, kk == 7) for kk in range(8)]
                items += [(pg[:, 256:256 + Tn], w_pp[:, k2, o * 128:(o + 1) * 128], pT[:, k2, 0:Tn], k2 == 0, k2 == 1) for k2 in range(2)]
                mmlist(items, [tw, TPTT] + TXN + CONST, [tpg])
                fi = o % 2
                act_op(AF.Sigmoid, sg[fi][:, 0:Tn], pg[:, 0:Tn], [tpg], [TSG[fi]])
                tt(sg[fi][:, 0:Tn], pg[:, 256:256 + Tn], sg[fi][:, 0:Tn], ALU.mult, [TSG[fi], tpg], [TSG[fi]])
                tt(hT[:, o, 0:Tn], hT[:, o, 0:Tn], sg[fi][:, 0:Tn], ALU.add, [THT[o], TSG[fi]], [THT[o]])
            rms(hT, THT, 8, g_fin, hT, THT, Tn, 1024.0)
            for b in range(nsb):
                oi = nexto()
                for half in range(2):
                    pg, tpg = nextpg()
                    trlist([(pg[:, i * 128:(i + 1) * 128], hT[:, half * 4 + i, b * 128:(b + 1) * 128]) for i in range(4)], THT + CONST, [tpg])
                    evac(ost[oi][:, half * 512:(half + 1) * 512], pg[:, :], [tpg], [TOS[oi]])
                r0 = y_r0 + b * 128
                k.dma("pool", f"os{oi}", lambda h, oi=oi, r0=r0: h.dma_start(out=y_dst[r0:r0 + 128, :], in_=ost[oi][:]), [TOS[oi]], [])

        def sample_tile(Ts):
            k.dma("pool", "x0", lambda h: h.dma_start(out=xtok[0][:, 0, :], in_=xsm_d), (), [TXT[0]])
            k.dma("pool", "sst", lambda h: h.dma_start(out=XRs[:, :, :, 0:3], in_=sconv_d), (), [TXRS])
            k.dma("pool", "sst", lambda h: h.dma_start(out=hS[:], in_=sh_d), (), [THS_S])
            TXRS.lastw = (("D", "sst"), 32)
            THS_S.lastw = (("D", "sst"), 32)
            transpose_in(xtok[0], TXT[0], 1, Ts)
            rms(hT, THT, 8, g_mix, xn, TXN, Ts, 1024.0)
            proj_main(Ts, 0, 1, True, True, True, True)
            lru_all(Ts, 0, True, True)
            cp("act", hso[:], hs[:, :, 0:Ts].rearrange("p c (s t) -> p c s t", t=32)[:, :, :, 31], THS, [THSO])
            k.dma("pool", "st3", lambda h: h.dma_start(out=hsm_d, in_=hso[:]), [THSO], [])
            k.dma("pool", "st4", lambda h: h.dma_start(out=convs_d, in_=XRs[:, :, :, 32:35]), [TXRS], [])
            attention_sample()
            back_half(Ts, 1, psm_d, 0, ys_d, 0)

        tile_list = list(range(NT)) if dbg_tiles is None else list(dbg_tiles)
        if dbg_tiles is not None and dbg_conv:
            emit_conv(1000)
        NL2 = NLEAN - 2
        leans = [g for g in tile_list if g < NL2]
        if tile_list[0] >= NL2:
            load_x(tile_list[0])
        else:
            for g_ in leans[:2]:
                lean_cast(g_)
            lean_tr(leans[0])
        fronted = set()
        for ti, g in enumerate(tile_list):
            nxt = tile_list[ti + 1] if ti + 1 < len(tile_list) else None
            if nxt is not None and nxt >= NL2:
                load_x(nxt)
            if g < NLEAN and dbg_conv:
                emit_conv(3)
            if g < NL2:
                li = leans.index(g)
                if li + 2 < len(leans):
                    lean_cast(leans[li + 2])
                if li + 1 < len(leans):
                    lean_tr(leans[li + 1])
                if g not in fronted:
                    lean_front(g)
                if nxt is not None and nxt < NL2:
                    fronted.add(nxt)
                    lru_p(0, T, g, False)
                    lean_f0(nxt)
                    lru_p(1, T, g, False)
                    lru_p(2, T, g, False)
                    lru_p(3, T, g, False)
                    lean_f2(nxt)
                    lru_q(0, T, g, False, False)
                    lru_q(1, T, g, False, False)
                    lean_f3(nxt)
                    lru_q(2, T, g, False, False)
                    lru_q(3, T, g, False, False)
                else:
                    lru_all(T, g, False, False)
                continue
            transpose_in(xtok[g % 2], TXT[g % 2], 2, T)
            rms(hT, THT, 8, g_mix, xn, TXN, T, 1024.0)
            for b in range(2):
                s6 = (2 * g + b) % 6
                ts(vones[:, s6, :], ones_bf[:, 0:64], flags[:, g:g + 1], None, ALU.mult, None, CONST, [TVO[s6]])
            if g < NLEAN:
                proj_main(T, g, 2, False, False, False, False, want_xr=True)
                lru_all(T, g, False, False)
            else:
                proj_main(T, g, 2, True, True, g >= NT - 2, False)
                for c in range(4):
                    lru_p(c, T, g, False)
                attention_prompt(g, hook=lambda ui: lru_q(ui // 4, T, g, False, True) if ui % 4 == 3 else None)
                if _DBG.get("dump") == g:
                    k.dma("pool", "dbgo", lambda h: h.dma_start(out=dbg_d, in_=attnT[:]), TAT, [])
                back_half(T, 2, pm_d, (g - NLEAN) * T, y_d, (g - NLEAN) * T)
        if dbg_conv:
            emit_conv(1000)
        k.dma("pool", "st1", lambda h: h.dma_start(out=convp_d, in_=XR2[NT % 2][:, :, 0:3]), TXR2[NT % 2], [])
        k.dma("pool", "st2", lambda h: h.dma_start(out=hp_d, in_=hst[:]), THST, [])
        Ts = 128
        pgmod[0] = 3
        if dbg_sample:
            sample_tile(Ts)

        def _unused():
            pass

        print("total ops", k.total, {e: len(p) for e, p in k.prog.items()})
        out_keys = [kk for kk in ["os0", "os1", "st1", "st2", "st3", "st4"] if kk in k.dsem] + (["dbgo"] if "dbgo" in k.dsem else [])
        with nc.Block() as block:
            @block.tensor
            def _(h):
                k.replay("pe", h)

            @block.scalar
            def _(h):
                k.replay("act", h)

            @block.vector
            def _(h):
                k.replay("dve", h)

            @block.gpsimd
            def _(h):
                with nc.allow_non_contiguous_dma(reason="small strided state/param transfers"):
                    k.replay("pool", h, out_keys)

            @block.sync
            def _(h):
                k.replay("sp", h)
    return nc


_NC_CACHE = {}


def _fm(v, nk):
    return np.ascontiguousarray(np.asarray(v, np.float32).reshape(nk, 128).T)


def kernel(x_prompt, x_sample, p_prompt, p_sample, cache_k, cache_v, state_conv, state_h,
           g_mix, w_in, conv_w, conv_b, w_rgate, b_rgate, w_igate, b_igate, lru_lambda,
           rel_bias_table, g_attn_out, g_lru_out, w_out, g_ffn, w_ffn_gate, w_ffn_up,
           w_ffn_down, g_ple, w_ple_gate, w_ple_proj, g_final):
    f32 = np.float32
    x_prompt = np.asarray(x_prompt, f32)
    x_sample = np.asarray(x_sample, f32)
    p_prompt = np.asarray(p_prompt, f32)
    p_sample = np.asarray(p_sample, f32)
    cache_k = np.asarray(cache_k, f32)
    cache_v = np.asarray(cache_v, f32)
    state_conv = np.asarray(state_conv, f32)
    state_h = np.asarray(state_h, f32)
    table = np.asarray(rel_bias_table, f32)[0]

    p_idx = np.arange(128)
    biasp = np.full((128, 8, 5, 128), NEG, f32)
    for kb in range(5):
        rel = (kb * 128 + p_idx[:, None] - 512) - p_idx[None, :]
        idx = np.clip(rel, -128, 63) + 128
        kc = (kb * 128 + p_idx[:, None]) // 64 - 8
        qc = p_idx[None, :] // 64
        valid = (kc <= qc) & (kc >= qc - 8)
        vals = table[:, idx]
        blk = np.where(valid[None], vals, f32(NEG))
        biasp[:, :, kb, :] = blk.transpose(1, 0, 2)
    biass = np.full((128, 8, 5, 32), NEG, f32)
    q32 = np.arange(32)
    for kb in range(4):
        rel = (kb * 128 + p_idx[:, None] - 512) - q32[None, :]
        idx = np.clip(rel, -128, 63) + 128
        biass[:, :, kb, :] = table[:, idx].transpose(1, 0, 2)
    rel = q32[:, None] - q32[None, :]
    idx = np.clip(rel, -128, 63) + 128
    biass[0:32, :, 4, :] = table[:, idx].transpose(1, 0, 2)
    cwl = np.asarray(conv_w, f32)[0]
    cwfm = np.ascontiguousarray(cwl.reshape(4, 4, 128).transpose(2, 1, 0)).reshape(128, 16)
    vecs = np.concatenate([
        _fm(g_mix[0], 8), _fm(g_ffn[0], 8), _fm(g_ple[0], 8), _fm(g_final, 8),
        _fm(g_attn_out[0], 4), _fm(g_lru_out[0], 4), _fm(conv_b[0], 4), _fm(b_rgate[0], 4),
        _fm(b_igate[0], 4), _fm(lru_lambda[0], 4), cwfm, np.zeros((128, 2), f32)], axis=1).astype(f32)
    ident = np.eye(128, dtype=f32)
    shared = {
        "band": np.ascontiguousarray((biasp[:, 0] > -1e29).astype(f32).reshape(128, 640)),
        "biasp": biasp.reshape(128, 8 * 640), "biass": biass.reshape(128, 8 * 160), "ident": ident, "vecs": vecs,
        "w_in": np.asarray(w_in, f32)[0], "w_out": np.asarray(w_out, f32)[0],
        "w_ffn_gate": np.asarray(w_ffn_gate, f32)[0], "w_ffn_up": np.asarray(w_ffn_up, f32)[0],
        "w_ffn_down": np.asarray(w_ffn_down, f32)[0], "w_ple_gate": np.asarray(w_ple_gate, f32)[0],
        "w_ple_proj": np.asarray(w_ple_proj, f32)[0], "w_rgate": np.asarray(w_rgate, f32)[0],
        "w_igate": np.asarray(w_igate, f32)[0],
    }
    in_maps = []
    for c in range(8):
        s, j = c // 4, c % 4
        npad = (3 - j) * SEG
        xs = np.zeros((NT * T, D), f32)
        xs[npad:] = x_prompt[s, 0:(j + 1) * SEG]
        fl = np.zeros((NT,), f32)
        fl[npad // T:] = 1.0
        sq_ = slice(4 * c, 4 * c + 4)
        m = dict(shared)
        vc_ = vecs.copy()
        vc_[:, 72] = (1.0 / 576.0) if j == 0 else 0.0
        vc_[:, 73] = 0.0 if j == 0 else 1.0
        m["vecs"] = vc_
        m.update({
            "xs": xs, "pm": np.ascontiguousarray(p_prompt[0, s, j * SEG:(j + 1) * SEG]),
            "xsm": np.ascontiguousarray(x_sample[sq_].reshape(128, D)),
            "psm": np.ascontiguousarray(p_sample[0, sq_].reshape(128, 256)),
            "ck": np.ascontiguousarray(cache_k[0, sq_].reshape(4, 512, 512)),
            "cv": np.ascontiguousarray(cache_v[0, sq_].reshape(4, 512, 512)),
            "sconv": np.ascontiguousarray(state_conv[0, sq_].reshape(4, 3, 4, 128).transpose(3, 2, 0, 1)),
            "sh": np.ascontiguousarray(state_h[0, sq_].reshape(4, 4, 128).transpose(2, 1, 0)),
            "flags": np.ascontiguousarray(np.broadcast_to(fl[None, :], (128, NT))),
        })
        in_maps.append(m)
    if _DBG.get("prep_only"):
        return in_maps
    if "nc" not in _NC_CACHE:
        _NC_CACHE["nc"] = build_nc()
    res = run_bass_kernel_spmd(_NC_CACHE["nc"], in_maps, core_ids=list(range(8)))
    R = res.results
    y_prompt = np.stack([np.concatenate([R[s * 4 + j]["y"] for j in range(4)], axis=0) for s in range(2)]).astype(f32)
    y_sample = np.concatenate([R[c]["ys"].reshape(4, 32, D) for c in range(8)], axis=0).astype(f32)
    new_k_prompt = np.stack([R[s * 4 + 3]["kp"].reshape(512, 8, 64) for s in range(2)])[None].astype(f32)
    new_v_prompt = np.stack([R[s * 4 + 3]["vp"].reshape(512, 8, 64) for s in range(2)])[None].astype(f32)
    new_conv_prompt = np.stack([R[s * 4 + 3]["convp"].transpose(2, 1, 0).reshape(3, 512) for s in range(2)])[None].astype(f32)
    new_h_prompt = np.stack([R[s * 4 + 3]["hp"].T.reshape(512) for s in range(2)])[None].astype(f32)
    new_k_sample = np.concatenate([R[c]["ksm"].reshape(4, 32, 8, 64) for c in range(8)], axis=0)[None].astype(f32)
    new_v_sample = np.concatenate([R[c]["vsm"].reshape(4, 32, 8, 64) for c in range(8)], axis=0)[None].astype(f32)
    new_conv_sample = np.concatenate([R[c]["convs"].transpose(2, 3, 1, 0).reshape(4, 3, 512) for c in range(8)], axis=0)[None].astype(f32)
    new_h_sample = np.concatenate([R[c]["hsm"].transpose(2, 1, 0).reshape(4, 512) for c in range(8)], axis=0)[None].astype(f32)
    return (y_prompt, y_sample, new_k_prompt, new_v_prompt, new_conv_prompt, new_h_prompt,
            new_k_sample, new_v_sample, new_conv_sample, new_h_sample)
```
